# Optimizing a Trainium2 kernel written in Bass

```python
import jax, jax.numpy as jnp
from jax import lax
import numpy as np

D_MODEL = 1024
BATCH = 8
SEQ = 2048
DEPTH = 1
DEC_BATCH = 128
DEC_SEQ = 1
PAST_LEN = 16384
PAGE_SIZE = 128

MIX_W = D_MODEL
POOL_W = MIX_W // 2
RWKV_W = MIX_W - POOL_W
POOL_WINDOWS = (2, 4, 8, 16)
N_POOL_GROUPS = len(POOL_WINDOWS)
POOL_GC = POOL_W // N_POOL_GROUPS
POOL_BUF = max(POOL_WINDOWS) - 1
HEAD_DIM = 64
N_HEADS = RWKV_W // HEAD_DIM
W_LORA = 64
A_LORA = 64
G_LORA = 128
SHIFT_W = 3 * RWKV_W + W_LORA + A_LORA + G_LORA
IN_W = POOL_W + SHIFT_W
D_FF = 4 * D_MODEL
RMS_EPS = 1e-6
GN_EPS = 64e-5
NORM_EPS = 1e-12

kernel_name = 'hymba_pool_rwkv7_decode_step'


def _rmsnorm(x, g):
    xf = x.astype(jnp.float32)
    y = xf * lax.rsqrt(jnp.mean(xf * xf, axis=-1, keepdims=True) + RMS_EPS)
    return (y * g.astype(jnp.float32)).astype(x.dtype)


def _pool_mixer(u, buf, pos0, pool_w, pool_scale):
    B, T, C = u.shape
    f32 = jnp.float32
    ext = jnp.concatenate([buf.astype(f32), u.astype(f32)], axis=1)
    cs = jnp.concatenate([jnp.zeros((B, 1, C), f32), jnp.cumsum(ext, axis=1)], axis=1)
    end = cs[:, POOL_BUF + 1:]
    pos = pos0 + jnp.arange(T, dtype=jnp.int32)
    means = []
    for gi, win in enumerate(POOL_WINDOWS):
        sl = slice(gi * POOL_GC, (gi + 1) * POOL_GC)
        start = cs[:, POOL_BUF + 1 - win: POOL_BUF + 1 - win + T, sl]
        cnt = jnp.minimum(win, pos + 1).astype(f32)[None, :, None]
        means.append((end[..., sl] - start) / cnt)
    mean = jnp.concatenate(means, axis=-1)
    d = (mean - ext[:, POOL_BUF:]).reshape(B, T, N_POOL_GROUPS, POOL_GC)
    out = jnp.einsum('btgc,gcd->btgd', d, pool_w.astype(f32)).reshape(B, T, C) * pool_scale.astype(f32)
    new_buf = ext[:, -POOL_BUF:].astype(buf.dtype)
    return out, new_buf


def _rwkv7_mixer(p, shift_prev, S0, shift_mu, w0, w_lora_up, a0, a_lora_up, g_lora_up,
                 k_k, k_a, r_k, ln_x_g, ln_x_b):
    B, T, _ = p.shape
    f32 = jnp.float32
    pf = p.astype(f32)
    prev = jnp.concatenate([shift_prev.astype(f32), pf[:, :-1]], axis=1)
    ps = pf + (prev - pf) * shift_mu.astype(f32)
    cuts = [RWKV_W, 2 * RWKV_W, 3 * RWKV_W, 3 * RWKV_W + W_LORA, 3 * RWKV_W + W_LORA + A_LORA]
    r, k, v, zw, za, zg = jnp.split(ps, cuts, axis=-1)
    w_log = -jax.nn.softplus(-(w0 + jnp.tanh(zw) @ w_lora_up)) - 0.5
    decay = jnp.exp(-jnp.exp(w_log))
    a = jax.nn.sigmoid(a0 + za @ a_lora_up)
    g = jax.nn.sigmoid(zg) @ g_lora_up
    heads = lambda t: t.reshape(B, T, N_HEADS, HEAD_DIM)
    kk = heads(k * k_k)
    kk = kk * lax.rsqrt(jnp.sum(kk * kk, axis=-1, keepdims=True) + NORM_EPS)
    k = k * (1.0 + (a - 1.0) * k_a)
    r, k, v, decay, a = map(heads, (r, k, v, decay, a))

    def step(S, inp):
        r_t, w_t, k_t, v_t, aa_t, bb_t = inp
        sa = jnp.einsum('bhvk,bhk->bhv', S, aa_t)
        S = S * w_t[:, :, None, :] + sa[..., None] * bb_t[:, :, None, :] + v_t[..., None] * k_t[:, :, None, :]
        return S, jnp.einsum('bhvk,bhk->bhv', S, r_t)

    xs = tuple(jnp.moveaxis(t, 1, 0) for t in (r, decay, k, v, -kk, kk * a))
    S_final, y = lax.scan(step, S0.astype(f32), xs)
    y = jnp.moveaxis(y, 0, 1)
    mu = jnp.mean(y, axis=-1, keepdims=True)
    var = jnp.mean(jnp.square(y - mu), axis=-1, keepdims=True)
    y = ((y - mu) * lax.rsqrt(var + GN_EPS)).reshape(B, T, RWKV_W) * ln_x_g + ln_x_b
    bonus = jnp.sum(r * k * r_k, axis=-1, keepdims=True) * v
    y = (y + bonus.reshape(B, T, RWKV_W)) * g
    return y, S_final.astype(S0.dtype), p[:, -1:]


def _layer(x, S0, shift_prev, pool_buf, pos0, norm1_g, w_in, shift_mu, pool_w, pool_scale,
           w0, w_lora_up, a0, a_lora_up, g_lora_up, k_k, k_a, r_k, ln_x_g, ln_x_b,
           w_out, norm2_g, w_up, w_down):
    h = _rmsnorm(x, norm1_g)
    proj = h @ w_in
    pool_out, new_pool = _pool_mixer(proj[..., :POOL_W], pool_buf, pos0, pool_w, pool_scale)
    rwkv_out, new_S, new_shift = _rwkv7_mixer(proj[..., POOL_W:], shift_prev, S0, shift_mu, w0,
                                              w_lora_up, a0, a_lora_up, g_lora_up, k_k, k_a, r_k,
                                              ln_x_g, ln_x_b)
    mix = jnp.concatenate([pool_out, rwkv_out], axis=-1).astype(x.dtype) @ w_out
    x = x + mix.astype(x.dtype)
    h2 = _rmsnorm(x, norm2_g)
    x = x + (jnp.square(jax.nn.relu(h2 @ w_up)) @ w_down).astype(x.dtype)
    return x, new_S, new_shift, new_pool


def setup_inputs(seed: int = 0) -> dict:
    key = jax.random.key(seed)
    ks = jax.random.split(key, 32)
    f32 = jnp.float32
    nrm = lambda k, shape, s: jax.random.normal(k, shape, f32) * s
    L = DEPTH
    return {
        'x_prompt': nrm(ks[0], (BATCH, SEQ, D_MODEL), 1.0),
        'x_sample': nrm(ks[1], (DEC_BATCH, DEC_SEQ, D_MODEL), 1.0),
        'state_wkv': nrm(ks[2], (L, DEC_BATCH, N_HEADS, HEAD_DIM, HEAD_DIM), 0.1),
        'state_shift': nrm(ks[3], (L, DEC_BATCH, 1, SHIFT_W), 1.0),
        'state_pool': nrm(ks[4], (L, DEC_BATCH, POOL_BUF, POOL_W), 1.0),
        'norm1_g': 1.0 + nrm(ks[5], (L, D_MODEL), 0.02),
        'w_in': nrm(ks[6], (L, D_MODEL, IN_W), D_MODEL ** -0.5),
        'shift_mu': jax.random.uniform(ks[7], (L, SHIFT_W), f32, 0.0, 1.0),
        'pool_w': nrm(ks[8], (L, N_POOL_GROUPS, POOL_GC, POOL_GC), POOL_GC ** -0.5),
        'pool_scale': 1.0 + nrm(ks[9], (L, POOL_W), 0.1),
        'w0': -1.0 + nrm(ks[10], (L, RWKV_W), 0.5),
        'w_lora_up': nrm(ks[11], (L, W_LORA, RWKV_W), 0.5 * W_LORA ** -0.5),
        'a0': nrm(ks[12], (L, RWKV_W), 0.3),
        'a_lora_up': nrm(ks[13], (L, A_LORA, RWKV_W), 0.5 * A_LORA ** -0.5),
        'g_lora_up': nrm(ks[14], (L, G_LORA, RWKV_W), G_LORA ** -0.5),
        'k_k': 0.85 + nrm(ks[15], (L, RWKV_W), 0.05),
        'k_a': 1.0 + nrm(ks[16], (L, RWKV_W), 0.05),
        'r_k': nrm(ks[17], (L, N_HEADS, HEAD_DIM), 0.1),
        'ln_x_g': 1.0 + nrm(ks[18], (L, RWKV_W), 0.02),
        'ln_x_b': nrm(ks[19], (L, RWKV_W), 0.02),
        'w_out': nrm(ks[20], (L, MIX_W, D_MODEL), MIX_W ** -0.5),
        'norm2_g': 1.0 + nrm(ks[21], (L, D_MODEL), 0.02),
        'w_up': nrm(ks[22], (L, D_MODEL, D_FF), D_MODEL ** -0.5),
        'w_down': nrm(ks[23], (L, D_FF, D_MODEL), D_FF ** -0.5),
        'norm_f_g': 1.0 + nrm(ks[24], (D_MODEL,), 0.02),
    }


def reference(x_prompt, x_sample, state_wkv, state_shift, state_pool, norm1_g, w_in, shift_mu,
              pool_w, pool_scale, w0, w_lora_up, a0, a_lora_up, g_lora_up, k_k, k_a, r_k,
              ln_x_g, ln_x_b, w_out, norm2_g, w_up, w_down, norm_f_g):
    dt_p = x_prompt.dtype
    zero_wkv = jnp.zeros((BATCH, N_HEADS, HEAD_DIM, HEAD_DIM), state_wkv.dtype)
    zero_shift = jnp.zeros((BATCH, 1, SHIFT_W), dt_p)
    zero_pool = jnp.zeros((BATCH, POOL_BUF, POOL_W), dt_p)
    hp, hs = x_prompt, x_sample
    wkv_p, sh_p, pb_p, wkv_s, sh_s, pb_s = [], [], [], [], [], []
    for l in range(DEPTH):
        lw = (norm1_g[l], w_in[l], shift_mu[l], pool_w[l], pool_scale[l], w0[l], w_lora_up[l],
              a0[l], a_lora_up[l], g_lora_up[l], k_k[l], k_a[l], r_k[l], ln_x_g[l], ln_x_b[l],
              w_out[l], norm2_g[l], w_up[l], w_down[l])
        hp, s1, s2, s3 = _layer(hp, zero_wkv, zero_shift, zero_pool, 0, *lw)
        hs, t1, t2, t3 = _layer(hs, state_wkv[l], state_shift[l], state_pool[l], PAST_LEN, *lw)
        wkv_p.append(s1); sh_p.append(s2); pb_p.append(s3)
        wkv_s.append(t1); sh_s.append(t2); pb_s.append(t3)
    y_prompt = _rmsnorm(hp, norm_f_g)
    y_sample = _rmsnorm(hs, norm_f_g)
    return (y_prompt, y_sample, jnp.stack(wkv_p), jnp.stack(sh_p), jnp.stack(pb_p),
            jnp.stack(wkv_s), jnp.stack(sh_s), jnp.stack(pb_s))
```

```python
import numpy as np
from contextlib import ExitStack
import concourse.bass as bass
import concourse.mybir as mybir
from concourse.bass_utils import run_bass_kernel_spmd

F32, BF16 = mybir.dt.float32, mybir.dt.bfloat16
AF = mybir.ActivationFunctionType
ALU = mybir.AluOpType
AX = mybir.AxisListType

NCORES = 8
D = 1024
T = 2048
NTILE = T // 128
NS = 16
IN_W = 2304
SHIFT_W = 1792
DFF = 4096
CW = -float(np.exp(-0.5))
RMS_EPS = 1e-6
GN_EPS = 64e-5
ST = 256
DEBUG_TAPS = False
KSTOP = 99.0
KVAR = 0


class _Stop(Exception):
    pass


class _Rec:
    def __getattr__(self, name):
        return lambda *a, **k: (name, a, k)


_REC = _Rec()


class Prog:
    def __init__(self, nc, es):
        self.nc, self.es = nc, es
        self.streams = {k: [] for k in ("pe", "dve", "act", "pool", "sp")}
        self.csem = {k: es.enter_context(nc.semaphore("c_" + k)) for k in ("pe", "dve", "act", "pool")}
        self.cnt = {k: 0 for k in self.csem}
        self.dsem, self.dcnt = {}, {}
        self.seen = {k: {} for k in self.streams}
        self.reg = {}
        self.pend = {k: [] for k in self.streams}
        self.alias = {}
        self.vc = {k: {} for k in self.streams}
        self.hist = {}

    def _exp(self, keys):
        out = []
        for k in keys:
            out.extend(self.alias.get(k, [k]))
        return out

    def _need(self, eng, key, val):
        if key in self.dcnt:
            val = self.dcnt[key]
        else:
            if eng == "pe" and key == "pe":
                return
            assert val <= self.cnt[key], (eng, key, val, self.cnt[key])
        vc = self.vc[eng]
        if vc.get(key, 0) >= val:
            return
        for k2, v2 in self.hist.get((key, val), {key: val}).items():
            if vc.get(k2, 0) < v2:
                vc[k2] = v2
        sem = self.csem[key] if key in self.csem else self.dsem[key]
        self.pend[eng].append((sem, val))

    def _flush(self, eng, keep_last):
        p = self.pend[eng]
        last = p.pop() if (keep_last and p) else None
        for sem, val in p:
            self.streams[eng].append(lambda e, sem=sem, val=val: e.wait_ge(sem, val))
        self.pend[eng] = []
        return last

    def _deps(self, eng, reads, writes):
        for r in reads:
            st = self.reg.get(r)
            if st and st["w"]:
                self._need(eng, *st["w"])
        for w in writes:
            st = self.reg.get(w)
            if st:
                if st["w"]:
                    self._need(eng, *st["w"])
                for k, v in st["r"].items():
                    self._need(eng, k, v)

    def _mark(self, ev, reads, writes):
        for r in reads:
            st = self.reg.setdefault(r, {"w": None, "r": {}})
            st["r"][ev[0]] = max(st["r"].get(ev[0], 0), ev[1])
        for w in writes:
            self.reg[w] = {"w": ev, "r": {}}

    def op(self, eng, fn, reads=(), writes=(), inc=True):
        name, a, k = fn(_REC)
        reads, writes = self._exp(reads), self._exp(writes)
        self._deps(eng, reads, writes)
        w = self._flush(eng, True)

        def emit(e, name=name, a=a, k=k, w=w, sem=(self.csem[eng] if inc else None)):
            ins = getattr(e, name)(*a, **k)
            if w is not None:
                ins = ins._wait_ge(w[0], w[1])
            if sem is not None:
                ins.then_inc(sem, 1)
        if inc:
            self.cnt[eng] += 1
            ev = (eng, self.cnt[eng])
            snap = dict(self.vc[eng])
            snap[eng] = self.cnt[eng]
            self.hist[ev] = snap
            self.vc[eng][eng] = self.cnt[eng] if eng == "pe" else self.vc[eng].get(eng, 0)
        else:
            ev = (eng, self.cnt[eng] + 1)
        self.streams[eng].append(emit)
        self._mark(ev, reads, writes)

    def dma(self, out, in_, reads=(), writes=(), sem="d0", q="sp", chain=False, **kw):
        if sem not in self.dsem:
            self.dsem[sem] = self.es.enter_context(self.nc.semaphore("d_" + sem))
            self.dcnt[sem] = 0
        if not chain and self.dcnt[sem] > 0:
            self._need(q, sem, self.dcnt[sem])
        reads, writes = self._exp(reads), self._exp(writes)
        self._deps(q, reads, writes)
        w = self._flush(q, True)
        self.dcnt[sem] += 16
        ev = (sem, self.dcnt[sem])
        snap = dict(self.vc[q])
        snap[sem] = self.dcnt[sem]
        self.hist[ev] = snap
        s = self.dsem[sem]

        def emit(e, out=out, in_=in_, s=s, kw=kw, w=w):
            ins = e.dma_start(out=out, in_=in_, **kw)
            if w is not None:
                ins = ins._wait_ge(w[0], w[1])
            ins.then_inc(s, 16)
        self.streams[q].append(emit)
        self._mark(ev, reads, writes)

    def barrier(self):
        for e in self.streams:
            for k in self.csem:
                if self.cnt[k] > 0:
                    self._need(e, k, self.cnt[k])
            for k in self.dcnt:
                if self.dcnt[k] > 0:
                    self._need(e, k, self.dcnt[k])
            self._flush(e, False)

    def finish(self):
        for k in self.csem:
            if self.cnt[k] > 0:
                self._need("sp", k, self.cnt[k])
        for k in self.dcnt:
            self._need("sp", k, self.dcnt[k])
        for e in self.streams:
            self._flush(e, False)

    def emit(self, block):
        S = self.streams

        @block.sync
        def _(e):
            for f in S["sp"]:
                f(e)

        @block.tensor
        def _(e):
            for f in S["pe"]:
                f(e)

        @block.vector
        def _(e):
            for f in S["dve"]:
                f(e)

        @block.scalar
        def _(e):
            for f in S["act"]:
                f(e)

        @block.gpsimd
        def _(e):
            for f in S["pool"]:
                f(e)


class Mem:
    BASE, LIMIT = 16512, 229344

    def __init__(self, nc):
        self.nc, self.off, self.n = nc, self.BASE, 0

    def alloc(self, name, shape, dtype):
        nb = 2 if dtype == BF16 else 4
        size = int(np.prod(shape[1:])) * nb
        size = (size + 63) // 64 * 64
        assert self.off + size <= self.LIMIT, ("SBUF overflow", name, self.off, size)
        self.n += 1
        t = self.nc.alloc_sbuf_tensor_at(f"{name}_{self.n}", list(shape), dtype, offset=self.off)
        self.off += size
        return t

    def mark(self):
        return self.off

    def release(self, m):
        self.off = m


def _make_consts():
    c = {}
    i = np.arange(128)
    c["ident"] = np.eye(128, dtype=np.float32)
    c["m_su"] = (i[:, None] < i[None, :]).astype(np.float32)
    c["m_iu"] = (i[:, None] <= i[None, :]).astype(np.float32)
    c["m_sl"] = (i[:, None] > i[None, :]).astype(np.float32)
    c["hones"] = ((i[:, None] // 64) == (i[None, :] // 64)).astype(np.float32)
    rst = np.ones((128, ST), np.float32)
    rst[:, ::128] = 0.0
    c["restart"] = rst
    wins = (2, 4, 8, 16)
    band = np.zeros((4, 128, 128), np.float32)
    bprev = np.zeros((4, 128, 128), np.float32)
    bfirst = np.zeros((4, 128, 128), np.float64)
    for g, w in enumerate(wins):
        for t in range(128):
            for s in range(t - w + 1, t + 1):
                if s >= 0:
                    band[g, s, t] += 1.0 / w
                    bfirst[g, s, t] += 1.0 / min(w, t + 1)
                else:
                    bprev[g, 128 + s, t] += 1.0 / w
            band[g, t, t] -= 1.0
            bfirst[g, t, t] -= 1.0
    c["band"] = np.concatenate(list(band), axis=1)
    c["bprev"] = np.concatenate(list(bprev), axis=1)
    import ml_dtypes
    hi = bfirst.astype(np.float32).astype(ml_dtypes.bfloat16).astype(np.float32)
    lo = (bfirst - hi).astype(np.float32)
    c["bfhi"] = np.concatenate(list(hi), axis=1)
    c["bflo"] = np.concatenate(list(lo), axis=1)
    sel = np.zeros((128, 2, 4, 16), np.float32)
    for tl in range(2):
        for bl in range(8):
            for j in range(15):
                for g, w in enumerate(wins):
                    if j >= 16 - w:
                        sel[bl * 15 + j, tl, g, tl * 8 + bl] = 1.0 / w
    c["sel"] = sel.reshape(128, 128)
    ind = np.zeros((128, 4, 8), np.float32)
    for p in range(128):
        for hp in range(4):
            ind[p, hp, 2 * hp + p // 64] = 1.0
    c["ind"] = ind.reshape(128, 32)
    return c


_CONST_ORDER = ["ident", "m_su", "m_iu", "m_sl", "hones", "restart", "band", "bprev", "bfhi", "bflo", "sel", "ind"]


def _pack_consts():
    c = _make_consts()
    offs, cols, o = {}, [], 0
    for k in _CONST_ORDER:
        offs[k] = (o, c[k].shape[1])
        o += c[k].shape[1]
        cols.append(c[k])
    return np.concatenate(cols, axis=1).astype(np.float32), offs


def build_program():
    cst_np, coff = _pack_consts()
    NCST = cst_np.shape[1]
    nc = bass.Bass("TRN2", target_bir_lowering=False)
    dram = lambda n, s, k="ExternalInput": nc.dram_tensor(n, list(s), F32, kind=k).ap()
    x_d = dram("x", [T, D]); xs_d = dram("xs", [NS, D])
    swkv_d = dram("swkv", [128, 4096]); sshift_d = dram("sshift", [NS, SHIFT_W]); spool_d = dram("spool", [NS * 15, 512])
    w_in_d = dram("w_in", [D, IN_W]); w_out_d = dram("w_out", [D, D]); w_up_d = dram("w_up", [D, DFF]); w_dn_d = dram("w_down", [DFF, D])
    poolw_d = dram("pool_w", [4, 128, 128]); lora12_d = dram("lora12", [128, 512]); glora_d = dram("g_lora_up", [128, 512])
    pcol_d = dram("pcol", [128, 64]); rows_d = dram("rows", [3, 1024]); bh_d = dram("bhrows", [128, 192])
    cst_d = dram("cst", [128, NCST])
    y_d = dram("y", [T, D], "ExternalOutput"); ys_d = dram("ys", [NS, D], "ExternalOutput")
    wkvp_d = dram("wkv_p", [8, 64, 64], "ExternalOutput"); shp_d = dram("shift_p", [14, 128], "ExternalOutput")
    plp_d = dram("pool_p", [16, 512], "ExternalOutput")
    wkvs_d = dram("wkv_s", [128, 4096], "ExternalOutput"); shs_d = dram("shift_s", [NS, SHIFT_W], "ExternalOutput")
    pls_d = dram("pool_s", [NS, 15, 512], "ExternalOutput")
    scr_d = dram("scr", [8, NS, 512], "Internal")
    taps = {}

    es = ExitStack()
    with es:
        P = Prog(nc, es)
        M = Mem(nc)
        PS = [es.enter_context(nc.psum_tensor(f"ps{i}", [128, 512], F32)) for i in range(8)]
        psn = [0]

        CX = {"banks": list(range(8)), "pn": "all"}
        pcount = {}

        def bank():
            lst = CX["banks"]
            n = pcount.get(CX["pn"], 0)
            pcount[CX["pn"]] = n + 1
            i = lst[n % len(lst)]
            return PS[i], ("ps", i)

        def pe_warm(nmm):
            pw, pwk = bank()
            for _ in range(nmm):
                P.op("pe", lambda e: e.matmul(pw[:, :], lhsT=WARM[0], rhs=WARM[1], start=True, stop=True), reads=["CB"], writes=[pwk], inc=False)

        WARM = [None, None]

        def pe_warm_small(nmm, pw, pwk):
            for _ in range(nmm):
                P.op("pe", lambda e: e.matmul(pw[:, 0:128], lhsT=WARM[0], rhs=WARM[1][:, 0:128], start=True, stop=True), reads=["CB"], writes=[pwk], inc=False)

        def tap(name, ap, shape, reads):
            if not DEBUG_TAPS:
                return
            t = nc.dram_tensor("tap_" + name, list(shape), ap.dtype, kind="ExternalOutput").ap()
            taps[name] = t
            P.dma(t, ap, reads=reads, sem="tap")

        X1 = M.alloc("X1", [128, NTILE, D], F32)
        X1s = M.alloc("X1s", [128, D], F32)
        CB = M.alloc("CB", [128, NCST], BF16)
        identf = M.alloc("identf", [128, 128], F32)
        honesf = M.alloc("honesf", [128, 128], F32)
        restart = M.alloc("restart", [128, ST], F32)
        pcol = M.alloc("pcol", [128, 64], F32)
        pder = M.alloc("pder", [128, 32], F32)
        rs2 = M.alloc("rs2", [128, NTILE + 1], F32)
        cb = lambda k: CB[:, coff[k][0]:coff[k][0] + coff[k][1]]
        identb = cb("ident")
        WARM[0], WARM[1] = identb, CB[:, 0:512]
        MU = lambda j: pcol[:, j:j + 1]
        OMMU = lambda j: pder[:, j:j + 1]
        W0 = lambda hp: pcol[:, 14 + hp:15 + hp]
        A0 = lambda hp: pcol[:, 18 + hp:19 + hp]
        KKc = lambda hp: pcol[:, 22 + hp:23 + hp]
        KAc = lambda hp: pcol[:, 26 + hp:27 + hp]
        OMKA = lambda hp: pder[:, 14 + hp:15 + hp]
        RKc = lambda hp: pcol[:, 30 + hp:31 + hp]
        PSC = lambda g: pcol[:, 34 + g:35 + g]
        G1 = lambda c: pcol[:, 38 + c:39 + c]
        G2 = lambda c: pcol[:, 46 + c:47 + c]

        try:
            P.dma(pcol[:], pcol_d[:, :], writes=["pcol"], sem="par")
            P.dma(X1s[0:NS, :], xs_d[:, :], writes=["x1s"], sem="xs")

            m0 = M.mark()
            stg = [M.alloc(f"stg{i}", [128, IN_W], F32) for i in range(2)]
            half = NCST // 2 + 1
            for i, (a, b) in enumerate([(0, min(IN_W, NCST)), (min(IN_W, NCST), NCST)]):
                if b <= a:
                    continue
                P.dma(stg[i][:, 0:b - a], cst_d[:, a:b], writes=[("stg", i)], sem="cst")
                P.op("dve", lambda e, i=i, a=a, b=b: e.tensor_copy(out=CB[:, a:b], in_=stg[i][:, 0:b - a]),
                     reads=[("stg", i)], writes=["CB"])
            assert NCST <= 2 * IN_W
            o, n = coff["ident"]
            P.dma(identf[:], cst_d[:, o:o + n], writes=["identf"], sem="cst")
            o, n = coff["hones"]
            P.dma(honesf[:], cst_d[:, o:o + n], writes=["honesf"], sem="cst")
            o, n = coff["restart"]
            P.dma(restart[:], cst_d[:, o:o + n], writes=["restart"], sem="cst")
            P.op("dve", lambda e: e.tensor_scalar(out=pder[:, 0:14], in0=pcol[:, 0:14], scalar1=-1.0, scalar2=1.0, op0=ALU.mult, op1=ALU.add),
                 reads=["pcol"], writes=["pder"])
            P.op("dve", lambda e: e.tensor_scalar(out=pder[:, 14:18], in0=pcol[:, 26:30], scalar1=-1.0, scalar2=1.0, op0=ALU.mult, op1=ALU.add),
                 reads=["pcol"], writes=["pder"])
            M.release(m0)

            mA = M.mark()
            WIN = M.alloc("WIN", [128, 8, IN_W], BF16)
            WOUT = M.alloc("WOUT", [128, 8, D], BF16)
            POOLW = M.alloc("POOLW", [128, 4, 128], BF16)
            L12 = M.alloc("L12", [128, 512], BF16)
            GLO = M.alloc("GLO", [128, 512], BF16)
            RKI = M.alloc("RKI", [128, 4, 8], BF16)
            LNG = M.alloc("LNG", [128, 512], BF16); LNB = M.alloc("LNB", [128, 512], BF16)
            m1 = M.mark()
            stg = [M.alloc(f"stgw{i}", [128, IN_W], F32) for i in range(2)]
            for c in range(8):
                s = c % 2
                P.dma(stg[s][:, :], w_in_d[c * 128:(c + 1) * 128, :], writes=[("stg", s)], sem=f"wst{s}")
                P.op("dve",
                     lambda e, c=c, s=s: e.tensor_scalar(out=WIN[:, c, :], in0=stg[s][:, :], scalar1=G1(c), scalar2=None, op0=ALU.mult),
                     reads=[("stg", s), "pcol"], writes=["WIN"])
            for c in range(8):
                s = c % 2
                P.dma(stg[s][:, 0:D], w_out_d[c * 128:(c + 1) * 128, :], writes=[("stg", s)], sem=f"wst{s}")
                P.op("dve", lambda e, c=c, s=s: e.tensor_copy(out=WOUT[:, c, :], in_=stg[s][:, 0:D]),
                     reads=[("stg", s)], writes=["WOUT"])
            small = [(POOLW[:].rearrange("p g d -> p (g d)"), None, "POOLW"), (L12[:], lora12_d[:, :], "L12"), (GLO[:], glora_d[:, :], "GLO")]
            for i, (dst, src, key) in enumerate(small):
                s = i % 2
                if key == "POOLW":
                    P.dma(stg[s][:, 0:512].rearrange("p (g d) -> p g d", g=4), poolw_d.rearrange("g c d -> c g d"), writes=[("stg", s)], sem=f"wst{s}")
                else:
                    P.dma(stg[s][:, 0:512], src, writes=[("stg", s)], sem=f"wst{s}")
                P.op("dve", lambda e, dst=dst, s=s: e.tensor_copy(out=dst, in_=stg[s][:, 0:512]), reads=[("stg", s)], writes=[key])
            for i, (dst, key) in enumerate([(LNG, "LNG"), (LNB, "LNB")]):
                P.dma(stg[i][:, 0:512], rows_d[i, 0:512].partition_broadcast(128), writes=[("stg", i)], sem=f"wst{i}")
                P.op("dve", lambda e: e.tensor_copy(out=dst[:], in_=stg[i][:, 0:512]), reads=[("stg", i)], writes=[key])
            for hp in range(4):
                o, n = coff["ind"]
                P.op("dve", lambda e, hp=hp, o=o: e.tensor_scalar(out=RKI[:, hp, :], in0=CB[:, o + hp * 8:o + hp * 8 + 8], scalar1=RKc(hp), scalar2=None, op0=ALU.mult),
                     reads=["CB", "pcol"], writes=["RKI"])
            P.barrier()
            M.release(m1)
            for ti in range(NTILE):
                P.dma(X1[:, ti, :], x_d[ti * 128:(ti + 1) * 128, :], writes=[("x1", ti)], sem=f"x{ti // 4}", chain=True)
            if KSTOP <= 1:
                raise _Stop()

            hTs = [M.alloc("hT0", [128, 8, ST + 1], BF16)] * 2
            sgzbs = [M.alloc("sgzb0", [128, ST], BF16)] * 2
            mixPs = [M.alloc("mixP0", [128, 4, ST], BF16)] * 2
            CX.update(hT=hTs[0], khT=("hT", 0), sgzb=sgzbs[0], ksg=("sgzb", 0), mixdst=None, mixsrc=None)
            hnb = M.alloc("hnb", [128, D], BF16)
            junk = hnb
            stat = M.alloc("stat", [128, 8], F32)
            dTb4 = M.alloc("dTb", [128, 4, ST], BF16)
            mixT = M.alloc("mixT", [128, 8, ST], BF16)
            plast = M.alloc("plast", [128, 16], F32)
            E1 = M.alloc("E1", [128, ST], F32); E3 = M.alloc("E3", [128, ST], F32); Z12 = E1; Z13 = E3
            z12b = M.alloc("z12b", [128, ST], BF16)
            RKV = [[M.alloc(f"{nm}{i}", [128, ST], F32) for nm in ("Rb", "Kb", "Vb")] for i in range(1)] * 2
            SG4 = M.alloc("SG4", [128, 4, ST], F32); AS4 = M.alloc("AS4", [128, 4, ST], F32)
            KK2 = M.alloc("KK2", [128, ST], F32); RN = M.alloc("RN", [128, ST], F32); tmpS = RN
            KKN = M.alloc("KKN", [128, ST], F32); Bb = M.alloc("Bb", [128, ST], F32)
            KF = M.alloc("KF", [128, ST], F32)
            Mf = M.alloc("Mf", [128, 4, 64], F32); Mb = M.alloc("Mb", [128, 4, 2, 64], BF16); Mt = M.alloc("Mt", [128, 4, 64], F32)
            Yw = M.alloc("Yw", [128, 512], F32); Yq = M.alloc("Yq", [128, 512], F32); Yo = M.alloc("Yo", [128, 512], BF16)
            gst = M.alloc("gst", [128, 40], F32)
            P.alias["mixT"] = [("mixT", c) for c in range(8)]
            mS = M.mark()
            SMP = M.alloc("SMP", [128, 6, 4, NS], F32)
            SQ = 8
            S0qs = [M.alloc(f"S0q{i}", [128, SQ, 64], F32) for i in range(2)]; S1qs = [M.alloc("S1q0", [128, SQ, 64], F32)] * 2
            Stq = M.alloc("Stq", [128, SQ, 64], F32)
            BH = M.alloc("BH", [128, 6, 64], F32)
            BHR = M.alloc("BHR", [128, 192], F32)
            sa_s = M.alloc("sa_s", [128, 64], F32); y_s = M.alloc("y_s", [128, 64], F32); y_s2 = M.alloc("y_s2", [128, 64], F32)
            spl = M.alloc("spl", [128, 2, 512], BF16)
            shT = M.alloc("shT", [128, 14, NS], F32); shtok = M.alloc("shtok", [128, SHIFT_W], F32)
            ptok = M.alloc("ptok", [128, SHIFT_W], F32)
            ytok = M.alloc("ytok", [128, 512], F32)
            Stq2 = ytok[:, :].rearrange("p (v k) -> p v k", k=64)
            splf = ytok
            stgp = shtok[:, 0:1024].rearrange("p (a c) -> p a c", a=2)

            o_su, _ = coff["m_su"]; o_iu, _ = coff["m_iu"]; o_sl, _ = coff["m_sl"]
            mask_ai = CB[:, o_su:o_su + 256].rearrange("p (a t) -> p a t", a=2)
            mask_sl = CB[:, o_sl:o_sl + 128]

            def rms_stats(eng_in, key, npart, col):
                P.op("act", lambda e: e.activation(out=junk[0:npart, :], in_=eng_in, func=AF.Square, accum_out=stat[0:npart, col:col + 1]),
                     reads=[key], writes=["hnb", ("stat", col)])

            def norm_to_hT(xin, key, npart, c0, ncols_total):
                if npart == 128:
                    pe_warm(6)
                rms_stats(xin, key, npart, 0)
                P.op("act", lambda e: e.activation(out=stat[0:npart, 1:2], in_=stat[0:npart, 0:1], func=AF.Ln, scale=1.0 / D, bias=stat[0:npart, 7:8]),
                     reads=[("stat", 0), ("stat", 7)], writes=[("stat", 1)])
                P.op("act", lambda e: e.activation(out=stat[0:npart, 2:3], in_=stat[0:npart, 1:2], func=AF.Exp, scale=-0.5), reads=[("stat", 1)], writes=[("stat", 2)])
                P.op("act", lambda e: e.activation(out=hnb[0:npart, :], in_=xin, func=AF.Copy, scale=stat[0:npart, 2:3]),
                     reads=[key, ("stat", 2)], writes=["hnb"])
                pb, pk = bank()
                pbb = pb[:].bitcast(BF16)
                for c in range(8):
                    P.op("pe", lambda e, c=c: e.transpose(out=pbb[:, c * 128:c * 128 + npart], in_=hnb[0:npart, c * 128:(c + 1) * 128], identity=identb[0:npart, 0:npart]),
                         reads=["hnb", "CB"], writes=[pk], inc=(c == 7))
                P.op("dve", lambda e: e.tensor_copy(out=CX["hT"][:, :, c0:c0 + npart], in_=pbb.rearrange("p (c t) -> p c t", c=8)[:, :, 0:npart]),
                     reads=[pk], writes=[CX["khT"]])

            def proj_fm(j, ncols):
                pb, pk = bank()
                for c in range(8):
                    P.op("pe", lambda e, c=c: e.matmul(pb[:, 0:ncols], lhsT=WIN[:, c, 512 + j * 128:512 + (j + 1) * 128], rhs=CX["hT"][:, c, 0:ncols], start=(c == 0), stop=(c == 7)),
                         reads=["WIN", CX["khT"]], writes=[pk], inc=(c == 7))
                return pb, pk

            def shift_evac(pb, pk, j, out, okey, ncols, prevT=None):
                P.op("act", lambda e: e.activation(out=tmpS[:, 0:ncols], in_=pb[:, 0:ncols], func=AF.Copy, scale=OMMU(j)),
                     reads=[pk, "pder"], writes=["RN"])
                if prevT is None:
                    raise AssertionError("prompt path uses shift_evac_p")
                else:
                    P.op("dve", lambda e: e.scalar_tensor_tensor(out=out[:, 0:ncols], in0=prevT, scalar=MU(j), in1=tmpS[:, 0:ncols], op0=ALU.mult, op1=ALU.add),
                         reads=["shT", "RN", "pcol"], writes=[okey])

            sh_ctr = [0]

            def shift_evac_p(pb, pk, j, out, okey, last):
                tb, tkey = ((RN, "RN"), (KKN, "KKN"))[sh_ctr[0] % 2]
                sh_ctr[0] += 1
                P.op("act", lambda e: e.activation(out=tb[:, 0:ST], in_=pb[:, 1:ST + 1], func=AF.Copy, scale=OMMU(j)),
                     reads=[pk, "pder"], writes=[tkey])
                P.op("dve", lambda e: e.scalar_tensor_tensor(out=out[:, 0:ST], in0=pb[:, 0:ST], scalar=MU(j), in1=tb[:, 0:ST], op0=ALU.mult, op1=ALU.add),
                     reads=[pk, tkey, "pcol"], writes=[okey])
                if last:
                    P.op("dve", lambda e: e.tensor_copy(out=plast[:, j:j + 1], in_=pb[:, ST:ST + 1]), reads=[pk], writes=[("plast", j)])

            def lora_stage(ncols):
                P.op("act", lambda e: e.activation(out=z12b[0:64, 0:ncols], in_=Z12[0:64, 0:ncols], func=AF.Tanh), reads=["E1"], writes=["z12b"])
                P.op("dve", lambda e: e.tensor_copy(out=z12b[64:128, 0:ncols], in_=Z12[64:128, 0:ncols]), reads=["E1"], writes=["z12b"])
                P.op("act", lambda e: e.activation(out=CX["sgzb"][:, 0:ncols], in_=Z13[:, 0:ncols], func=AF.Sigmoid), reads=["E3"], writes=[CX["ksg"]])

            def sigm_stage(ncols):
                n = ncols
                for hp in range(4):
                    pb, pk = bank()
                    P.op("pe", lambda e: e.matmul(pb[:, 0:n], lhsT=L12[0:64, hp * 128:(hp + 1) * 128], rhs=z12b[0:64, 0:n], start=True, stop=True),
                         reads=["L12", "z12b"], writes=[pk])
                    pb2, pk2 = bank()
                    P.op("pe", lambda e: e.matmul(pb2[:, 0:n], lhsT=L12[64:128, hp * 128:(hp + 1) * 128], rhs=z12b[64:128, 0:n], start=True, stop=True),
                         reads=["L12", "z12b"], writes=[pk2])
                    P.op("act", lambda e: e.activation(out=SG4[:, hp, 0:n], in_=pb[:, 0:n], func=AF.Sigmoid, bias=W0(hp)), reads=[pk, "pcol"], writes=[("SG", hp)])
                    P.op("act", lambda e: e.activation(out=AS4[:, hp, 0:n], in_=pb2[:, 0:n], func=AF.Sigmoid, bias=A0(hp)), reads=[pk2, "pcol"], writes=[("AS", hp)])

            def preprocess(hp, ncols, sample):
                n = ncols
                Rb, Kb, Vb = RKV[hp % 2]
                kR, kK, kV = ("Rb", 0), ("Kb", 0), ("Vb", 0)
                SG, AS = SG4[:, hp, :], AS4[:, hp, :]
                P.op("act", lambda e: e.activation(out=KK2[:, 0:n], in_=Kb[:, 0:n], func=AF.Square, scale=KKc(hp)), reads=[kK, "pcol"], writes=["KK2"])
                pb3, pk3 = bank()
                P.op("pe", lambda e: e.matmul(pb3[:, 0:n], lhsT=honesf[:], rhs=KK2[:, 0:n], start=True, stop=True), reads=["honesf", "KK2"], writes=[pk3])
                if not sample:
                    pw, pwk = bank()
                    for _ in range(10):
                        P.op("pe", lambda e: e.matmul(pw[:, :], lhsT=identb, rhs=CB[:, 0:512], start=True, stop=True), reads=["CB"], writes=[pwk], inc=False)
                P.op("act", lambda e: e.activation(out=RN[:, 0:n], in_=pb3[:, 0:n], func=AF.Ln, bias=stat[:, 6:7]), reads=[pk3, ("stat", 6)], writes=["RN"])
                P.op("act", lambda e: e.activation(out=RN[:, 0:n], in_=RN[:, 0:n], func=AF.Exp, scale=-0.5), reads=["RN"], writes=["RN"])
                P.op("dve", lambda e: e.scalar_tensor_tensor(out=KKN[:, 0:n], in0=Kb[:, 0:n], scalar=KKc(hp), in1=RN[:, 0:n], op0=ALU.mult, op1=ALU.mult), reads=[kK, "RN", "pcol"], writes=["KKN"])
                P.op("dve", lambda e: e.tensor_tensor(out=Bb[:, 0:n], in0=KKN[:, 0:n], in1=AS[:, 0:n], op=ALU.mult), reads=["KKN", ("AS", hp)], writes=["Bb"])
                P.op("dve", lambda e: e.tensor_scalar(out=KK2[:, 0:n], in0=AS[:, 0:n], scalar1=KAc(hp), scalar2=OMKA(hp), op0=ALU.mult, op1=ALU.add),
                     reads=[("AS", hp), "pcol", "pder"], writes=["KK2"])
                P.op("dve", lambda e: e.tensor_tensor(out=KF[:, 0:n], in0=Kb[:, 0:n], in1=KK2[:, 0:n], op=ALU.mult), reads=[kK, "KK2"], writes=["KF"])
                if sample:
                    P.op("act", lambda e: e.activation(out=SMP[:, 0, hp, :], in_=Rb[:, 0:n], func=AF.Copy), reads=[kR], writes=["SMP"])
                    P.op("act", lambda e: e.activation(out=SMP[:, 1, hp, :], in_=SG[:, 0:n], func=AF.Exp, scale=CW), reads=[("SG", hp)], writes=["SMP"])
                    P.op("dve", lambda e: e.tensor_copy(out=SMP[:, 2, hp, :], in_=KF[:, 0:n]), reads=["KF"], writes=["SMP"])
                    P.op("act", lambda e: e.activation(out=SMP[:, 3, hp, :], in_=Vb[:, 0:n], func=AF.Copy), reads=[kV], writes=["SMP"])
                    P.op("dve", lambda e: e.tensor_scalar(out=SMP[:, 4, hp, :], in0=KKN[:, 0:n], scalar1=-1.0, scalar2=None, op0=ALU.mult), reads=["KKN"], writes=["SMP"])
                    P.op("dve", lambda e: e.tensor_copy(out=SMP[:, 5, hp, :], in_=Bb[:, 0:n]), reads=["Bb"], writes=["SMP"])
                    return
                P.op("pool", lambda e: e.tensor_tensor(out=RKB[:, hp, 0:n], in0=Rb[:, 0:n], in1=KF[:, 0:n], op=ALU.mult), reads=[kR, "KF"], writes=["RKB"])
                P.op("pool", lambda e: e.tensor_copy(out=VBF[:, hp, 0:n], in_=Vb[:, 0:n]), reads=[kV], writes=["VBF"])
                P.op("dve", lambda e: e.tensor_tensor_scan(out=CUM[:, 0:n], data0=restart[:, 0:n], data1=SG[:, 0:n], initial=0.0, op0=ALU.mult, op1=ALU.add),
                     reads=["restart", ("SG", hp)], writes=["CUM"])
                P.op("act", lambda e: e.activation(out=E1[:, 0:n], in_=CUM[:, 0:n], func=AF.Exp, scale=CW), reads=["CUM"], writes=["E1"])
                P.op("dve", lambda e: e.tensor_tensor(out=ART[:, hp, 1, 0:n], in0=Rb[:, 0:n], in1=E1[:, 0:n], op=ALU.mult), reads=[kR, "E1"], writes=["ART"])
                P.op("dve", lambda e: e.tensor_tensor(out=CM[:, 0:n], in0=CUM[:, 0:n], in1=SG[:, 0:n], op=ALU.subtract), reads=["CUM", ("SG", hp)], writes=["E2"])
                P.op("act", lambda e: e.activation(out=E2[:, 0:n], in_=CM[:, 0:n], func=AF.Exp, scale=CW), reads=["E2"], writes=["E2"])
                P.op("dve", lambda e: e.scalar_tensor_tensor(out=ART[:, hp, 0, 0:n], in0=KKN[:, 0:n], scalar=-1.0, in1=E2[:, 0:n], op0=ALU.mult, op1=ALU.mult),
                     reads=["KKN", "E2"], writes=["ART"])
                P.op("act", lambda e: e.activation(out=E3[:, 0:n], in_=CUM[:, 0:n], func=AF.Exp, scale=-CW), reads=["CUM"], writes=["E3"])
                P.op("dve", lambda e: e.tensor_tensor(out=BT[:, hp, 0:n], in0=Bb[:, 0:n], in1=E3[:, 0:n], op=ALU.mult), reads=["Bb", "E3"], writes=["BT"])
                P.op("pool", lambda e: e.tensor_tensor(out=KT[:, hp, 0:n], in0=KF[:, 0:n], in1=E3[:, 0:n], op=ALU.mult), reads=["KF", "E3"], writes=["KT"])
                P.op("act", lambda e: e.activation(out=WC[:, hp, :], in_=E1[:, 0:n].rearrange("p (c t) -> p c t", t=128)[:, :, 127], func=AF.Copy),
                     reads=["E1"], writes=["WC"])

            def out_proj(npart, c0, x1ap, x1key, ti_stat):
                for dh in range(2):
                    pb, pk = bank()
                    for c in range(8):
                        msrc, mkey = (mixT[:, c, c0:c0 + npart], ("mixT", c)) if (c >= 4 or CX["mixsrc"] is None) else CX["mixsrc"](c, c0, npart)
                        P.op("pe", lambda e, c=c, dh=dh, pb=pb: e.matmul(pb[0:npart, :], lhsT=msrc, rhs=WOUT[:, c, dh * 512:(dh + 1) * 512], start=(c == 0), stop=(c == 7)),
                             reads=[mkey, "WOUT"], writes=[pk], inc=(c == 7))
                    P.op("dve", lambda e, dh=dh, pb=pb: e.tensor_tensor(out=x1ap[:, dh * 512:(dh + 1) * 512], in0=pb[0:npart, :], in1=x1ap[:, dh * 512:(dh + 1) * 512], op=ALU.add),
                         reads=[pk, x1key], writes=[x1key])
                rms_stats(x1ap, x1key, npart, 3)
                P.op("dve", lambda e: e.tensor_scalar(out=stat[0:npart, 4:5], in0=stat[0:npart, 3:4], scalar1=1.0 / D, scalar2=RMS_EPS, op0=ALU.mult, op1=ALU.add),
                     reads=[("stat", 3)], writes=[("stat", 4)])
                P.op("dve", lambda e: e.reciprocal(out=rs2[0:npart, ti_stat:ti_stat + 1], in_=stat[0:npart, 4:5]), reads=[("stat", 4)], writes=["rs2"])

            def pool_mix_a(pd, pdk, g, ncols):
                P.op("act" if g % 2 else "dve", (lambda e: e.activation(out=dTb4[:, g, 0:ncols], in_=pd[:, 0:ncols], func=AF.Copy)) if g % 2 else (lambda e: e.tensor_copy(out=dTb4[:, g, 0:ncols], in_=pd[:, 0:ncols])),
                     reads=[pdk], writes=[("dTb", g)])

            def pool_mix_b(g, ncols):
                pb, pk = bank()
                P.op("pe", lambda e: e.matmul(pb[:, 0:ncols], lhsT=POOLW[:, g, :], rhs=dTb4[:, g, 0:ncols], start=True, stop=True), reads=["POOLW", ("dTb", g)], writes=[pk])
                mdst, mkey = (mixT[:, g, 0:ncols], ("mixT", g)) if CX["mixdst"] is None else CX["mixdst"](g)
                P.op("act", lambda e: e.activation(out=mdst, in_=pb[:, 0:ncols], func=AF.Copy, scale=PSC(g)), reads=[pk, "pcol"], writes=[mkey])

            def gn_stage(ysrc, ykey, npart, nh, eps_col):
                y3 = lambda ap: ap.rearrange("p (h v) -> p h v", v=64)
                P.op("dve", lambda e: e.tensor_reduce(out=gst[0:npart, 0:nh], in_=y3(ysrc), axis=AX.X, op=ALU.add), reads=[ykey], writes=["gst"])
                P.op("dve", lambda e: e.tensor_scalar(out=gst[0:npart, 8:8 + nh], in0=gst[0:npart, 0:nh], scalar1=1.0 / 64, scalar2=None, op0=ALU.mult), reads=["gst"], writes=["gst"])
                P.op("dve", lambda e: e.tensor_tensor(out=y3(Yw[0:npart, 0:nh * 64]), in0=y3(ysrc), in1=gst[0:npart, 8:8 + nh].unsqueeze(2).to_broadcast([npart, nh, 64]), op=ALU.subtract),
                     reads=[ykey, "gst"], writes=["Yw"])
                P.op("act", lambda e: e.activation(out=Yq[0:npart, 0:nh * 64], in_=Yw[0:npart, 0:nh * 64], func=AF.Square), reads=["Yw"], writes=["Yq"])
                P.op("dve", lambda e: e.tensor_reduce(out=gst[0:npart, 16:16 + nh], in_=y3(Yq[0:npart, 0:nh * 64]), axis=AX.X, op=ALU.add), reads=["Yq"], writes=["gst"])
                P.op("act", lambda e: e.activation(out=gst[0:npart, 24:24 + nh], in_=gst[0:npart, 16:16 + nh], func=AF.Ln, scale=1.0 / 64, bias=stat[0:npart, eps_col:eps_col + 1]),
                     reads=["gst", ("stat", eps_col)], writes=["gst"])
                P.op("act", lambda e: e.activation(out=gst[0:npart, 32:32 + nh], in_=gst[0:npart, 24:24 + nh], func=AF.Exp, scale=-0.5), reads=["gst"], writes=["gst"])
                P.op("dve", lambda e: e.tensor_tensor(out=y3(Yw[0:npart, 0:nh * 64]), in0=y3(Yw[0:npart, 0:nh * 64]), in1=gst[0:npart, 32:32 + nh].unsqueeze(2).to_broadcast([npart, nh, 64]), op=ALU.mult),
                     reads=["Yw", "gst"], writes=["Yw"])

            P.op("pool", lambda e: e.memset(stat[:, 7:8], RMS_EPS), writes=[("stat", 7)])
            P.op("pool", lambda e: e.memset(stat[:, 6:7], 1e-12), writes=[("stat", 6)])
            P.op("pool", lambda e: e.memset(stat[:, 5:6], GN_EPS), writes=[("stat", 5)])
            P.op("pool", lambda e: e.memset(plast[:], 0.0), writes=[("plast", j) for j in range(14)])
            P.op("pool", lambda e: e.memset(Mf[:], 0.0), writes=["Mf"])
            P.op("pool", lambda e: e.memset(Mb[:], 0.0), writes=["Mb"])

            n = NS
            norm_to_hT(X1s[0:n, :], "x1s", n, 0, n)
            for (c0, c1, dst) in [(0, 512, None), (512, 1024, 0), (1024, 1536, 512), (1536, 2048, 1024), (2048, 2304, 1536)]:
                pb, pk = bank()
                w = c1 - c0
                for c in range(8):
                    P.op("pe", lambda e, c=c, pb=pb, c0=c0, c1=c1, w=w: e.matmul(pb[0:n, 0:w], lhsT=CX["hT"][:, c, 0:n], rhs=WIN[:, c, c0:c1], start=(c == 0), stop=(c == 7)),
                         reads=["WIN", CX["khT"]], writes=[pk], inc=(c == 7))
                if dst is None:
                    P.op("act", lambda e, pb=pb: e.activation(out=splf[0:n, :], in_=pb[0:n, :], func=AF.Copy), reads=[pk], writes=["ytok"])
                else:
                    P.op("act", lambda e, pb=pb, dst=dst, w=w: e.activation(out=ptok[0:n, dst:dst + w], in_=pb[0:n, 0:w], func=AF.Copy), reads=[pk], writes=["ptok"])
            P.dma(shs_d[:, :], ptok[0:n, :], reads=["ptok"], sem="outs")
            P.dma(pls_d[:, 14, :], splf[0:n, :], reads=["ytok"], sem="outs")
            P.dma(pls_d[:, 0:14, :], spool_d.rearrange("(b j) c -> b j c", j=15)[:, 1:15, :], sem="outs")
            P.dma(shtok[0:n, :], sshift_d[:, :], writes=["shtok"], sem="sin")
            pbs = []
            for j in range(14):
                if j % 8 == 0:
                    pb, pk = bank()
                    pbs.append((pb, pk))
                P.op("pe", lambda e, j=j, pb=pb: e.transpose(out=pb[:, (j % 8) * n:(j % 8 + 1) * n], in_=shtok[0:n, j * 128:(j + 1) * 128], identity=identf[0:n, 0:n]),
                     reads=["shtok", "identf"], writes=[pk], inc=(j % 8 == 7 or j == 13))
            P.op("dve", lambda e: e.tensor_copy(out=shT[:, 0:8, :], in_=pbs[0][0][:, 0:8 * n].rearrange("p (j b) -> p j b", b=n)), reads=[pbs[0][1]], writes=["shT"])
            P.op("dve", lambda e: e.tensor_copy(out=shT[:, 8:14, :], in_=pbs[1][0][:, 0:6 * n].rearrange("p (j b) -> p j b", b=n)), reads=[pbs[1][1]], writes=["shT"])
            for tl in range(2):
                P.dma(stgp[0:120, tl, :], spool_d[tl * 120:(tl + 1) * 120, :], writes=["shtok"], sem="sin")
            P.op("dve", lambda e: e.tensor_copy(out=spl[0:120, :, :], in_=stgp[0:120, :, :]), reads=["shtok"], writes=["spl"])
            o_sel, _ = coff["sel"]
            wins = (2, 4, 8, 16)
            for g in range(4):
                pu, puk = bank()
                for c in range(8):
                    P.op("pe", lambda e, c=c, pu=pu, g=g: e.matmul(pu[:, 0:n], lhsT=WIN[:, c, g * 128:(g + 1) * 128], rhs=CX["hT"][:, c, 0:n], start=(c == 0), stop=(c == 7)),
                         reads=["WIN", CX["khT"]], writes=[puk], inc=(c == 7))
                P.op("act", lambda e, pu=pu: e.activation(out=tmpS[:, 0:n], in_=pu[:, 0:n], func=AF.Copy, scale=1.0 / wins[g] - 1.0), reads=[puk], writes=["RN"])
                pd, pdk = bank()
                for tl in range(2):
                    P.op("pe", lambda e, tl=tl, g=g, pd=pd: e.matmul(pd[:, 0:n], lhsT=spl[0:120, tl, g * 128:(g + 1) * 128], rhs=CB[0:120, o_sel + tl * 64 + g * 16:o_sel + tl * 64 + g * 16 + 16], start=(tl == 0), stop=(tl == 1)),
                         reads=["spl", "CB"], writes=[pdk], inc=(tl == 1))
                P.op("dve", lambda e, pd=pd: e.tensor_tensor(out=dTb4[:, 0, 0:n], in0=pd[:, 0:n], in1=tmpS[:, 0:n], op=ALU.add), reads=[pdk, "RN"], writes=[("dTb", 0)])
                pb, pk = bank()
                P.op("pe", lambda e, pb=pb, g=g: e.matmul(pb[:, 0:n], lhsT=POOLW[:, g, :], rhs=dTb4[:, 0, 0:n], start=True, stop=True), reads=["POOLW", ("dTb", 0)], writes=[pk])
                P.op("act", lambda e, pb=pb, g=g: e.activation(out=mixT[:, g, 0:n], in_=pb[:, 0:n], func=AF.Copy, scale=PSC(g)), reads=[pk, "pcol"], writes=["mixT"])
            for j, (dst, key) in [(12, (Z12, "E1")), (13, (Z13, "E3"))]:
                pb, pk = proj_fm(j, n)
                shift_evac(pb, pk, j, dst, key, n, prevT=shT[:, j, :])
            lora_stage(n)
            sigm_stage(n)
            def proj_pair_s(hp):
                for q, nm in enumerate(("Rb", "Kb", "Vb")):
                    j = 4 * q + hp
                    pb, pk = proj_fm(j, n)
                    shift_evac(pb, pk, j, RKV[hp % 2][q], (nm, 0), n, prevT=shT[:, j, :])
            for hp in range(4):
                proj_pair_s(hp)
                preprocess(hp, n, True)
            for q in range(6):
                pb, pk = bank()
                for hp in range(4):
                    P.op("pe", lambda e, q=q, hp=hp, pb=pb: e.transpose(out=pb[0:n, hp * 128:(hp + 1) * 128], in_=SMP[:, q, hp, :], identity=identf[:]),
                         reads=["SMP", "identf"], writes=[pk], inc=(hp == 3))
                P.op("act" if q % 2 else "dve", (lambda e, pb=pb: e.activation(out=ytok[0:n, :], in_=pb[0:n, :], func=AF.Copy)) if q % 2 else (lambda e, pb=pb: e.tensor_copy(out=ytok[0:n, :], in_=pb[0:n, :])),
                     reads=[pk], writes=["ytok"])
                P.dma(scr_d[q], ytok[0:n, :], reads=["ytok"], writes=[("scr", q)], sem="scr")
                P.dma(BH[:, q, :], scr_d[q].rearrange("b (h k) -> (b h) k", k=64), reads=[("scr", q)], writes=["BH"], sem="scr")
            P.dma(BHR[:], bh_d[:, :], writes=["BHR"], sem="sin")
            bq = lambda q: BH[:, q, :]
            bc_v = lambda ap: ap.unsqueeze(1).to_broadcast([128, SQ, 64])
            for sq in range(64 // SQ):
                vs = slice(sq * SQ, (sq + 1) * SQ)
                bc_k = lambda ap: ap[:, vs].unsqueeze(2).to_broadcast([128, SQ, 64])
                bi = sq % 2
                S0q, S1q = S0qs[bi], S1qs[bi]
                k0, k1 = ("S0q", bi), ("S1q", 0)
                if sq == 0:
                    P.dma(S0q[:].rearrange("p v k -> p (v k)"), swkv_d[:, 0:SQ * 64], writes=[k0], sem="sin0")
                if sq + 1 < 64 // SQ:
                    P.dma(S0qs[1 - bi][:].rearrange("p v k -> p (v k)"), swkv_d[:, (sq + 1) * SQ * 64:(sq + 2) * SQ * 64], writes=[("S0q", 1 - bi)], sem=f"sin{1 - bi}")
                P.op("pool", lambda e, bc_k=bc_k: e.tensor_tensor(out=Stq2[:], in0=bc_v(bq(2)), in1=bc_k(bq(3)), op=ALU.mult), reads=["BH"], writes=["ytok"])
                P.op("dve", lambda e: e.tensor_tensor(out=Stq[:], in0=S0q[:], in1=bc_v(bq(4)), op=ALU.mult), reads=[k0, "BH"], writes=["Stq"])
                P.op("dve", lambda e, vs=vs: e.tensor_reduce(out=sa_s[:, vs], in_=Stq[:], axis=AX.X, op=ALU.add), reads=["Stq"], writes=["sa_s"])
                P.op("dve", lambda e: e.tensor_tensor(out=S1q[:], in0=S0q[:], in1=bc_v(bq(1)), op=ALU.mult), reads=[k0, "BH"], writes=[k1])
                P.op("dve", lambda e, bc_k=bc_k: e.tensor_tensor(out=Stq[:], in0=bc_v(bq(5)), in1=bc_k(sa_s), op=ALU.mult), reads=["sa_s", "BH"], writes=["Stq"])
                P.op("dve", lambda e: e.tensor_tensor(out=S1q[:], in0=S1q[:], in1=Stq[:], op=ALU.add), reads=[k1, "Stq"], writes=[k1])
                P.op("dve", lambda e: e.tensor_tensor(out=S1q[:], in0=S1q[:], in1=Stq2[:], op=ALU.add), reads=[k1, "ytok"], writes=[k1])
                P.dma(wkvs_d[:, sq * SQ * 64:(sq + 1) * SQ * 64], S1q[:].rearrange("p v k -> p (v k)"), reads=[k1], sem=f"outw{bi}")
                P.op("dve", lambda e: e.tensor_tensor(out=Stq[:], in0=S1q[:], in1=bc_v(bq(0)), op=ALU.mult), reads=[k1, "BH"], writes=["Stq"])
                P.op("dve", lambda e, vs=vs: e.tensor_reduce(out=y_s[:, vs], in_=Stq[:], axis=AX.X, op=ALU.add), reads=["Stq"], writes=["y_s"])
            gn_stage(y_s[:, :], "y_s", 128, 1, 5)
            P.op("dve", lambda e: e.tensor_tensor(out=y_s2[:], in0=Yw[:, 0:64], in1=BHR[:, 0:64], op=ALU.mult), reads=["Yw", "BHR"], writes=["y_s2"])
            P.op("dve", lambda e: e.tensor_tensor(out=y_s2[:], in0=y_s2[:], in1=BHR[:, 64:128], op=ALU.add), reads=["y_s2", "BHR"], writes=["y_s2"])
            P.op("dve", lambda e: e.tensor_tensor(out=y_s[:], in0=bq(0), in1=bq(2), op=ALU.mult), reads=["BH", "y_s"], writes=["y_s"])
            P.op("dve", lambda e: e.tensor_tensor(out=y_s[:], in0=y_s[:], in1=BHR[:, 128:192], op=ALU.mult), reads=["y_s", "BHR"], writes=["y_s"])
            P.op("dve", lambda e: e.tensor_reduce(out=gst[:, 39:40], in_=y_s[:], axis=AX.X, op=ALU.add), reads=["y_s"], writes=["gst"])
            P.op("dve", lambda e: e.scalar_tensor_tensor(out=y_s2[:], in0=bq(3), scalar=gst[:, 39:40], in1=y_s2[:], op0=ALU.mult, op1=ALU.add), reads=["BH", "gst", "y_s2"], writes=["y_s2"])
            P.dma(scr_d[6].rearrange("b (h k) -> (b h) k", k=64), y_s2[:], reads=["y_s2"], writes=[("scr", 6)], sem="scr")
            P.dma(ytok[0:n, :], scr_d[6], reads=[("scr", 6)], writes=["ytok"], sem="scr")
            pg, pgk = bank()
            P.op("pe", lambda e: e.matmul(pg[0:n, :], lhsT=CX["sgzb"][:, 0:n], rhs=GLO[:], start=True, stop=True), reads=[CX["ksg"], "GLO"], writes=[pgk])
            P.op("dve", lambda e: e.tensor_tensor(out=Yo[0:n, :], in0=pg[0:n, :], in1=ytok[0:n, :], op=ALU.mult), reads=[pgk, "ytok"], writes=["Yo"])
            pb, pk = bank()
            pbb = pb[:].bitcast(BF16)
            for hp in range(4):
                P.op("pe", lambda e, hp=hp: e.transpose(out=pbb[:, hp * n:(hp + 1) * n], in_=Yo[0:n, hp * 128:(hp + 1) * 128], identity=identb[0:n, 0:n]),
                     reads=["Yo", "CB"], writes=[pk], inc=(hp == 3))
            P.op("dve", lambda e: e.tensor_copy(out=mixT[:, 4:8, 0:n], in_=pbb[:, 0:4 * n].rearrange("p (c t) -> p c t", t=n)), reads=[pk], writes=["mixT"])
            out_proj(n, 0, X1s[0:n, :], "x1s", NTILE)

            if KSTOP <= 2:
                raise _Stop()
            P.barrier()
            M.release(mS)
            ubf = M.alloc("ubf", [128, 3, 512], BF16)
            uf32 = Yq
            CUM = M.alloc("CUM", [128, ST], F32)
            E2 = M.alloc("E2", [128, ST], F32)
            CM = E2
            ART = M.alloc("ART", [128, 4, 2, ST], BF16)
            BT = M.alloc("BT", [128, 4, ST], BF16); KT = M.alloc("KT", [128, 4, ST], BF16)
            VBF = M.alloc("VBF", [128, 4, ST], BF16); RKB = M.alloc("RKB", [128, 4, ST], BF16)
            WC = M.alloc("WC", [128, 4, ST // 128], F32)
            btok = M.alloc("btok", [128, 512], BF16); ktok = M.alloc("ktok", [128, 512], BF16); vtok = M.alloc("vtok", [128, 512], BF16)
            L0 = M.alloc("L0", [128, 8, 128], BF16); Q0 = M.alloc("Q0", [128, 8, 128], BF16)
            UA = M.alloc("UA", [128, 8, 2, 128], BF16)
            KA = M.alloc("KA", [128, 8, 2, 128], BF16)
            Qb = [Q0, Q0]
            Xsb = KKN[:].bitcast(BF16); SAsb = Bb[:].bitcast(BF16)
            def P1gen(st):
                ntl = ST // 128
                for tl in range(ntl):
                    ti = st * ntl + tl
                    if tl == 0:
                        if st == 0:
                            P.op("pool", lambda e: e.memset(CX["hT"][:, :, 0:1], 0.0), writes=[CX["khT"]])
                        else:
                            P.op("dve", lambda e: e.tensor_copy(out=CX["hT"][:, :, 0:1], in_=hTs[0][:, :, ST:ST + 1]), reads=[("hT", 0)], writes=[CX["khT"]])
                    norm_to_hT(X1[:, ti, :], ("x1", ti), 128, 1 + tl * 128, ST)
                    yield
                for tl in range(ntl):
                    ti = st * ntl + tl
                    pb, pk = bank()
                    for c in range(8):
                        P.op("pe", lambda e, c=c, pb=pb, tl=tl: e.matmul(pb[:, :], lhsT=CX["hT"][:, c, 1 + tl * 128:1 + (tl + 1) * 128], rhs=WIN[:, c, 0:512], start=(c == 0), stop=(c == 7)),
                             reads=["WIN", CX["khT"]], writes=[pk], inc=(c == 7))
                    P.op("act", lambda e, pb=pb, ti=ti: e.activation(out=ubf[:, ti % 3, :], in_=pb[:, :], func=AF.Copy), reads=[pk], writes=[("ubf", ti % 3)])
                    if ti == NTILE - 1 and KVAR != 2 and KVAR != 3:
                        P.op("act", lambda e, pb=pb: e.activation(out=uf32[:, :], in_=pb[:, :], func=AF.Copy), reads=[pk], writes=["Yq"])
                        if KVAR == 6:
                            continue
                        pb2, pk2 = bank()
                        P.op("pe", lambda e: e.matmul(pb2[0:16, :], lhsT=identf[:, 112:128], rhs=uf32[:, :], start=True, stop=True),
                             reads=["Yq", "identf"], writes=[pk2])
                        P.op("dve", lambda e: e.tensor_copy(out=Yw[0:16, :], in_=pb2[0:16, :]), reads=[pk2], writes=["Yw"])
                        if KVAR != 5:
                            P.dma(plp_d[:, :], Yw[0:16, :], reads=["Yw"], sem="outp")
                yield
                ob, _ = coff["band"]; obp, _ = coff["bprev"]; obh, _ = coff["bfhi"]; obl, _ = coff["bflo"]
                for g in range(4):
                    pd, pdk = bank()
                    for tl in range(ntl):
                        ti = st * ntl + tl
                        lhs = ubf[:, ti % 3, g * 128:(g + 1) * 128]
                        ocol = pd[:, tl * 128:(tl + 1) * 128]
                        if ti == 0:
                            P.op("pe", lambda e, lhs=lhs, ocol=ocol, g=g: e.matmul(ocol, lhsT=lhs, rhs=CB[:, obh + g * 128:obh + (g + 1) * 128], start=True, stop=False),
                                 reads=[("ubf", 0), "CB"], writes=[pdk], inc=False)
                            P.op("pe", lambda e, lhs=lhs, ocol=ocol, g=g: e.matmul(ocol, lhsT=lhs, rhs=CB[:, obl + g * 128:obl + (g + 1) * 128], start=False, stop=True),
                                 reads=[("ubf", 0), "CB"], writes=[pdk], inc=True)
                        else:
                            lhsp = ubf[:, (ti - 1) % 3, g * 128:(g + 1) * 128]
                            P.op("pe", lambda e, lhs=lhs, ocol=ocol, g=g: e.matmul(ocol, lhsT=lhs, rhs=CB[:, ob + g * 128:ob + (g + 1) * 128], start=True, stop=False),
                                 reads=[("ubf", ti % 3), "CB"], writes=[pdk], inc=False)
                            P.op("pe", lambda e, lhsp=lhsp, ocol=ocol, g=g: e.matmul(ocol, lhsT=lhsp, rhs=CB[:, obp + g * 128:obp + (g + 1) * 128], start=False, stop=True),
                                 reads=[("ubf", (ti - 1) % 3), "CB"], writes=[pdk], inc=True)
                    pool_mix_a(pd, pdk, g, ST)
                    yield
                for g in range(4):
                    pool_mix_b(g, ST)
                    yield
                for j, (dst, key) in [(12, (Z12, "E1")), (13, (Z13, "E3"))]:
                    pb, pk = proj_fm(j, ST + 1)
                    shift_evac_p(pb, pk, j, dst, key, st == T // ST - 1)
                    yield
                lora_stage(ST)
                sigm_stage(ST)
                yield

            def P2(st):
                def proj_pair(hp):
                    for q, nm in enumerate(("Rb", "Kb", "Vb")):
                        j = 4 * q + hp
                        pb, pk = proj_fm(j, ST + 1)
                        shift_evac_p(pb, pk, j, RKV[hp % 2][q], (nm, 0), st == T // ST - 1)
                for hp in range(4):
                    proj_pair(hp)
                    preprocess(hp, ST, False)
                if st == T // ST - 1 and KVAR != 2 and KVAR != 4:
                    pb, pk = bank()
                    P.op("pe", lambda e: e.transpose(out=pb[0:14, 0:128], in_=plast[:, 0:14], identity=identf[:]),
                         reads=[("plast", j) for j in range(14)] + ["identf"], writes=[pk])
                    P.op("dve", lambda e: e.tensor_copy(out=tmpS[0:14, 0:128], in_=pb[0:14, 0:128]), reads=[pk], writes=["RN"])
                    P.dma(shp_d[:, :], tmpS[0:14, 0:128], reads=["RN"], sem="outp")
                if DEBUG_TAPS and st == 0:
                    tap("art", ART[:].rearrange("p a b t -> p (a b t)"), [128, 8 * ST], ["ART"])
                    tap("bt", BT[:].rearrange("p a t -> p (a t)"), [128, 4 * ST], ["BT"])
                    tap("kt", KT[:].rearrange("p a t -> p (a t)"), [128, 4 * ST], ["KT"])
                    tap("mixT", mixT[:].rearrange("p a t -> p (a t)"), [128, 8 * ST], ["mixT"])

            def CHgen(st):
                ntl = ST // 128
                for ck in range(ntl):
                    ti = st * ntl + ck
                    cs = slice(ck * 128, (ck + 1) * 128)
                    for src, skey, dst, dkey in [(BT, "BT", btok, "btok"), (KT, "KT", ktok, "ktok"), (VBF, "VBF", vtok, "vtok")]:
                        pb, pk = bank()
                        pbb = pb[:].bitcast(BF16)
                        for hp in range(4):
                            P.op("pe", lambda e, hp=hp, src=src, pbb=pbb: e.transpose(out=pbb[:, hp * 128:(hp + 1) * 128], in_=src[:, hp, cs], identity=identb),
                                 reads=[skey, "CB"], writes=[pk], inc=(hp == 3))
                        P.op("act", lambda e, dst=dst, pbb=pbb: e.activation(out=dst[:, :], in_=pbb[:, 0:512], func=AF.Copy), reads=[pk], writes=[dkey])
                    yield
                    for hl in range(2):
                        pr = slice(hl * 64, hl * 64 + 64)
                        pl, plk = bank()
                        for hb in range(2):
                            pa, pak = bank()
                            pk2b, pk2k = bank()
                            for j in range(2):
                                h2 = 2 * hb + j
                                h = 2 * h2 + hl
                                rhs_ar = ART[pr, h2, :, cs]
                                P.op("pe", lambda e: e.matmul(pa[:, j * 256:(j + 1) * 256].rearrange("p (a t) -> p a t", a=2), lhsT=BT[pr, h2, cs], rhs=rhs_ar, start=True, stop=True),
                                     reads=["BT", "ART"], writes=[pak], inc=(j == 1))
                                P.op("pe", lambda e: e.matmul(pk2b[:, j * 256:(j + 1) * 256].rearrange("p (a t) -> p a t", a=2), lhsT=KT[pr, h2, cs], rhs=rhs_ar, start=True, stop=True),
                                     reads=["KT", "ART"], writes=[pk2k], inc=(j == 1))
                                P.op("pe", lambda e: e.matmul(pl[:, h2 * 128:(h2 + 1) * 128], lhsT=ART[pr, h2, 0, cs], rhs=BT[pr, h2, cs], start=True, stop=True),
                                     reads=["BT", "ART"], writes=[plk], inc=(j == 1))
                            if KVAR != 1:
                                mb4 = mask_ai.unsqueeze(1).to_broadcast([128, 2, 2, 128])
                                uav = UA[:].rearrange("p (b j l) a t -> p b j l a t", b=2, j=2, l=2)[:, hb, :, hl, :, :]
                                kav = KA[:].rearrange("p (b j l) a t -> p b j l a t", b=2, j=2, l=2)[:, hb, :, hl, :, :]
                                P.op("dve", lambda e: e.tensor_tensor(out=uav, in0=pa[:, :].rearrange("p (j a t) -> p j a t", j=2, a=2), in1=mb4, op=ALU.mult),
                                     reads=[pak, "CB"], writes=[("U", hb), "ARB"])
                                P.op("dve", lambda e: e.tensor_tensor(out=kav, in0=pk2b[:, :].rearrange("p (j a t) -> p j a t", j=2, a=2), in1=mb4, op=ALU.mult),
                                     reads=[pk2k, "CB"], writes=["AAK", "ARK"])
                        if KVAR != 1:
                            lav = L0[:].rearrange("p (a l) t -> p a l t", l=2)[:, :, hl, :]
                            P.op("dve", lambda e: e.tensor_tensor(out=lav, in0=pl[:, :].rearrange("p (a t) -> p a t", a=4), in1=mask_sl.unsqueeze(1).to_broadcast([128, 4, 128]), op=ALU.mult),
                                 reads=[plk, "CB"], writes=[("L", 0), ("L", 1)])
                    yield
                    P.op("pool", lambda e: e.tensor_tensor(out=Q0[:], in0=UA[:, :, 0, :], in1=identb.unsqueeze(1).to_broadcast([128, 8, 128]), op=ALU.add),
                         reads=[("U", 0), ("U", 1), "CB"], writes=[("Q", 0), ("Q", 1)])
                    for lv in range(1, 7):
                        need_u = lv < 6
                        bk = {}
                        for hg in range(2):
                            pu, puk = bank() if need_u else (None, None)
                            pl, plk = bank()
                            bk[hg] = (pu, puk, pl, plk)
                            for hh in range(4):
                                h = hg * 4 + hh
                                if need_u:
                                    P.op("pe", lambda e: e.matmul(pu[:, hh * 128:(hh + 1) * 128], lhsT=L0[:, h, :], rhs=UA[:, h, 0, :], start=True, stop=True),
                                         reads=[("L", hg), ("U", hg)], writes=[puk], inc=(hh == 3))
                                P.op("pe", lambda e: e.matmul(pl[:, hh * 128:(hh + 1) * 128], lhsT=UA[:, h, 0, :], rhs=L0[:, h, :], start=True, stop=True),
                                     reads=[("L", hg), ("U", hg)], writes=[plk], inc=(hh == 3))
                        for hg in range(2):
                            hs = slice(hg * 4, hg * 4 + 4)
                            pu, puk, pl, plk = bk[hg]
                            P.op("act", lambda e: e.activation(out=L0[:, hs, :], in_=pl[:, :].rearrange("p (h t) -> p h t", h=4), func=AF.Copy),
                                 reads=[plk], writes=[("L", hg)])
                            if need_u:
                                if hg == 0:
                                    P.op("act", lambda e: e.activation(out=UA[:, hs, 0, :], in_=pu[:, :].rearrange("p (h t) -> p h t", h=4), func=AF.Copy),
                                         reads=[puk], writes=[("U", hg)])
                                else:
                                    P.op("dve", lambda e: e.tensor_copy(out=UA[:, hs, 0, :], in_=pu[:, :].rearrange("p (h t) -> p h t", h=4)),
                                         reads=[puk], writes=[("U", hg)])
                        bq_ = {}
                        for hg in range(2):
                            pq, pqk = bank()
                            bq_[hg] = (pq, pqk)
                            pe_warm_small(2, pq, pqk)
                            for hh in range(4):
                                h = hg * 4 + hh
                                P.op("pe", lambda e: e.matmul(pq[:, hh * 128:(hh + 1) * 128], lhsT=L0[:, h, :], rhs=Q0[:, h, :], start=True, stop=True),
                                     reads=[("L", hg), ("Q", hg)], writes=[pqk], inc=(hh == 3))
                        for hg in range(2):
                            hs = slice(hg * 4, hg * 4 + 4)
                            pq, pqk = bq_[hg]
                            P.op("dve", lambda e: e.tensor_tensor(out=Q0[:, hs, :], in0=pq[:, :].rearrange("p (h t) -> p h t", h=4), in1=Q0[:, hs, :], op=ALU.add),
                                 reads=[pqk, ("Q", hg)], writes=[("Q", hg)])
                        yield
                    Qf = Qb[0]
                    px, pxk = bank()
                    pe_warm_small(2, px, pxk)
                    for h in range(8):
                        h2, hl = h // 2, h % 2
                        pr = slice(hl * 64, hl * 64 + 64)
                        oc = px[:, h * 64:(h + 1) * 64]
                        P.op("pe", lambda e, oc=oc, pr=pr, h2=h2: e.matmul(oc, lhsT=ART[:, h2, 0, cs], rhs=Mb[:, h2, hl, :], start=True, stop=False),
                             reads=["ART", "Mb"], writes=[pxk], inc=False)
                        P.op("pe", lambda e, oc=oc, h=h: e.matmul(oc, lhsT=KA[:, h, 0, :], rhs=vtok[:, h * 64:(h + 1) * 64], start=False, stop=True),
                             reads=["AAK", "vtok"], writes=[pxk], inc=(h == 7))
                    P.op("act", lambda e, px=px: e.activation(out=Xsb[:, :], in_=px[:, :], func=AF.Copy), reads=[pxk], writes=["KKN"])
                    yield
                    psa, psak = bank()
                    for h in range(8):
                        P.op("pe", lambda e, h=h, psa=psa: e.matmul(psa[:, h * 64:(h + 1) * 64], lhsT=Qf[:, h, :], rhs=Xsb[:, h * 64:(h + 1) * 64], start=True, stop=True),
                             reads=[("Q", h // 4), "KKN"], writes=[psak], inc=(h == 7))
                    P.op("act", lambda e, psa=psa: e.activation(out=SAsb[:, :], in_=psa[:, :], func=AF.Copy), reads=[psak], writes=["Bb"])
                    yield
                    py, pyk = bank()
                    for h in range(8):
                        h2, hl = h // 2, h % 2
                        pr = slice(hl * 64, hl * 64 + 64)
                        oc = py[:, h * 64:(h + 1) * 64]
                        P.op("pe", lambda e, oc=oc, pr=pr, h2=h2: e.matmul(oc, lhsT=ART[:, h2, 1, cs], rhs=Mb[:, h2, hl, :], start=True, stop=False),
                             reads=["ART", "Mb"], writes=[pyk], inc=False)
                        P.op("pe", lambda e, oc=oc, h=h: e.matmul(oc, lhsT=UA[:, h, 1, :], rhs=SAsb[:, h * 64:(h + 1) * 64], start=False, stop=False),
                             reads=["ARB", "Bb"], writes=[pyk], inc=False)
                        P.op("pe", lambda e, oc=oc, h=h: e.matmul(oc, lhsT=KA[:, h, 1, :], rhs=vtok[:, h * 64:(h + 1) * 64], start=False, stop=True),
                             reads=["ARK", "vtok"], writes=[pyk], inc=(h == 7))
                    pm, pmk = bank()
                    for h in range(8):
                        h2 = h // 2
                        oc = pm[:, h * 64:(h + 1) * 64]
                        P.op("pe", lambda e, oc=oc, h=h, h2=h2: e.matmul(oc, lhsT=btok[:, h2 * 128:(h2 + 1) * 128], rhs=SAsb[:, h * 64:(h + 1) * 64], start=True, stop=False),
                             reads=["btok", "Bb"], writes=[pmk], inc=False)
                        P.op("pe", lambda e, oc=oc, h=h, h2=h2: e.matmul(oc, lhsT=ktok[:, h2 * 128:(h2 + 1) * 128], rhs=vtok[:, h * 64:(h + 1) * 64], start=False, stop=True),
                             reads=["ktok", "vtok"], writes=[pmk], inc=(h == 7))
                    for hl in range(2):
                        pr = slice(hl * 64, hl * 64 + 64)
                        src = pm[pr, :].rearrange("p (a l v) -> p a l v", a=4, l=2)[:, :, hl, :]
                        P.op("dve", lambda e, pr=pr, src=src: e.tensor_tensor(out=Mt[pr, :, :], in0=src, in1=Mf[pr, :, :], op=ALU.add), reads=[pmk, "Mf"], writes=["Mt"])
                        P.op("dve", lambda e, pr=pr: e.tensor_tensor(out=Mf[pr, :, :], in0=Mt[pr, :, :], in1=WC[pr, :, ck:ck + 1].to_broadcast([64, 4, 64]), op=ALU.mult),
                             reads=["Mt", "WC"], writes=["Mf"])
                        P.op("act", lambda e, pr=pr: e.activation(out=Mb[pr, :, hl, :], in_=Mf[pr, :, :], func=AF.Copy), reads=["Mf"], writes=["Mb"])
                    pe_warm(10)
                    gn_stage(py[:, :], pyk, 128, 8, 5)
                    P.op("dve", lambda e: e.tensor_tensor(out=Yw[:, :], in0=Yw[:, :], in1=LNG[:, :], op=ALU.mult), reads=["Yw", "LNG"], writes=["Yw"])
                    P.op("dve", lambda e: e.tensor_tensor(out=Yw[:, :], in0=Yw[:, :], in1=LNB[:, :], op=ALU.add), reads=["Yw", "LNB"], writes=["Yw"])
                    pbn, pbnk = bank()
                    for hp in range(4):
                        P.op("pe", lambda e, hp=hp, pbn=pbn: e.matmul(pbn[:, 0:8], lhsT=RKB[:, hp, cs], rhs=RKI[:, hp, :], start=(hp == 0), stop=(hp == 3)),
                             reads=["RKB", "RKI"], writes=[pbnk], inc=(hp == 3))
                    P.op("act", lambda e, pbn=pbn: e.activation(out=gst[:, 0:8], in_=pbn[:, 0:8], func=AF.Copy), reads=[pbnk], writes=["gst"])
                    P.op("dve", lambda e: e.tensor_tensor(out=Yq[:, :].rearrange("p (h v) -> p h v", v=64), in0=vtok[:, :].rearrange("p (h v) -> p h v", v=64),
                                                           in1=gst[:, 0:8].unsqueeze(2).to_broadcast([128, 8, 64]), op=ALU.mult), reads=["vtok", "gst"], writes=["Yq"])
                    P.op("dve", lambda e: e.tensor_tensor(out=Yw[:, :], in0=Yw[:, :], in1=Yq[:, :], op=ALU.add), reads=["Yw", "Yq"], writes=["Yw"])
                    pg, pgk = bank()
                    P.op("pe", lambda e, pg=pg: e.matmul(pg[:, :], lhsT=CX["sgzb"][:, cs], rhs=GLO[:], start=True, stop=True), reads=[CX["ksg"], "GLO"], writes=[pgk])
                    P.op("dve", lambda e, pg=pg: e.tensor_tensor(out=Yo[:, :], in0=pg[:, :], in1=Yw[:, :], op=ALU.mult), reads=[pgk, "Yw"], writes=["Yo"])
                    pb, pk = bank()
                    pbb = pb[:].bitcast(BF16)
                    for hp in range(4):
                        P.op("pe", lambda e, hp=hp, pbb=pbb: e.transpose(out=pbb[:, hp * 128:(hp + 1) * 128], in_=Yo[:, hp * 128:(hp + 1) * 128], identity=identb),
                             reads=["Yo", "CB"], writes=[pk], inc=(hp == 3))
                    P.op("act", lambda e, pbb=pbb: e.activation(out=mixT[:, 4:8, cs], in_=pbb[:, 0:512].rearrange("p (c t) -> p c t", c=4), func=AF.Copy), reads=[pk], writes=["mixT"])
                    yield

            def OUT(st):
                ntl = ST // 128
                for tl in range(ntl):
                    ti = st * ntl + tl
                    out_proj(128, tl * 128, X1[:, ti, :], ("x1", ti), ti)
                if DEBUG_TAPS and st == 0:
                    tap("x1_0", X1[:, 0, :], [128, D], [("x1", 0)])

            def ctx(st, pn, banks):
                i = 0
                CX.update(hT=hTs[i], khT=("hT", i), sgzb=sgzbs[i], ksg=("sgzb", i), pn=pn, banks=banks,
                          mixdst=lambda g: (mixPs[i][:, g, 0:ST], ("mixP", i, g)),
                          mixsrc=lambda c, c0, npart: (mixPs[i][:, c, c0:c0 + npart], ("mixP", i, c)))

            ALLB = list(range(8))
            ctx(0, "all", ALLB)
            for _ in P1gen(0):
                pass
            for st in range(T // ST):
                ctx(st, "all", ALLB)
                P2(st)
                for _ in CHgen(st):
                    pass
                g_p1 = P1gen(st + 1) if st + 1 < T // ST else None
                if g_p1 is not None:
                    for _ in range(3):
                        next(g_p1)
                OUT(st)
                if g_p1 is not None:
                    for _ in g_p1:
                        pass
            if KSTOP <= 2.6:
                raise _Stop()
            for hp in range(4):
                pb, pk = bank()
                P.op("pe", lambda e, hp=hp, pb=pb: e.transpose(out=pb[0:64, 0:128], in_=Mf[:, hp, :], identity=identf[:]), reads=["Mf", "identf"], writes=[pk])
                P.op("dve", lambda e, pb=pb: e.tensor_copy(out=Yw[0:64, 0:128], in_=pb[0:64, 0:128]), reads=[pk], writes=["Yw"])
                P.dma(wkvp_d[2 * hp:2 * hp + 2].rearrange("l v k -> v l k"), Yw[0:64, 0:128].rearrange("p (l k) -> p l k", l=2), reads=["Yw"], sem="outp")

            if KSTOP <= 3:
                raise _Stop()
            P.barrier()
            M.release(mA)
            FB = 1024
            GF = M.alloc("GF", [128, D], F32)
            P.dma(GF[:], rows_d[2, :].partition_broadcast(128), writes=["GF"], sem="par")
            WUP = [M.alloc(f"WUP{i}", [128, 8, FB], BF16) for i in range(2)]
            WDN = [M.alloc(f"WDN{i}", [128, 8, D], BF16) for i in range(2)]
            stg = [M.alloc(f"stgm{i}", [128, FB], F32) for i in range(2)]
            xb16 = M.alloc("xb16", [128, D], BF16)
            H2T = M.alloc("H2T", [128, 8, T], BF16)
            H2Ts = M.alloc("H2Ts", [128, 8, NS], BF16)
            rl = M.alloc("rl", [128, 512], BF16)
            actT = M.alloc("actT", [128, 8, 512], BF16)
            statB = M.alloc("statB", [128, 8], F32)
            P.op("pool", lambda e: e.memset(statB[:, 7:8], RMS_EPS), writes=[("statB", 7)])
            nfb = DFF // FB
            sn = [0]

            def load_block(fb):
                sl = fb % 2
                for c in range(8):
                    s = sn[0] % 2
                    sn[0] += 1
                    P.dma(stg[s][:, :], w_up_d[c * 128:(c + 1) * 128, fb * FB:(fb + 1) * FB], writes=[("stgm", s)], sem=f"wm{s}")
                    P.op("dve", lambda e, c=c, s=s, sl=sl: e.tensor_scalar(out=WUP[sl][:, c, :], in0=stg[s][:, :], scalar1=G2(c), scalar2=None, op0=ALU.mult),
                         reads=[("stgm", s), "pcol"], writes=[("WUP", sl)])
                for fc in range(8):
                    s = sn[0] % 2
                    sn[0] += 1
                    r0 = fb * FB + fc * 128
                    P.dma(stg[s][:, :], w_dn_d[r0:r0 + 128, :], writes=[("stgm", s)], sem=f"wm{s}")
                    P.op("act", lambda e, fc=fc, s=s, sl=sl: e.activation(out=WDN[sl][:, fc, :], in_=stg[s][:, :], func=AF.Copy),
                         reads=[("stgm", s)], writes=[("WDN", sl)])

            def block_chunks(fb):
                sl = fb % 2
                out = []
                for k in range(16):
                    s_ = k % 2
                    if k < 8:
                        c = k
                        d = lambda s_=s_, c=c: P.dma(stg[s_][:, :], w_up_d[c * 128:(c + 1) * 128, fb * FB:(fb + 1) * FB], writes=[("stgm", s_)], sem=f"wm{s_}")
                        f = lambda s_=s_, c=c: P.op("dve", lambda e: e.tensor_scalar(out=WUP[sl][:, c, :], in0=stg[s_][:, :], scalar1=G2(c), scalar2=None, op0=ALU.mult),
                                                   reads=[("stgm", s_), "pcol"], writes=[("WUP", sl)])
                    else:
                        fc = k - 8
                        r0 = fb * FB + fc * 128
                        d = lambda s_=s_, r0=r0: P.dma(stg[s_][:, :], w_dn_d[r0:r0 + 128, :], writes=[("stgm", s_)], sem=f"wm{s_}")
                        f = lambda s_=s_, fc=fc: P.op("act", lambda e: e.activation(out=WDN[sl][:, fc, :], in_=stg[s_][:, :], func=AF.Copy),
                                                     reads=[("stgm", s_)], writes=[("WDN", sl)])
                    out.append((d, f))
                return out

            groups = [(g * 4, 4, 128) for g in range(NTILE // 4)] + [(NTILE, 1, NS)]
            xap = lambda tcol: (X1[:, tcol, :], ("x1", tcol)) if tcol < NTILE else (X1s[0:NS, :], "x1s")
            blk0 = block_chunks(0)
            blk0[0][0]()
            blk0[1][0]()
            for tcol in range(NTILE + 1):
                npart = 128 if tcol < NTILE else NS
                xa, xk = xap(tcol)
                P.op("act", lambda e: e.activation(out=xb16[0:npart, :], in_=xa, func=AF.Copy), reads=[xk], writes=["xb16"])
                pb, pk = bank()
                pbb = pb[:].bitcast(BF16)
                for c in range(8):
                    P.op("pe", lambda e: e.transpose(out=pbb[:, c * 128:c * 128 + npart], in_=xb16[0:npart, c * 128:(c + 1) * 128], identity=identb[0:npart, 0:npart]),
                         reads=["xb16", "CB"], writes=[pk], inc=(c == 7))
                dst = H2T[:, :, tcol * 128:(tcol + 1) * 128] if tcol < NTILE else H2Ts[:, :, 0:NS]
                P.op("dve", lambda e: e.tensor_copy(out=dst, in_=pbb.rearrange("p (c t) -> p c t", c=8)[:, :, 0:npart]),
                     reads=[pk], writes=[("H2T", tcol)])
                if tcol < 16:
                    blk0[tcol][1]()
                    if tcol + 2 < 16:
                        blk0[tcol + 2][0]()
            for fb in range(nfb):
                sl = fb % 2
                nxt = block_chunks(fb + 1) if fb + 1 < nfb else None
                if nxt is not None:
                    nxt[0][0]()
                    nxt[1][0]()
                gi = 0
                for (t0, nt, npart) in groups:
                    ncols = nt * npart
                    hsrc = (lambda c: H2T[:, c, t0 * 128:t0 * 128 + ncols]) if npart == 128 else (lambda c: H2Ts[:, c, 0:NS])
                    hkeys = [("H2T", t0 + k) for k in range(nt)]
                    if nxt is not None and gi < 4:
                        for kk_ in range(4 * gi, 4 * gi + 4):
                            nxt[kk_][1]()
                            if kk_ + 2 < 16:
                                nxt[kk_ + 2][0]()
                    gi += 1
                    for fc in range(8):
                        pb, pk = bank()
                        for c in range(8):
                            P.op("pe", lambda e: e.matmul(pb[:, 0:ncols], lhsT=WUP[sl][:, c, fc * 128:(fc + 1) * 128], rhs=hsrc(c), start=(c == 0), stop=(c == 7)),
                                 reads=[("WUP", sl)] + hkeys, writes=[pk], inc=(c == 7))
                        P.op("act", lambda e: e.activation(out=rl[:, 0:ncols], in_=pb[:, 0:ncols], func=AF.Relu), reads=[pk], writes=["rl"])
                        P.op("dve", lambda e: e.tensor_tensor(out=actT[:, fc, 0:ncols], in0=rl[:, 0:ncols], in1=rl[:, 0:ncols], op=ALU.mult), reads=["rl"], writes=[("actT", fc)])
                    for k in range(nt):
                        tcol = t0 + k
                        xa, xk = xap(tcol)
                        for dh in range(2):
                            pb, pk = bank()
                            for fc in range(8):
                                P.op("pe", lambda e: e.matmul(pb[0:npart, :], lhsT=actT[:, fc, k * npart:(k + 1) * npart], rhs=WDN[sl][:, fc, dh * 512:(dh + 1) * 512], start=(fc == 0), stop=(fc == 7)),
                                     reads=[("actT", fc), ("WDN", sl)], writes=[pk], inc=(fc == 7))
                            P.op("dve", lambda e: e.scalar_tensor_tensor(out=xa[:, dh * 512:(dh + 1) * 512], in0=pb[0:npart, :], scalar=rs2[0:npart, tcol:tcol + 1],
                                                                         in1=xa[:, dh * 512:(dh + 1) * 512], op0=ALU.mult, op1=ALU.add),
                                 reads=[pk, xk, "rs2"], writes=[xk])
                        if fb == nfb - 1:
                            P.op("act", lambda e: e.activation(out=xb16[0:npart, :], in_=xa, func=AF.Square, accum_out=statB[0:npart, 0:1]), reads=[xk], writes=["xb16", ("statB", 0)])
                            P.op("act", lambda e: e.activation(out=statB[0:npart, 1:2], in_=statB[0:npart, 0:1], func=AF.Ln, scale=1.0 / D, bias=statB[0:npart, 7:8]),
                                 reads=[("statB", 0), ("statB", 7)], writes=[("statB", 1)])
                            P.op("act", lambda e: e.activation(out=statB[0:npart, 2:3], in_=statB[0:npart, 1:2], func=AF.Exp, scale=-0.5), reads=[("statB", 1)], writes=[("statB", 2)])
                            P.op("dve", lambda e: e.scalar_tensor_tensor(out=xa, in0=xa, scalar=statB[0:npart, 2:3], in1=GF[0:npart, :], op0=ALU.mult, op1=ALU.mult),
                                 reads=[xk, ("statB", 2), "GF"], writes=[xk])
                            if npart == 128:
                                P.dma(y_d[tcol * 128:(tcol + 1) * 128, :], xa, reads=[xk], sem=f"yo{tcol % 4}")
                            else:
                                P.dma(ys_d[:, :], xa, reads=[xk], sem=f"yo{tcol % 4}")
        except _Stop:
            pass
        P.finish()
        block = es.enter_context(nc.Block())
        P.emit(block)
    return nc, cst_np, taps


_CACHE = {}


def kernel(x_prompt, x_sample, state_wkv, state_shift, state_pool, norm1_g, w_in, shift_mu,
           pool_w, pool_scale, w0, w_lora_up, a0, a_lora_up, g_lora_up, k_k, k_a, r_k,
           ln_x_g, ln_x_b, w_out, norm2_g, w_up, w_down, norm_f_g):
    f = lambda a: np.ascontiguousarray(np.asarray(a, dtype=np.float32))
    if "nc" not in _CACHE:
        _CACHE["nc"] = build_program()
    nc, cst_np, taps = _CACHE["nc"]
    col = lambda v, n: f(v).reshape(n, 128).T
    pcol = np.zeros((128, 64), np.float32)
    pcol[:, 0:14] = col(shift_mu[0], 14)
    pcol[:, 14:18] = col(w0[0], 4)
    pcol[:, 18:22] = col(a0[0], 4)
    pcol[:, 22:26] = col(k_k[0], 4)
    pcol[:, 26:30] = col(k_a[0], 4)
    pcol[:, 30:34] = col(f(r_k[0]).reshape(512), 4)
    pcol[:, 34:38] = col(pool_scale[0], 4)
    pcol[:, 38:46] = col(norm1_g[0], 8)
    pcol[:, 46:54] = col(norm2_g[0], 8)
    rows = np.zeros((3, 1024), np.float32)
    rows[0, 0:512] = f(ln_x_g[0]); rows[1, 0:512] = f(ln_x_b[0]); rows[2, :] = f(norm_f_g)
    bh = np.zeros((128, 192), np.float32)
    bh[:, 0:64] = np.tile(f(ln_x_g[0]).reshape(8, 64), (NS, 1))
    bh[:, 64:128] = np.tile(f(ln_x_b[0]).reshape(8, 64), (NS, 1))
    bh[:, 128:192] = np.tile(f(r_k[0]).reshape(8, 64), (NS, 1))
    lora12 = np.concatenate([f(w_lora_up[0]), f(a_lora_up[0])], axis=0)
    shared = {"w_in": f(w_in[0]), "w_out": f(w_out[0]), "w_up": f(w_up[0]), "w_down": f(w_down[0]),
              "pool_w": f(pool_w[0]), "lora12": lora12, "g_lora_up": f(g_lora_up[0]),
              "pcol": pcol, "rows": rows, "bhrows": bh, "cst": cst_np}
    xp, xs = f(x_prompt), f(x_sample)
    swkv, ssh, spl = f(state_wkv[0]), f(state_shift[0]), f(state_pool[0])
    in_maps = []
    for i in range(NCORES):
        b = slice(i * NS, (i + 1) * NS)
        m = dict(shared)
        m["x"] = xp[i]
        m["xs"] = xs[b, 0, :]
        m["swkv"] = swkv[b].reshape(NS * 8, 4096)
        m["sshift"] = ssh[b, 0, :]
        m["spool"] = spl[b].reshape(NS * 15, 512)
        in_maps.append(m)
    res = run_bass_kernel_spmd(nc, in_maps, core_ids=list(range(NCORES)))
    R = res.results
    _CACHE["last"] = R
    y_prompt = np.stack([R[i]["y"] for i in range(NCORES)], axis=0)
    y_sample = np.concatenate([R[i]["ys"] for i in range(NCORES)], axis=0)[:, None, :]
    wkv_p = np.stack([R[i]["wkv_p"] for i in range(NCORES)], axis=0)[None]
    sh_p = np.stack([R[i]["shift_p"].reshape(1, SHIFT_W) for i in range(NCORES)], axis=0)[None]
    pl_p = np.stack([R[i]["pool_p"][1:16] for i in range(NCORES)], axis=0)[None]
    wkv_s = np.concatenate([R[i]["wkv_s"].reshape(NS, 8, 64, 64) for i in range(NCORES)], axis=0)[None]
    sh_s = np.concatenate([R[i]["shift_s"] for i in range(NCORES)], axis=0)[:, None, :][None]
    pl_s = np.concatenate([R[i]["pool_s"] for i in range(NCORES)], axis=0)[None]
    out = (y_prompt, y_sample, wkv_p, sh_p, pl_p, wkv_s, sh_s, pl_s)
    return tuple(np.ascontiguousarray(o.astype(np.float32)) for o in out)
```

```python
import numpy as np
from contextlib import ExitStack
import concourse.bass as bass
import concourse.mybir as mybir
from concourse.bass_utils import run_bass_kernel_spmd

F32, BF16 = mybir.dt.float32, mybir.dt.bfloat16
AF = mybir.ActivationFunctionType
ALU = mybir.AluOpType
AX = mybir.AxisListType

NCORES = 8
D = 1024
T = 2048
NTILE = T // 128
NS = 16
IN_W = 2304
SHIFT_W = 1792
DFF = 4096
CW = -float(np.exp(-0.5))
RMS_EPS = 1e-6
GN_EPS = 64e-5
ST = 256
DEBUG_TAPS = False
KSTOP = 99.0
KVAR = 0


class _Stop(Exception):
    pass


class _Rec:
    def __getattr__(self, name):
        return lambda *a, **k: (name, a, k)


_REC = _Rec()


class Prog:
    def __init__(self, nc, es):
        self.nc, self.es = nc, es
        self.streams = {k: [] for k in ("pe", "dve", "act", "pool", "sp")}
        self.csem = {k: es.enter_context(nc.semaphore("c_" + k)) for k in ("pe", "dve", "act", "pool")}
        self.cnt = {k: 0 for k in self.csem}
        self.dsem, self.dcnt = {}, {}
        self.seen = {k: {} for k in self.streams}
        self.reg = {}
        self.pend = {k: [] for k in self.streams}
        self.alias = {}
        self.vc = {k: {} for k in self.streams}
        self.hist = {}

    def _exp(self, keys):
        out = []
        for k in keys:
            out.extend(self.alias.get(k, [k]))
        return out

    def _need(self, eng, key, val):
        if key in self.dcnt:
            val = self.dcnt[key]
        else:
            if eng == "pe" and key == "pe":
                return
            assert val <= self.cnt[key], (eng, key, val, self.cnt[key])
        vc = self.vc[eng]
        if vc.get(key, 0) >= val:
            return
        for k2, v2 in self.hist.get((key, val), {key: val}).items():
            if vc.get(k2, 0) < v2:
                vc[k2] = v2
        sem = self.csem[key] if key in self.csem else self.dsem[key]
        self.pend[eng].append((sem, val))

    def _flush(self, eng, keep_last):
        p = self.pend[eng]
        last = p.pop() if (keep_last and p) else None
        for sem, val in p:
            self.streams[eng].append(lambda e, sem=sem, val=val: e.wait_ge(sem, val))
        self.pend[eng] = []
        return last

    def _deps(self, eng, reads, writes):
        for r in reads:
            st = self.reg.get(r)
            if st and st["w"]:
                self._need(eng, *st["w"])
        for w in writes:
            st = self.reg.get(w)
            if st:
                if st["w"]:
                    self._need(eng, *st["w"])
                for k, v in st["r"].items():
                    self._need(eng, k, v)

    def _mark(self, ev, reads, writes):
        for r in reads:
            st = self.reg.setdefault(r, {"w": None, "r": {}})
            st["r"][ev[0]] = max(st["r"].get(ev[0], 0), ev[1])
        for w in writes:
            self.reg[w] = {"w": ev, "r": {}}

    def op(self, eng, fn, reads=(), writes=(), inc=True):
        name, a, k = fn(_REC)
        reads, writes = self._exp(reads), self._exp(writes)
        self._deps(eng, reads, writes)
        w = self._flush(eng, True)

        def emit(e, name=name, a=a, k=k, w=w, sem=(self.csem[eng] if inc else None)):
            ins = getattr(e, name)(*a, **k)
            if w is not None:
                ins = ins._wait_ge(w[0], w[1])
            if sem is not None:
                ins.then_inc(sem, 1)
        if inc:
            self.cnt[eng] += 1
            ev = (eng, self.cnt[eng])
            snap = dict(self.vc[eng])
            snap[eng] = self.cnt[eng]
            self.hist[ev] = snap
            self.vc[eng][eng] = self.cnt[eng] if eng == "pe" else self.vc[eng].get(eng, 0)
        else:
            ev = (eng, self.cnt[eng] + 1)
        self.streams[eng].append(emit)
        self._mark(ev, reads, writes)

    def dma(self, out, in_, reads=(), writes=(), sem="d0", q="sp", chain=False, **kw):
        if sem not in self.dsem:
            self.dsem[sem] = self.es.enter_context(self.nc.semaphore("d_" + sem))
            self.dcnt[sem] = 0
        if not chain and self.dcnt[sem] > 0:
            self._need(q, sem, self.dcnt[sem])
        reads, writes = self._exp(reads), self._exp(writes)
        self._deps(q, reads, writes)
        w = self._flush(q, True)
        self.dcnt[sem] += 16
        ev = (sem, self.dcnt[sem])
        snap = dict(self.vc[q])
        snap[sem] = self.dcnt[sem]
        self.hist[ev] = snap
        s = self.dsem[sem]

        def emit(e, out=out, in_=in_, s=s, kw=kw, w=w):
            ins = e.dma_start(out=out, in_=in_, **kw)
            if w is not None:
                ins = ins._wait_ge(w[0], w[1])
            ins.then_inc(s, 16)
        self.streams[q].append(emit)
        self._mark(ev, reads, writes)

    def barrier(self):
        for e in self.streams:
            for k in self.csem:
                if self.cnt[k] > 0:
                    self._need(e, k, self.cnt[k])
            for k in self.dcnt:
                if self.dcnt[k] > 0:
                    self._need(e, k, self.dcnt[k])
            self._flush(e, False)

    def finish(self):
        for k in self.csem:
            if self.cnt[k] > 0:
                self._need("sp", k, self.cnt[k])
        for k in self.dcnt:
            self._need("sp", k, self.dcnt[k])
        for e in self.streams:
            self._flush(e, False)

    def emit(self, block):
        S = self.streams

        @block.sync
        def _(e):
            for f in S["sp"]:
                f(e)

        @block.tensor
        def _(e):
            for f in S["pe"]:
                f(e)

        @block.vector
        def _(e):
            for f in S["dve"]:
                f(e)

        @block.scalar
        def _(e):
            for f in S["act"]:
                f(e)

        @block.gpsimd
        def _(e):
            for f in S["pool"]:
                f(e)


class Mem:
    BASE, LIMIT = 16512, 229344

    def __init__(self, nc):
        self.nc, self.off, self.n = nc, self.BASE, 0

    def alloc(self, name, shape, dtype):
        nb = 2 if dtype == BF16 else 4
        size = int(np.prod(shape[1:])) * nb
        size = (size + 63) // 64 * 64
        assert self.off + size <= self.LIMIT, ("SBUF overflow", name, self.off, size)
        self.n += 1
        t = self.nc.alloc_sbuf_tensor_at(f"{name}_{self.n}", list(shape), dtype, offset=self.off)
        self.off += size
        return t

    def mark(self):
        return self.off

    def release(self, m):
        self.off = m


def _make_consts():
    c = {}
    i = np.arange(128)
    c["ident"] = np.eye(128, dtype=np.float32)
    c["m_su"] = (i[:, None] < i[None, :]).astype(np.float32)
    c["m_iu"] = (i[:, None] <= i[None, :]).astype(np.float32)
    c["m_sl"] = (i[:, None] > i[None, :]).astype(np.float32)
    c["hones"] = ((i[:, None] // 64) == (i[None, :] // 64)).astype(np.float32)
    rst = np.ones((128, ST), np.float32)
    rst[:, ::128] = 0.0
    c["restart"] = rst
    wins = (2, 4, 8, 16)
    band = np.zeros((4, 128, 128), np.float32)
    bprev = np.zeros((4, 128, 128), np.float32)
    bfirst = np.zeros((4, 128, 128), np.float64)
    for g, w in enumerate(wins):
        for t in range(128):
            for s in range(t - w + 1, t + 1):
                if s >= 0:
                    band[g, s, t] += 1.0 / w
                    bfirst[g, s, t] += 1.0 / min(w, t + 1)
                else:
                    bprev[g, 128 + s, t] += 1.0 / w
            band[g, t, t] -= 1.0
            bfirst[g, t, t] -= 1.0
    c["band"] = np.concatenate(list(band), axis=1)
    c["bprev"] = np.concatenate(list(bprev), axis=1)
    import ml_dtypes
    hi = bfirst.astype(np.float32).astype(ml_dtypes.bfloat16).astype(np.float32)
    lo = (bfirst - hi).astype(np.float32)
    c["bfhi"] = np.concatenate(list(hi), axis=1)
    c["bflo"] = np.concatenate(list(lo), axis=1)
    sel = np.zeros((128, 2, 4, 16), np.float32)
    for tl in range(2):
        for bl in range(8):
            for j in range(15):
                for g, w in enumerate(wins):
                    if j >= 16 - w:
                        sel[bl * 15 + j, tl, g, tl * 8 + bl] = 1.0 / w
    c["sel"] = sel.reshape(128, 128)
    ind = np.zeros((128, 4, 8), np.float32)
    for p in range(128):
        for hp in range(4):
            ind[p, hp, 2 * hp + p // 64] = 1.0
    c["ind"] = ind.reshape(128, 32)
    return c


_CONST_ORDER = ["ident", "m_su", "m_iu", "m_sl", "hones", "restart", "band", "bprev", "bfhi", "bflo", "sel", "ind"]


def _pack_consts():
    c = _make_consts()
    offs, cols, o = {}, [], 0
    for k in _CONST_ORDER:
        offs[k] = (o, c[k].shape[1])
        o += c[k].shape[1]
        cols.append(c[k])
    return np.concatenate(cols, axis=1).astype(np.float32), offs


def build_program():
    cst_np, coff = _pack_consts()
    NCST = cst_np.shape[1]
    nc = bass.Bass("TRN2", target_bir_lowering=False)
    dram = lambda n, s, k="ExternalInput": nc.dram_tensor(n, list(s), F32, kind=k).ap()
    x_d = dram("x", [T, D]); xs_d = dram("xs", [NS, D])
    swkv_d = dram("swkv", [128, 4096]); sshift_d = dram("sshift", [NS, SHIFT_W]); spool_d = dram("spool", [NS * 15, 512])
    w_in_d = dram("w_in", [D, IN_W]); w_out_d = dram("w_out", [D, D]); w_up_d = dram("w_up", [D, DFF]); w_dn_d = dram("w_down", [DFF, D])
    poolw_d = dram("pool_w", [4, 128, 128]); lora12_d = dram("lora12", [128, 512]); glora_d = dram("g_lora_up", [128, 512])
    pcol_d = dram("pcol", [128, 64]); rows_d = dram("rows", [3, 1024]); bh_d = dram("bhrows", [128, 192])
    cst_d = dram("cst", [128, NCST])
    y_d = dram("y", [T, D], "ExternalOutput"); ys_d = dram("ys", [NS, D], "ExternalOutput")
    wkvp_d = dram("wkv_p", [8, 64, 64], "ExternalOutput"); shp_d = dram("shift_p", [14, 128], "ExternalOutput")
    plp_d = dram("pool_p", [16, 512], "ExternalOutput")
    wkvs_d = dram("wkv_s", [128, 4096], "ExternalOutput"); shs_d = dram("shift_s", [NS, SHIFT_W], "ExternalOutput")
    pls_d = dram("pool_s", [NS, 15, 512], "ExternalOutput")
    scr_d = dram("scr", [8, NS, 512], "Internal")
    taps = {}

    es = ExitStack()
    with es:
        P = Prog(nc, es)
        M = Mem(nc)
        PS = [es.enter_context(nc.psum_tensor(f"ps{i}", [128, 512], F32)) for i in range(8)]
        psn = [0]

        CX = {"banks": list(range(8)), "pn": "all"}
        pcount = {}

        def bank():
            lst = CX["banks"]
            n = pcount.get(CX["pn"], 0)
            pcount[CX["pn"]] = n + 1
            i = lst[n % len(lst)]
            return PS[i], ("ps", i)

        def pe_warm(nmm):
            pw, pwk = bank()
            for _ in range(nmm):
                P.op("pe", lambda e: e.matmul(pw[:, :], lhsT=WARM[0], rhs=WARM[1], start=True, stop=True), reads=["CB"], writes=[pwk], inc=False)

        WARM = [None, None]

        def pe_warm_small(nmm, pw, pwk):
            for _ in range(nmm):
                P.op("pe", lambda e: e.matmul(pw[:, 0:128], lhsT=WARM[0], rhs=WARM[1][:, 0:128], start=True, stop=True), reads=["CB"], writes=[pwk], inc=False)

        def tap(name, ap, shape, reads):
            if not DEBUG_TAPS:
                return
            t = nc.dram_tensor("tap_" + name, list(shape), ap.dtype, kind="ExternalOutput").ap()
            taps[name] = t
            P.dma(t, ap, reads=reads, sem="tap")

        X1 = M.alloc("X1", [128, NTILE, D], F32)
        X1s = M.alloc("X1s", [128, D], F32)
        CB = M.alloc("CB", [128, NCST], BF16)
        identf = M.alloc("identf", [128, 128], F32)
        honesf = M.alloc("honesf", [128, 128], F32)
        restart = M.alloc("restart", [128, ST], F32)
        pcol = M.alloc("pcol", [128, 64], F32)
        pder = M.alloc("pder", [128, 32], F32)
        rs2 = M.alloc("rs2", [128, NTILE + 1], F32)
        cb = lambda k: CB[:, coff[k][0]:coff[k][0] + coff[k][1]]
        identb = cb("ident")
        WARM[0], WARM[1] = identb, CB[:, 0:512]
        MU = lambda j: pcol[:, j:j + 1]
        OMMU = lambda j: pder[:, j:j + 1]
        W0 = lambda hp: pcol[:, 14 + hp:15 + hp]
        A0 = lambda hp: pcol[:, 18 + hp:19 + hp]
        KKc = lambda hp: pcol[:, 22 + hp:23 + hp]
        KAc = lambda hp: pcol[:, 26 + hp:27 + hp]
        OMKA = lambda hp: pder[:, 14 + hp:15 + hp]
        RKc = lambda hp: pcol[:, 30 + hp:31 + hp]
        PSC = lambda g: pcol[:, 34 + g:35 + g]
        G1 = lambda c: pcol[:, 38 + c:39 + c]
        G2 = lambda c: pcol[:, 46 + c:47 + c]

        try:
            P.dma(pcol[:], pcol_d[:, :], writes=["pcol"], sem="par")
            P.dma(X1s[0:NS, :], xs_d[:, :], writes=["x1s"], sem="xs")

            m0 = M.mark()
            stg = [M.alloc(f"stg{i}", [128, IN_W], F32) for i in range(2)]
            half = NCST // 2 + 1
            for i, (a, b) in enumerate([(0, min(IN_W, NCST)), (min(IN_W, NCST), NCST)]):
                if b <= a:
                    continue
                P.dma(stg[i][:, 0:b - a], cst_d[:, a:b], writes=[("stg", i)], sem="cst")
                P.op("dve", lambda e, i=i, a=a, b=b: e.tensor_copy(out=CB[:, a:b], in_=stg[i][:, 0:b - a]),
                     reads=[("stg", i)], writes=["CB"])
            assert NCST <= 2 * IN_W
            o, n = coff["ident"]
            P.dma(identf[:], cst_d[:, o:o + n], writes=["identf"], sem="cst")
            o, n = coff["hones"]
            P.dma(honesf[:], cst_d[:, o:o + n], writes=["honesf"], sem="cst")
            o, n = coff["restart"]
            P.dma(restart[:], cst_d[:, o:o + n], writes=["restart"], sem="cst")
            P.op("dve", lambda e: e.tensor_scalar(out=pder[:, 0:14], in0=pcol[:, 0:14], scalar1=-1.0, scalar2=1.0, op0=ALU.mult, op1=ALU.add),
                 reads=["pcol"], writes=["pder"])
            P.op("dve", lambda e: e.tensor_scalar(out=pder[:, 14:18], in0=pcol[:, 26:30], scalar1=-1.0, scalar2=1.0, op0=ALU.mult, op1=ALU.add),
                 reads=["pcol"], writes=["pder"])
            M.release(m0)

            mA = M.mark()
            WIN = M.alloc("WIN", [128, 8, IN_W], BF16)
            WOUT = M.alloc("WOUT", [128, 8, D], BF16)
            POOLW = M.alloc("POOLW", [128, 4, 128], BF16)
            L12 = M.alloc("L12", [128, 512], BF16)
            GLO = M.alloc("GLO", [128, 512], BF16)
            RKI = M.alloc("RKI", [128, 4, 8], BF16)
            LNG = M.alloc("LNG", [128, 512], BF16); LNB = M.alloc("LNB", [128, 512], BF16)
            m1 = M.mark()
            stg = [M.alloc(f"stgw{i}", [128, IN_W], F32) for i in range(2)]
            for c in range(8):
                s = c % 2
                P.dma(stg[s][:, :], w_in_d[c * 128:(c + 1) * 128, :], writes=[("stg", s)], sem=f"wst{s}")
                P.op("dve",
                     lambda e, c=c, s=s: e.tensor_scalar(out=WIN[:, c, :], in0=stg[s][:, :], scalar1=G1(c), scalar2=None, op0=ALU.mult),
                     reads=[("stg", s), "pcol"], writes=["WIN"])
            for c in range(8):
                s = c % 2
                P.dma(stg[s][:, 0:D], w_out_d[c * 128:(c + 1) * 128, :], writes=[("stg", s)], sem=f"wst{s}")
                P.op("dve", lambda e, c=c, s=s: e.tensor_copy(out=WOUT[:, c, :], in_=stg[s][:, 0:D]),
                     reads=[("stg", s)], writes=["WOUT"])
            small = [(POOLW[:].rearrange("p g d -> p (g d)"), None, "POOLW"), (L12[:], lora12_d[:, :], "L12"), (GLO[:], glora_d[:, :], "GLO")]
            for i, (dst, src, key) in enumerate(small):
                s = i % 2
                if key == "POOLW":
                    P.dma(stg[s][:, 0:512].rearrange("p (g d) -> p g d", g=4), poolw_d.rearrange("g c d -> c g d"), writes=[("stg", s)], sem=f"wst{s}")
                else:
                    P.dma(stg[s][:, 0:512], src, writes=[("stg", s)], sem=f"wst{s}")
                P.op("dve", lambda e, dst=dst, s=s: e.tensor_copy(out=dst, in_=stg[s][:, 0:512]), reads=[("stg", s)], writes=[key])
            for i, (dst, key) in enumerate([(LNG, "LNG"), (LNB, "LNB")]):
                P.dma(stg[i][:, 0:512], rows_d[i, 0:512].partition_broadcast(128), writes=[("stg", i)], sem=f"wst{i}")
                P.op("dve", lambda e: e.tensor_copy(out=dst[:], in_=stg[i][:, 0:512]), reads=[("stg", i)], writes=[key])
            for hp in range(4):
                o, n = coff["ind"]
                P.op("dve", lambda e, hp=hp, o=o: e.tensor_scalar(out=RKI[:, hp, :], in0=CB[:, o + hp * 8:o + hp * 8 + 8], scalar1=RKc(hp), scalar2=None, op0=ALU.mult),
                     reads=["CB", "pcol"], writes=["RKI"])
            P.barrier()
            M.release(m1)
            for ti in range(NTILE):
                P.dma(X1[:, ti, :], x_d[ti * 128:(ti + 1) * 128, :], writes=[("x1", ti)], sem=f"x{ti // 4}", chain=True)
            if KSTOP <= 1:
                raise _Stop()

            hTs = [M.alloc("hT0", [128, 8, ST + 1], BF16)] * 2
            sgzbs = [M.alloc("sgzb0", [128, ST], BF16)] * 2
            mixPs = [M.alloc("mixP0", [128, 4, ST], BF16)] * 2
            CX.update(hT=hTs[0], khT=("hT", 0), sgzb=sgzbs[0], ksg=("sgzb", 0), mixdst=None, mixsrc=None)
            hnb = M.alloc("hnb", [128, D], BF16)
            junk = hnb
            stat = M.alloc("stat", [128, 8], F32)
            dTb4 = M.alloc("dTb", [128, 4, ST], BF16)
            mixT = M.alloc("mixT", [128, 8, ST], BF16)
            plast = M.alloc("plast", [128, 16], F32)
            E1 = M.alloc("E1", [128, ST], F32); E3 = M.alloc("E3", [128, ST], F32); Z12 = E1; Z13 = E3
            z12b = M.alloc("z12b", [128, ST], BF16)
            RKV = [[M.alloc(f"{nm}{i}", [128, ST], F32) for nm in ("Rb", "Kb", "Vb")] for i in range(1)] * 2
            SG4 = M.alloc("SG4", [128, 4, ST], F32); AS4 = M.alloc("AS4", [128, 4, ST], F32)
            KK2 = M.alloc("KK2", [128, ST], F32); RN = M.alloc("RN", [128, ST], F32); tmpS = RN
            KKN = M.alloc("KKN", [128, ST], F32); Bb = M.alloc("Bb", [128, ST], F32)
            KF = M.alloc("KF", [128, ST], F32)
            Mf = M.alloc("Mf", [128, 4, 64], F32); Mb = M.alloc("Mb", [128, 4, 2, 64], BF16); Mt = M.alloc("Mt", [128, 4, 64], F32)
            Yw = M.alloc("Yw", [128, 512], F32); Yq = M.alloc("Yq", [128, 512], F32); Yo = M.alloc("Yo", [128, 512], BF16)
            gst = M.alloc("gst", [128, 40], F32)
            P.alias["mixT"] = [("mixT", c) for c in range(8)]
            mS = M.mark()
            SMP = M.alloc("SMP", [128, 6, 4, NS], F32)
            SQ = 8
            S0qs = [M.alloc(f"S0q{i}", [128, SQ, 64], F32) for i in range(2)]; S1qs = [M.alloc("S1q0", [128, SQ, 64], F32)] * 2
            Stq = M.alloc("Stq", [128, SQ, 64], F32)
            BH = M.alloc("BH", [128, 6, 64], F32)
            BHR = M.alloc("BHR", [128, 192], F32)
            sa_s = M.alloc("sa_s", [128, 64], F32); y_s = M.alloc("y_s", [128, 64], F32); y_s2 = M.alloc("y_s2", [128, 64], F32)
            spl = M.alloc("spl", [128, 2, 512], BF16)
            shT = M.alloc("shT", [128, 14, NS], F32); shtok = M.alloc("shtok", [128, SHIFT_W], F32)
            ptok = M.alloc("ptok", [128, SHIFT_W], F32)
            ytok = M.alloc("ytok", [128, 512], F32)
            Stq2 = ytok[:, :].rearrange("p (v k) -> p v k", k=64)
            splf = ytok
            stgp = shtok[:, 0:1024].rearrange("p (a c) -> p a c", a=2)

            o_su, _ = coff["m_su"]; o_iu, _ = coff["m_iu"]; o_sl, _ = coff["m_sl"]
            mask_ai = CB[:, o_su:o_su + 256].rearrange("p (a t) -> p a t", a=2)
            mask_sl = CB[:, o_sl:o_sl + 128]

            def rms_stats(eng_in, key, npart, col):
                P.op("act", lambda e: e.activation(out=junk[0:npart, :], in_=eng_in, func=AF.Square, accum_out=stat[0:npart, col:col + 1]),
                     reads=[key], writes=["hnb", ("stat", col)])

            def norm_to_hT(xin, key, npart, c0, ncols_total):
                if npart == 128:
                    pe_warm(6)
                rms_stats(xin, key, npart, 0)
                P.op("act", lambda e: e.activation(out=stat[0:npart, 1:2], in_=stat[0:npart, 0:1], func=AF.Ln, scale=1.0 / D, bias=stat[0:npart, 7:8]),
                     reads=[("stat", 0), ("stat", 7)], writes=[("stat", 1)])
                P.op("act", lambda e: e.activation(out=stat[0:npart, 2:3], in_=stat[0:npart, 1:2], func=AF.Exp, scale=-0.5), reads=[("stat", 1)], writes=[("stat", 2)])
                P.op("act", lambda e: e.activation(out=hnb[0:npart, :], in_=xin, func=AF.Copy, scale=stat[0:npart, 2:3]),
                     reads=[key, ("stat", 2)], writes=["hnb"])
                pb, pk = bank()
                pbb = pb[:].bitcast(BF16)
                for c in range(8):
                    P.op("pe", lambda e, c=c: e.transpose(out=pbb[:, c * 128:c * 128 + npart], in_=hnb[0:npart, c * 128:(c + 1) * 128], identity=identb[0:npart, 0:npart]),
                         reads=["hnb", "CB"], writes=[pk], inc=(c == 7))
                P.op("dve", lambda e: e.tensor_copy(out=CX["hT"][:, :, c0:c0 + npart], in_=pbb.rearrange("p (c t) -> p c t", c=8)[:, :, 0:npart]),
                     reads=[pk], writes=[CX["khT"]])

            def proj_fm(j, ncols):
                pb, pk = bank()
                for c in range(8):
                    P.op("pe", lambda e, c=c: e.matmul(pb[:, 0:ncols], lhsT=WIN[:, c, 512 + j * 128:512 + (j + 1) * 128], rhs=CX["hT"][:, c, 0:ncols], start=(c == 0), stop=(c == 7)),
                         reads=["WIN", CX["khT"]], writes=[pk], inc=(c == 7))
                return pb, pk

            def shift_evac(pb, pk, j, out, okey, ncols, prevT=None):
                P.op("act", lambda e: e.activation(out=tmpS[:, 0:ncols], in_=pb[:, 0:ncols], func=AF.Copy, scale=OMMU(j)),
                     reads=[pk, "pder"], writes=["RN"])
                if prevT is None:
                    raise AssertionError("prompt path uses shift_evac_p")
                else:
                    P.op("dve", lambda e: e.scalar_tensor_tensor(out=out[:, 0:ncols], in0=prevT, scalar=MU(j), in1=tmpS[:, 0:ncols], op0=ALU.mult, op1=ALU.add),
                         reads=["shT", "RN", "pcol"], writes=[okey])

            sh_ctr = [0]

            def shift_evac_p(pb, pk, j, out, okey, last):
                tb, tkey = ((RN, "RN"), (KKN, "KKN"))[sh_ctr[0] % 2]
                sh_ctr[0] += 1
                P.op("act", lambda e: e.activation(out=tb[:, 0:ST], in_=pb[:, 1:ST + 1], func=AF.Copy, scale=OMMU(j)),
                     reads=[pk, "pder"], writes=[tkey])
                P.op("dve", lambda e: e.scalar_tensor_tensor(out=out[:, 0:ST], in0=pb[:, 0:ST], scalar=MU(j), in1=tb[:, 0:ST], op0=ALU.mult, op1=ALU.add),
                     reads=[pk, tkey, "pcol"], writes=[okey])
                if last:
                    P.op("dve", lambda e: e.tensor_copy(out=plast[:, j:j + 1], in_=pb[:, ST:ST + 1]), reads=[pk], writes=[("plast", j)])

            def lora_stage(ncols):
                P.op("act", lambda e: e.activation(out=z12b[0:64, 0:ncols], in_=Z12[0:64, 0:ncols], func=AF.Tanh), reads=["E1"], writes=["z12b"])
                P.op("dve", lambda e: e.tensor_copy(out=z12b[64:128, 0:ncols], in_=Z12[64:128, 0:ncols]), reads=["E1"], writes=["z12b"])
                P.op("act", lambda e: e.activation(out=CX["sgzb"][:, 0:ncols], in_=Z13[:, 0:ncols], func=AF.Sigmoid), reads=["E3"], writes=[CX["ksg"]])

            def sigm_stage(ncols):
                n = ncols
                for hp in range(4):
                    pb, pk = bank()
                    P.op("pe", lambda e: e.matmul(pb[:, 0:n], lhsT=L12[0:64, hp * 128:(hp + 1) * 128], rhs=z12b[0:64, 0:n], start=True, stop=True),
                         reads=["L12", "z12b"], writes=[pk])
                    pb2, pk2 = bank()
                    P.op("pe", lambda e: e.matmul(pb2[:, 0:n], lhsT=L12[64:128, hp * 128:(hp + 1) * 128], rhs=z12b[64:128, 0:n], start=True, stop=True),
                         reads=["L12", "z12b"], writes=[pk2])
                    P.op("act", lambda e: e.activation(out=SG4[:, hp, 0:n], in_=pb[:, 0:n], func=AF.Sigmoid, bias=W0(hp)), reads=[pk, "pcol"], writes=[("SG", hp)])
                    P.op("act", lambda e: e.activation(out=AS4[:, hp, 0:n], in_=pb2[:, 0:n], func=AF.Sigmoid, bias=A0(hp)), reads=[pk2, "pcol"], writes=[("AS", hp)])

            def preprocess(hp, ncols, sample):
                n = ncols
                Rb, Kb, Vb = RKV[hp % 2]
                kR, kK, kV = ("Rb", 0), ("Kb", 0), ("Vb", 0)
                SG, AS = SG4[:, hp, :], AS4[:, hp, :]
                P.op("act", lambda e: e.activation(out=KK2[:, 0:n], in_=Kb[:, 0:n], func=AF.Square, scale=KKc(hp)), reads=[kK, "pcol"], writes=["KK2"])
                pb3, pk3 = bank()
                P.op("pe", lambda e: e.matmul(pb3[:, 0:n], lhsT=honesf[:], rhs=KK2[:, 0:n], start=True, stop=True), reads=["honesf", "KK2"], writes=[pk3])
                if not sample:
                    pw, pwk = bank()
                    for _ in range(10):
                        P.op("pe", lambda e: e.matmul(pw[:, :], lhsT=identb, rhs=CB[:, 0:512], start=True, stop=True), reads=["CB"], writes=[pwk], inc=False)
                P.op("act", lambda e: e.activation(out=RN[:, 0:n], in_=pb3[:, 0:n], func=AF.Ln, bias=stat[:, 6:7]), reads=[pk3, ("stat", 6)], writes=["RN"])
                P.op("act", lambda e: e.activation(out=RN[:, 0:n], in_=RN[:, 0:n], func=AF.Exp, scale=-0.5), reads=["RN"], writes=["RN"])
                P.op("dve", lambda e: e.scalar_tensor_tensor(out=KKN[:, 0:n], in0=Kb[:, 0:n], scalar=KKc(hp), in1=RN[:, 0:n], op0=ALU.mult, op1=ALU.mult), reads=[kK, "RN", "pcol"], writes=["KKN"])
                P.op("dve", lambda e: e.tensor_tensor(out=Bb[:, 0:n], in0=KKN[:, 0:n], in1=AS[:, 0:n], op=ALU.mult), reads=["KKN", ("AS", hp)], writes=["Bb"])
                P.op("dve", lambda e: e.tensor_scalar(out=KK2[:, 0:n], in0=AS[:, 0:n], scalar1=KAc(hp), scalar2=OMKA(hp), op0=ALU.mult, op1=ALU.add),
                     reads=[("AS", hp), "pcol", "pder"], writes=["KK2"])
                P.op("dve", lambda e: e.tensor_tensor(out=KF[:, 0:n], in0=Kb[:, 0:n], in1=KK2[:, 0:n], op=ALU.mult), reads=[kK, "KK2"], writes=["KF"])
                if sample:
                    P.op("act", lambda e: e.activation(out=SMP[:, 0, hp, :], in_=Rb[:, 0:n], func=AF.Copy), reads=[kR], writes=["SMP"])
                    P.op("act", lambda e: e.activation(out=SMP[:, 1, hp, :], in_=SG[:, 0:n], func=AF.Exp, scale=CW), reads=[("SG", hp)], writes=["SMP"])
                    P.op("dve", lambda e: e.tensor_copy(out=SMP[:, 2, hp, :], in_=KF[:, 0:n]), reads=["KF"], writes=["SMP"])
                    P.op("act", lambda e: e.activation(out=SMP[:, 3, hp, :], in_=Vb[:, 0:n], func=AF.Copy), reads=[kV], writes=["SMP"])
                    P.op("dve", lambda e: e.tensor_scalar(out=SMP[:, 4, hp, :], in0=KKN[:, 0:n], scalar1=-1.0, scalar2=None, op0=ALU.mult), reads=["KKN"], writes=["SMP"])
                    P.op("dve", lambda e: e.tensor_copy(out=SMP[:, 5, hp, :], in_=Bb[:, 0:n]), reads=["Bb"], writes=["SMP"])
                    return
                P.op("pool", lambda e: e.tensor_tensor(out=RKB[:, hp, 0:n], in0=Rb[:, 0:n], in1=KF[:, 0:n], op=ALU.mult), reads=[kR, "KF"], writes=["RKB"])
                P.op("pool", lambda e: e.tensor_copy(out=VBF[:, hp, 0:n], in_=Vb[:, 0:n]), reads=[kV], writes=["VBF"])
                P.op("dve", lambda e: e.tensor_tensor_scan(out=CUM[:, 0:n], data0=restart[:, 0:n], data1=SG[:, 0:n], initial=0.0, op0=ALU.mult, op1=ALU.add),
                     reads=["restart", ("SG", hp)], writes=["CUM"])
                P.op("act", lambda e: e.activation(out=E1[:, 0:n], in_=CUM[:, 0:n], func=AF.Exp, scale=CW), reads=["CUM"], writes=["E1"])
                P.op("dve", lambda e: e.tensor_tensor(out=ART[:, hp, 1, 0:n], in0=Rb[:, 0:n], in1=E1[:, 0:n], op=ALU.mult), reads=[kR, "E1"], writes=["ART"])
                P.op("dve", lambda e: e.tensor_tensor(out=CM[:, 0:n], in0=CUM[:, 0:n], in1=SG[:, 0:n], op=ALU.subtract), reads=["CUM", ("SG", hp)], writes=["E2"])
                P.op("act", lambda e: e.activation(out=E2[:, 0:n], in_=CM[:, 0:n], func=AF.Exp, scale=CW), reads=["E2"], writes=["E2"])
                P.op("dve", lambda e: e.scalar_tensor_tensor(out=ART[:, hp, 0, 0:n], in0=KKN[:, 0:n], scalar=-1.0, in1=E2[:, 0:n], op0=ALU.mult, op1=ALU.mult),
                     reads=["KKN", "E2"], writes=["ART"])
                P.op("act", lambda e: e.activation(out=E3[:, 0:n], in_=CUM[:, 0:n], func=AF.Exp, scale=-CW), reads=["CUM"], writes=["E3"])
                P.op("dve", lambda e: e.tensor_tensor(out=BT[:, hp, 0:n], in0=Bb[:, 0:n], in1=E3[:, 0:n], op=ALU.mult), reads=["Bb", "E3"], writes=["BT"])
                P.op("pool", lambda e: e.tensor_tensor(out=KT[:, hp, 0:n], in0=KF[:, 0:n], in1=E3[:, 0:n], op=ALU.mult), reads=["KF", "E3"], writes=["KT"])
                P.op("act", lambda e: e.activation(out=WC[:, hp, :], in_=E1[:, 0:n].rearrange("p (c t) -> p c t", t=128)[:, :, 127], func=AF.Copy),
                     reads=["E1"], writes=["WC"])

            def out_proj(npart, c0, x1ap, x1key, ti_stat):
                for dh in range(2):
                    pb, pk = bank()
                    for c in range(8):
                        msrc, mkey = (mixT[:, c, c0:c0 + npart], ("mixT", c)) if (c >= 4 or CX["mixsrc"] is None) else CX["mixsrc"](c, c0, npart)
                        P.op("pe", lambda e, c=c, dh=dh, pb=pb: e.matmul(pb[0:npart, :], lhsT=msrc, rhs=WOUT[:, c, dh * 512:(dh + 1) * 512], start=(c == 0), stop=(c == 7)),
                             reads=[mkey, "WOUT"], writes=[pk], inc=(c == 7))
                    P.op("dve", lambda e, dh=dh, pb=pb: e.tensor_tensor(out=x1ap[:, dh * 512:(dh + 1) * 512], in0=pb[0:npart, :], in1=x1ap[:, dh * 512:(dh + 1) * 512], op=ALU.add),
                         reads=[pk, x1key], writes=[x1key])
                rms_stats(x1ap, x1key, npart, 3)
                P.op("dve", lambda e: e.tensor_scalar(out=stat[0:npart, 4:5], in0=stat[0:npart, 3:4], scalar1=1.0 / D, scalar2=RMS_EPS, op0=ALU.mult, op1=ALU.add),
                     reads=[("stat", 3)], writes=[("stat", 4)])
                P.op("dve", lambda e: e.reciprocal(out=rs2[0:npart, ti_stat:ti_stat + 1], in_=stat[0:npart, 4:5]), reads=[("stat", 4)], writes=["rs2"])

            def pool_mix_a(pd, pdk, g, ncols):
                P.op("act" if g % 2 else "dve", (lambda e: e.activation(out=dTb4[:, g, 0:ncols], in_=pd[:, 0:ncols], func=AF.Copy)) if g % 2 else (lambda e: e.tensor_copy(out=dTb4[:, g, 0:ncols], in_=pd[:, 0:ncols])),
                     reads=[pdk], writes=[("dTb", g)])

            def pool_mix_b(g, ncols):
                pb, pk = bank()
                P.op("pe", lambda e: e.matmul(pb[:, 0:ncols], lhsT=POOLW[:, g, :], rhs=dTb4[:, g, 0:ncols], start=True, stop=True), reads=["POOLW", ("dTb", g)], writes=[pk])
                mdst, mkey = (mixT[:, g, 0:ncols], ("mixT", g)) if CX["mixdst"] is None else CX["mixdst"](g)
                P.op("act", lambda e: e.activation(out=mdst, in_=pb[:, 0:ncols], func=AF.Copy, scale=PSC(g)), reads=[pk, "pcol"], writes=[mkey])

            def gn_stage(ysrc, ykey, npart, nh, eps_col):
                y3 = lambda ap: ap.rearrange("p (h v) -> p h v", v=64)
                P.op("dve", lambda e: e.tensor_reduce(out=gst[0:npart, 0:nh], in_=y3(ysrc), axis=AX.X, op=ALU.add), reads=[ykey], writes=["gst"])
                P.op("dve", lambda e: e.tensor_scalar(out=gst[0:npart, 8:8 + nh], in0=gst[0:npart, 0:nh], scalar1=1.0 / 64, scalar2=None, op0=ALU.mult), reads=["gst"], writes=["gst"])
                P.op("dve", lambda e: e.tensor_tensor(out=y3(Yw[0:npart, 0:nh * 64]), in0=y3(ysrc), in1=gst[0:npart, 8:8 + nh].unsqueeze(2).to_broadcast([npart, nh, 64]), op=ALU.subtract),
                     reads=[ykey, "gst"], writes=["Yw"])
                P.op("act", lambda e: e.activation(out=Yq[0:npart, 0:nh * 64], in_=Yw[0:npart, 0:nh * 64], func=AF.Square), reads=["Yw"], writes=["Yq"])
                P.op("dve", lambda e: e.tensor_reduce(out=gst[0:npart, 16:16 + nh], in_=y3(Yq[0:npart, 0:nh * 64]), axis=AX.X, op=ALU.add), reads=["Yq"], writes=["gst"])
                P.op("act", lambda e: e.activation(out=gst[0:npart, 24:24 + nh], in_=gst[0:npart, 16:16 + nh], func=AF.Ln, scale=1.0 / 64, bias=stat[0:npart, eps_col:eps_col + 1]),
                     reads=["gst", ("stat", eps_col)], writes=["gst"])
                P.op("act", lambda e: e.activation(out=gst[0:npart, 32:32 + nh], in_=gst[0:npart, 24:24 + nh], func=AF.Exp, scale=-0.5), reads=["gst"], writes=["gst"])
                P.op("dve", lambda e: e.tensor_tensor(out=y3(Yw[0:npart, 0:nh * 64]), in0=y3(Yw[0:npart, 0:nh * 64]), in1=gst[0:npart, 32:32 + nh].unsqueeze(2).to_broadcast([npart, nh, 64]), op=ALU.mult),
                     reads=["Yw", "gst"], writes=["Yw"])

            P.op("pool", lambda e: e.memset(stat[:, 7:8], RMS_EPS), writes=[("stat", 7)])
            P.op("pool", lambda e: e.memset(stat[:, 6:7], 1e-12), writes=[("stat", 6)])
            P.op("pool", lambda e: e.memset(stat[:, 5:6], GN_EPS), writes=[("stat", 5)])
            P.op("pool", lambda e: e.memset(plast[:], 0.0), writes=[("plast", j) for j in range(14)])
            P.op("pool", lambda e: e.memset(Mf[:], 0.0), writes=["Mf"])
            P.op("pool", lambda e: e.memset(Mb[:], 0.0), writes=["Mb"])

            n = NS
            norm_to_hT(X1s[0:n, :], "x1s", n, 0, n)
            for (c0, c1, dst) in [(0, 512, None), (512, 1024, 0), (1024, 1536, 512), (1536, 2048, 1024), (2048, 2304, 1536)]:
                pb, pk = bank()
                w = c1 - c0
                for c in range(8):
                    P.op("pe", lambda e, c=c, pb=pb, c0=c0, c1=c1, w=w: e.matmul(pb[0:n, 0:w], lhsT=CX["hT"][:, c, 0:n], rhs=WIN[:, c, c0:c1], start=(c == 0), stop=(c == 7)),
                         reads=["WIN", CX["khT"]], writes=[pk], inc=(c == 7))
                if dst is None:
                    P.op("act", lambda e, pb=pb: e.activation(out=splf[0:n, :], in_=pb[0:n, :], func=AF.Copy), reads=[pk], writes=["ytok"])
                else:
                    P.op("act", lambda e, pb=pb, dst=dst, w=w: e.activation(out=ptok[0:n, dst:dst + w], in_=pb[0:n, 0:w], func=AF.Copy), reads=[pk], writes=["ptok"])
            P.dma(shs_d[:, :], ptok[0:n, :], reads=["ptok"], sem="outs")
            P.dma(pls_d[:, 14, :], splf[0:n, :], reads=["ytok"], sem="outs")
            P.dma(pls_d[:, 0:14, :], spool_d.rearrange("(b j) c -> b j c", j=15)[:, 1:15, :], sem="outs")
            P.dma(shtok[0:n, :], sshift_d[:, :], writes=["shtok"], sem="sin")
            pbs = []
            for j in range(14):
                if j % 8 == 0:
                    pb, pk = bank()
                    pbs.append((pb, pk))
                P.op("pe", lambda e, j=j, pb=pb: e.transpose(out=pb[:, (j % 8) * n:(j % 8 + 1) * n], in_=shtok[0:n, j * 128:(j + 1) * 128], identity=identf[0:n, 0:n]),
                     reads=["shtok", "identf"], writes=[pk], inc=(j % 8 == 7 or j == 13))
            P.op("dve", lambda e: e.tensor_copy(out=shT[:, 0:8, :], in_=pbs[0][0][:, 0:8 * n].rearrange("p (j b) -> p j b", b=n)), reads=[pbs[0][1]], writes=["shT"])
            P.op("dve", lambda e: e.tensor_copy(out=shT[:, 8:14, :], in_=pbs[1][0][:, 0:6 * n].rearrange("p (j b) -> p j b", b=n)), reads=[pbs[1][1]], writes=["shT"])
            for tl in range(2):
                P.dma(stgp[0:120, tl, :], spool_d[tl * 120:(tl + 1) * 120, :], writes=["shtok"], sem="sin")
            P.op("dve", lambda e: e.tensor_copy(out=spl[0:120, :, :], in_=stgp[0:120, :, :]), reads=["shtok"], writes=["spl"])
            o_sel, _ = coff["sel"]
            wins = (2, 4, 8, 16)
            for g in range(4):
                pu, puk = bank()
                for c in range(8):
                    P.op("pe", lambda e, c=c, pu=pu, g=g: e.matmul(pu[:, 0:n], lhsT=WIN[:, c, g * 128:(g + 1) * 128], rhs=CX["hT"][:, c, 0:n], start=(c == 0), stop=(c == 7)),
                         reads=["WIN", CX["khT"]], writes=[puk], inc=(c == 7))
                P.op("act", lambda e, pu=pu: e.activation(out=tmpS[:, 0:n], in_=pu[:, 0:n], func=AF.Copy, scale=1.0 / wins[g] - 1.0), reads=[puk], writes=["RN"])
                pd, pdk = bank()
                for tl in range(2):
                    P.op("pe", lambda e, tl=tl, g=g, pd=pd: e.matmul(pd[:, 0:n], lhsT=spl[0:120, tl, g * 128:(g + 1) * 128], rhs=CB[0:120, o_sel + tl * 64 + g * 16:o_sel + tl * 64 + g * 16 + 16], start=(tl == 0), stop=(tl == 1)),
                         reads=["spl", "CB"], writes=[pdk], inc=(tl == 1))
                P.op("dve", lambda e, pd=pd: e.tensor_tensor(out=dTb4[:, 0, 0:n], in0=pd[:, 0:n], in1=tmpS[:, 0:n], op=ALU.add), reads=[pdk, "RN"], writes=[("dTb", 0)])
                pb, pk = bank()
                P.op("pe", lambda e, pb=pb, g=g: e.matmul(pb[:, 0:n], lhsT=POOLW[:, g, :], rhs=dTb4[:, 0, 0:n], start=True, stop=True), reads=["POOLW", ("dTb", 0)], writes=[pk])
                P.op("act", lambda e, pb=pb, g=g: e.activation(out=mixT[:, g, 0:n], in_=pb[:, 0:n], func=AF.Copy, scale=PSC(g)), reads=[pk, "pcol"], writes=["mixT"])
            for j, (dst, key) in [(12, (Z12, "E1")), (13, (Z13, "E3"))]:
                pb, pk = proj_fm(j, n)
                shift_evac(pb, pk, j, dst, key, n, prevT=shT[:, j, :])
            lora_stage(n)
            sigm_stage(n)
            def proj_pair_s(hp):
                for q, nm in enumerate(("Rb", "Kb", "Vb")):
                    j = 4 * q + hp
                    pb, pk = proj_fm(j, n)
                    shift_evac(pb, pk, j, RKV[hp % 2][q], (nm, 0), n, prevT=shT[:, j, :])
            for hp in range(4):
                proj_pair_s(hp)
                preprocess(hp, n, True)
            for q in range(6):
                pb, pk = bank()
                for hp in range(4):
                    P.op("pe", lambda e, q=q, hp=hp, pb=pb: e.transpose(out=pb[0:n, hp * 128:(hp + 1) * 128], in_=SMP[:, q, hp, :], identity=identf[:]),
                         reads=["SMP", "identf"], writes=[pk], inc=(hp == 3))
                P.op("act" if q % 2 else "dve", (lambda e, pb=pb: e.activation(out=ytok[0:n, :], in_=pb[0:n, :], func=AF.Copy)) if q % 2 else (lambda e, pb=pb: e.tensor_copy(out=ytok[0:n, :], in_=pb[0:n, :])),
                     reads=[pk], writes=["ytok"])
                P.dma(scr_d[q], ytok[0:n, :], reads=["ytok"], writes=[("scr", q)], sem="scr")
                P.dma(BH[:, q, :], scr_d[q].rearrange("b (h k) -> (b h) k", k=64), reads=[("scr", q)], writes=["BH"], sem="scr")
            P.dma(BHR[:], bh_d[:, :], writes=["BHR"], sem="sin")
            bq = lambda q: BH[:, q, :]
            bc_v = lambda ap: ap.unsqueeze(1).to_broadcast([128, SQ, 64])
            for sq in range(64 // SQ):
                vs = slice(sq * SQ, (sq + 1) * SQ)
                bc_k = lambda ap: ap[:, vs].unsqueeze(2).to_broadcast([128, SQ, 64])
                bi = sq % 2
                S0q, S1q = S0qs[bi], S1qs[bi]
                k0, k1 = ("S0q", bi), ("S1q", 0)
                if sq == 0:
                    P.dma(S0q[:].rearrange("p v k -> p (v k)"), swkv_d[:, 0:SQ * 64], writes=[k0], sem="sin0")
                if sq + 1 < 64 // SQ:
                    P.dma(S0qs[1 - bi][:].rearrange("p v k -> p (v k)"), swkv_d[:, (sq + 1) * SQ * 64:(sq + 2) * SQ * 64], writes=[("S0q", 1 - bi)], sem=f"sin{1 - bi}")
                P.op("pool", lambda e, bc_k=bc_k: e.tensor_tensor(out=Stq2[:], in0=bc_v(bq(2)), in1=bc_k(bq(3)), op=ALU.mult), reads=["BH"], writes=["ytok"])
                P.op("dve", lambda e: e.tensor_tensor(out=Stq[:], in0=S0q[:], in1=bc_v(bq(4)), op=ALU.mult), reads=[k0, "BH"], writes=["Stq"])
                P.op("dve", lambda e, vs=vs: e.tensor_reduce(out=sa_s[:, vs], in_=Stq[:], axis=AX.X, op=ALU.add), reads=["Stq"], writes=["sa_s"])
                P.op("dve", lambda e: e.tensor_tensor(out=S1q[:], in0=S0q[:], in1=bc_v(bq(1)), op=ALU.mult), reads=[k0, "BH"], writes=[k1])
                P.op("dve", lambda e, bc_k=bc_k: e.tensor_tensor(out=Stq[:], in0=bc_v(bq(5)), in1=bc_k(sa_s), op=ALU.mult), reads=["sa_s", "BH"], writes=["Stq"])
                P.op("dve", lambda e: e.tensor_tensor(out=S1q[:], in0=S1q[:], in1=Stq[:], op=ALU.add), reads=[k1, "Stq"], writes=[k1])
                P.op("dve", lambda e: e.tensor_tensor(out=S1q[:], in0=S1q[:], in1=Stq2[:], op=ALU.add), reads=[k1, "ytok"], writes=[k1])
                P.dma(wkvs_d[:, sq * SQ * 64:(sq + 1) * SQ * 64], S1q[:].rearrange("p v k -> p (v k)"), reads=[k1], sem=f"outw{bi}", q="pool")
                P.op("dve", lambda e: e.tensor_tensor(out=Stq[:], in0=S1q[:], in1=bc_v(bq(0)), op=ALU.mult), reads=[k1, "BH"], writes=["Stq"])
                P.op("dve", lambda e, vs=vs: e.tensor_reduce(out=y_s[:, vs], in_=Stq[:], axis=AX.X, op=ALU.add), reads=["Stq"], writes=["y_s"])
            gn_stage(y_s[:, :], "y_s", 128, 1, 5)
            P.op("dve", lambda e: e.tensor_tensor(out=y_s2[:], in0=Yw[:, 0:64], in1=BHR[:, 0:64], op=ALU.mult), reads=["Yw", "BHR"], writes=["y_s2"])
            P.op("dve", lambda e: e.tensor_tensor(out=y_s2[:], in0=y_s2[:], in1=BHR[:, 64:128], op=ALU.add), reads=["y_s2", "BHR"], writes=["y_s2"])
            P.op("dve", lambda e: e.tensor_tensor(out=y_s[:], in0=bq(0), in1=bq(2), op=ALU.mult), reads=["BH", "y_s"], writes=["y_s"])
            P.op("dve", lambda e: e.tensor_tensor(out=y_s[:], in0=y_s[:], in1=BHR[:, 128:192], op=ALU.mult), reads=["y_s", "BHR"], writes=["y_s"])
            P.op("dve", lambda e: e.tensor_reduce(out=gst[:, 39:40], in_=y_s[:], axis=AX.X, op=ALU.add), reads=["y_s"], writes=["gst"])
            P.op("dve", lambda e: e.scalar_tensor_tensor(out=y_s2[:], in0=bq(3), scalar=gst[:, 39:40], in1=y_s2[:], op0=ALU.mult, op1=ALU.add), reads=["BH", "gst", "y_s2"], writes=["y_s2"])
            P.dma(scr_d[6].rearrange("b (h k) -> (b h) k", k=64), y_s2[:], reads=["y_s2"], writes=[("scr", 6)], sem="scr")
            P.dma(ytok[0:n, :], scr_d[6], reads=[("scr", 6)], writes=["ytok"], sem="scr")
            pg, pgk = bank()
            P.op("pe", lambda e: e.matmul(pg[0:n, :], lhsT=CX["sgzb"][:, 0:n], rhs=GLO[:], start=True, stop=True), reads=[CX["ksg"], "GLO"], writes=[pgk])
            P.op("dve", lambda e: e.tensor_tensor(out=Yo[0:n, :], in0=pg[0:n, :], in1=ytok[0:n, :], op=ALU.mult), reads=[pgk, "ytok"], writes=["Yo"])
            pb, pk = bank()
            pbb = pb[:].bitcast(BF16)
            for hp in range(4):
                P.op("pe", lambda e, hp=hp: e.transpose(out=pbb[:, hp * n:(hp + 1) * n], in_=Yo[0:n, hp * 128:(hp + 1) * 128], identity=identb[0:n, 0:n]),
                     reads=["Yo", "CB"], writes=[pk], inc=(hp == 3))
            P.op("dve", lambda e: e.tensor_copy(out=mixT[:, 4:8, 0:n], in_=pbb[:, 0:4 * n].rearrange("p (c t) -> p c t", t=n)), reads=[pk], writes=["mixT"])
            out_proj(n, 0, X1s[0:n, :], "x1s", NTILE)

            if KSTOP <= 2:
                raise _Stop()
            P.barrier()
            M.release(mS)
            ubf = M.alloc("ubf", [128, 3, 512], BF16)
            uf32 = Yq
            CUM = M.alloc("CUM", [128, ST], F32)
            E2 = M.alloc("E2", [128, ST], F32)
            CM = E2
            ART = M.alloc("ART", [128, 4, 2, ST], BF16)
            BT = M.alloc("BT", [128, 4, ST], BF16); KT = M.alloc("KT", [128, 4, ST], BF16)
            VBF = M.alloc("VBF", [128, 4, ST], BF16); RKB = M.alloc("RKB", [128, 4, ST], BF16)
            WC = M.alloc("WC", [128, 4, ST // 128], F32)
            btok = M.alloc("btok", [128, 512], BF16); ktok = M.alloc("ktok", [128, 512], BF16); vtok = M.alloc("vtok", [128, 512], BF16)
            L0 = M.alloc("L0", [128, 8, 128], BF16); Q0 = M.alloc("Q0", [128, 8, 128], BF16)
            UA = M.alloc("UA", [128, 8, 2, 128], BF16)
            KA = M.alloc("KA", [128, 8, 2, 128], BF16)
            Qb = [Q0, Q0]
            Xsb = KKN[:].bitcast(BF16); SAsb = Bb[:].bitcast(BF16)
            def P1gen(st):
                ntl = ST // 128
                for tl in range(ntl):
                    ti = st * ntl + tl
                    if tl == 0:
                        if st == 0:
                            P.op("pool", lambda e: e.memset(CX["hT"][:, :, 0:1], 0.0), writes=[CX["khT"]])
                        else:
                            P.op("dve", lambda e: e.tensor_copy(out=CX["hT"][:, :, 0:1], in_=hTs[0][:, :, ST:ST + 1]), reads=[("hT", 0)], writes=[CX["khT"]])
                    norm_to_hT(X1[:, ti, :], ("x1", ti), 128, 1 + tl * 128, ST)
                    yield
                for tl in range(ntl):
                    ti = st * ntl + tl
                    pb, pk = bank()
                    for c in range(8):
                        P.op("pe", lambda e, c=c, pb=pb, tl=tl: e.matmul(pb[:, :], lhsT=CX["hT"][:, c, 1 + tl * 128:1 + (tl + 1) * 128], rhs=WIN[:, c, 0:512], start=(c == 0), stop=(c == 7)),
                             reads=["WIN", CX["khT"]], writes=[pk], inc=(c == 7))
                    P.op("act", lambda e, pb=pb, ti=ti: e.activation(out=ubf[:, ti % 3, :], in_=pb[:, :], func=AF.Copy), reads=[pk], writes=[("ubf", ti % 3)])
                    if ti == NTILE - 1 and KVAR != 2 and KVAR != 3:
                        P.op("act", lambda e, pb=pb: e.activation(out=uf32[:, :], in_=pb[:, :], func=AF.Copy), reads=[pk], writes=["Yq"])
                        if KVAR == 6:
                            continue
                        pb2, pk2 = bank()
                        P.op("pe", lambda e: e.matmul(pb2[0:16, :], lhsT=identf[:, 112:128], rhs=uf32[:, :], start=True, stop=True),
                             reads=["Yq", "identf"], writes=[pk2])
                        P.op("dve", lambda e: e.tensor_copy(out=Yw[0:16, :], in_=pb2[0:16, :]), reads=[pk2], writes=["Yw"])
                        if KVAR != 5:
                            P.dma(plp_d[:, :], Yw[0:16, :], reads=["Yw"], sem="outp")
                yield
                ob, _ = coff["band"]; obp, _ = coff["bprev"]; obh, _ = coff["bfhi"]; obl, _ = coff["bflo"]
                for g in range(4):
                    pd, pdk = bank()
                    for tl in range(ntl):
                        ti = st * ntl + tl
                        lhs = ubf[:, ti % 3, g * 128:(g + 1) * 128]
                        ocol = pd[:, tl * 128:(tl + 1) * 128]
                        if ti == 0:
                            P.op("pe", lambda e, lhs=lhs, ocol=ocol, g=g: e.matmul(ocol, lhsT=lhs, rhs=CB[:, obh + g * 128:obh + (g + 1) * 128], start=True, stop=False),
                                 reads=[("ubf", 0), "CB"], writes=[pdk], inc=False)
                            P.op("pe", lambda e, lhs=lhs, ocol=ocol, g=g: e.matmul(ocol, lhsT=lhs, rhs=CB[:, obl + g * 128:obl + (g + 1) * 128], start=False, stop=True),
                                 reads=[("ubf", 0), "CB"], writes=[pdk], inc=True)
                        else:
                            lhsp = ubf[:, (ti - 1) % 3, g * 128:(g + 1) * 128]
                            P.op("pe", lambda e, lhs=lhs, ocol=ocol, g=g: e.matmul(ocol, lhsT=lhs, rhs=CB[:, ob + g * 128:ob + (g + 1) * 128], start=True, stop=False),
                                 reads=[("ubf", ti % 3), "CB"], writes=[pdk], inc=False)
                            P.op("pe", lambda e, lhsp=lhsp, ocol=ocol, g=g: e.matmul(ocol, lhsT=lhsp, rhs=CB[:, obp + g * 128:obp + (g + 1) * 128], start=False, stop=True),
                                 reads=[("ubf", (ti - 1) % 3), "CB"], writes=[pdk], inc=True)
                    pool_mix_a(pd, pdk, g, ST)
                    yield
                for g in range(4):
                    pool_mix_b(g, ST)
                    yield
                for j, (dst, key) in [(12, (Z12, "E1")), (13, (Z13, "E3"))]:
                    pb, pk = proj_fm(j, ST + 1)
                    shift_evac_p(pb, pk, j, dst, key, st == T // ST - 1)
                    yield
                lora_stage(ST)
                sigm_stage(ST)
                yield

            def P2(st):
                def proj_pair(hp):
                    for q, nm in enumerate(("Rb", "Kb", "Vb")):
                        j = 4 * q + hp
                        pb, pk = proj_fm(j, ST + 1)
                        shift_evac_p(pb, pk, j, RKV[hp % 2][q], (nm, 0), st == T // ST - 1)
                for hp in range(4):
                    proj_pair(hp)
                    preprocess(hp, ST, False)
                if st == T // ST - 1 and KVAR != 2 and KVAR != 4:
                    pb, pk = bank()
                    P.op("pe", lambda e: e.transpose(out=pb[0:14, 0:128], in_=plast[:, 0:14], identity=identf[:]),
                         reads=[("plast", j) for j in range(14)] + ["identf"], writes=[pk])
                    P.op("dve", lambda e: e.tensor_copy(out=tmpS[0:14, 0:128], in_=pb[0:14, 0:128]), reads=[pk], writes=["RN"])
                    P.dma(shp_d[:, :], tmpS[0:14, 0:128], reads=["RN"], sem="outp")
                if DEBUG_TAPS and st == 0:
                    tap("art", ART[:].rearrange("p a b t -> p (a b t)"), [128, 8 * ST], ["ART"])
                    tap("bt", BT[:].rearrange("p a t -> p (a t)"), [128, 4 * ST], ["BT"])
                    tap("kt", KT[:].rearrange("p a t -> p (a t)"), [128, 4 * ST], ["KT"])
                    tap("mixT", mixT[:].rearrange("p a t -> p (a t)"), [128, 8 * ST], ["mixT"])

            def CHgen(st):
                ntl = ST // 128
                for ck in range(ntl):
                    ti = st * ntl + ck
                    cs = slice(ck * 128, (ck + 1) * 128)
                    for src, skey, dst, dkey in [(BT, "BT", btok, "btok"), (KT, "KT", ktok, "ktok"), (VBF, "VBF", vtok, "vtok")]:
                        pb, pk = bank()
                        pbb = pb[:].bitcast(BF16)
                        for hp in range(4):
                            P.op("pe", lambda e, hp=hp, src=src, pbb=pbb: e.transpose(out=pbb[:, hp * 128:(hp + 1) * 128], in_=src[:, hp, cs], identity=identb),
                                 reads=[skey, "CB"], writes=[pk], inc=(hp == 3))
                        P.op("act", lambda e, dst=dst, pbb=pbb: e.activation(out=dst[:, :], in_=pbb[:, 0:512], func=AF.Copy), reads=[pk], writes=[dkey])
                    yield
                    for hl in range(2):
                        pr = slice(hl * 64, hl * 64 + 64)
                        pl, plk = bank()
                        for hb in range(2):
                            pa, pak = bank()
                            pk2b, pk2k = bank()
                            for j in range(2):
                                h2 = 2 * hb + j
                                h = 2 * h2 + hl
                                rhs_ar = ART[pr, h2, :, cs]
                                P.op("pe", lambda e: e.matmul(pa[:, j * 256:(j + 1) * 256].rearrange("p (a t) -> p a t", a=2), lhsT=BT[pr, h2, cs], rhs=rhs_ar, start=True, stop=True),
                                     reads=["BT", "ART"], writes=[pak], inc=(j == 1))
                                P.op("pe", lambda e: e.matmul(pk2b[:, j * 256:(j + 1) * 256].rearrange("p (a t) -> p a t", a=2), lhsT=KT[pr, h2, cs], rhs=rhs_ar, start=True, stop=True),
                                     reads=["KT", "ART"], writes=[pk2k], inc=(j == 1))
                                P.op("pe", lambda e: e.matmul(pl[:, h2 * 128:(h2 + 1) * 128], lhsT=ART[pr, h2, 0, cs], rhs=BT[pr, h2, cs], start=True, stop=True),
                                     reads=["BT", "ART"], writes=[plk], inc=(j == 1))
                            if KVAR != 1:
                                mb4 = mask_ai.unsqueeze(1).to_broadcast([128, 2, 2, 128])
                                uav = UA[:].rearrange("p (b j l) a t -> p b j l a t", b=2, j=2, l=2)[:, hb, :, hl, :, :]
                                kav = KA[:].rearrange("p (b j l) a t -> p b j l a t", b=2, j=2, l=2)[:, hb, :, hl, :, :]
                                P.op("dve", lambda e: e.tensor_tensor(out=uav, in0=pa[:, :].rearrange("p (j a t) -> p j a t", j=2, a=2), in1=mb4, op=ALU.mult),
                                     reads=[pak, "CB"], writes=[("U", hb), "ARB"])
                                P.op("dve", lambda e: e.tensor_tensor(out=kav, in0=pk2b[:, :].rearrange("p (j a t) -> p j a t", j=2, a=2), in1=mb4, op=ALU.mult),
                                     reads=[pk2k, "CB"], writes=["AAK", "ARK"])
                        if KVAR != 1:
                            lav = L0[:].rearrange("p (a l) t -> p a l t", l=2)[:, :, hl, :]
                            P.op("dve", lambda e: e.tensor_tensor(out=lav, in0=pl[:, :].rearrange("p (a t) -> p a t", a=4), in1=mask_sl.unsqueeze(1).to_broadcast([128, 4, 128]), op=ALU.mult),
                                 reads=[plk, "CB"], writes=[("L", 0), ("L", 1)])
                    yield
                    P.op("pool", lambda e: e.tensor_tensor(out=Q0[:], in0=UA[:, :, 0, :], in1=identb.unsqueeze(1).to_broadcast([128, 8, 128]), op=ALU.add),
                         reads=[("U", 0), ("U", 1), "CB"], writes=[("Q", 0), ("Q", 1)])
                    for lv in range(1, 7):
                        need_u = lv < 6
                        bk = {}
                        for hg in range(2):
                            pu, puk = bank() if need_u else (None, None)
                            pl, plk = bank()
                            bk[hg] = (pu, puk, pl, plk)
                            for hh in range(4):
                                h = hg * 4 + hh
                                if need_u:
                                    P.op("pe", lambda e: e.matmul(pu[:, hh * 128:(hh + 1) * 128], lhsT=L0[:, h, :], rhs=UA[:, h, 0, :], start=True, stop=True),
                                         reads=[("L", hg), ("U", hg)], writes=[puk], inc=(hh == 3))
                                P.op("pe", lambda e: e.matmul(pl[:, hh * 128:(hh + 1) * 128], lhsT=UA[:, h, 0, :], rhs=L0[:, h, :], start=True, stop=True),
                                     reads=[("L", hg), ("U", hg)], writes=[plk], inc=(hh == 3))
                        for hg in range(2):
                            hs = slice(hg * 4, hg * 4 + 4)
                            pu, puk, pl, plk = bk[hg]
                            P.op("act", lambda e: e.activation(out=L0[:, hs, :], in_=pl[:, :].rearrange("p (h t) -> p h t", h=4), func=AF.Copy),
                                 reads=[plk], writes=[("L", hg)])
                            if need_u:
                                if hg == 0:
                                    P.op("act", lambda e: e.activation(out=UA[:, hs, 0, :], in_=pu[:, :].rearrange("p (h t) -> p h t", h=4), func=AF.Copy),
                                         reads=[puk], writes=[("U", hg)])
                                else:
                                    P.op("dve", lambda e: e.tensor_copy(out=UA[:, hs, 0, :], in_=pu[:, :].rearrange("p (h t) -> p h t", h=4)),
                                         reads=[puk], writes=[("U", hg)])
                        bq_ = {}
                        for hg in range(2):
                            pq, pqk = bank()
                            bq_[hg] = (pq, pqk)
                            pe_warm_small(2, pq, pqk)
                            for hh in range(4):
                                h = hg * 4 + hh
                                P.op("pe", lambda e: e.matmul(pq[:, hh * 128:(hh + 1) * 128], lhsT=L0[:, h, :], rhs=Q0[:, h, :], start=True, stop=True),
                                     reads=[("L", hg), ("Q", hg)], writes=[pqk], inc=(hh == 3))
                        for hg in range(2):
                            hs = slice(hg * 4, hg * 4 + 4)
                            pq, pqk = bq_[hg]
                            P.op("dve", lambda e: e.tensor_tensor(out=Q0[:, hs, :], in0=pq[:, :].rearrange("p (h t) -> p h t", h=4), in1=Q0[:, hs, :], op=ALU.add),
                                 reads=[pqk, ("Q", hg)], writes=[("Q", hg)])
                        yield
                    Qf = Qb[0]
                    px, pxk = bank()
                    for h in range(8):
                        h2, hl = h // 2, h % 2
                        pr = slice(hl * 64, hl * 64 + 64)
                        oc = px[:, h * 64:(h + 1) * 64]
                        P.op("pe", lambda e, oc=oc, pr=pr, h2=h2: e.matmul(oc, lhsT=ART[:, h2, 0, cs], rhs=Mb[:, h2, hl, :], start=True, stop=False),
                             reads=["ART", "Mb"], writes=[pxk], inc=False)
                        P.op("pe", lambda e, oc=oc, h=h: e.matmul(oc, lhsT=KA[:, h, 0, :], rhs=vtok[:, h * 64:(h + 1) * 64], start=False, stop=True),
                             reads=["AAK", "vtok"], writes=[pxk], inc=(h == 7))
                    P.op("act", lambda e, px=px: e.activation(out=Xsb[:, :], in_=px[:, :], func=AF.Copy), reads=[pxk], writes=["KKN"])
                    yield
                    psa, psak = bank()
                    for h in range(8):
                        P.op("pe", lambda e, h=h, psa=psa: e.matmul(psa[:, h * 64:(h + 1) * 64], lhsT=Qf[:, h, :], rhs=Xsb[:, h * 64:(h + 1) * 64], start=True, stop=True),
                             reads=[("Q", h // 4), "KKN"], writes=[psak], inc=(h == 7))
                    P.op("act", lambda e, psa=psa: e.activation(out=SAsb[:, :], in_=psa[:, :], func=AF.Copy), reads=[psak], writes=["Bb"])
                    yield
                    py, pyk = bank()
                    for h in range(8):
                        h2, hl = h // 2, h % 2
                        pr = slice(hl * 64, hl * 64 + 64)
                        oc = py[:, h * 64:(h + 1) * 64]
                        P.op("pe", lambda e, oc=oc, pr=pr, h2=h2: e.matmul(oc, lhsT=ART[:, h2, 1, cs], rhs=Mb[:, h2, hl, :], start=True, stop=False),
                             reads=["ART", "Mb"], writes=[pyk], inc=False)
                        P.op("pe", lambda e, oc=oc, h=h: e.matmul(oc, lhsT=UA[:, h, 1, :], rhs=SAsb[:, h * 64:(h + 1) * 64], start=False, stop=False),
                             reads=["ARB", "Bb"], writes=[pyk], inc=False)
                        P.op("pe", lambda e, oc=oc, h=h: e.matmul(oc, lhsT=KA[:, h, 1, :], rhs=vtok[:, h * 64:(h + 1) * 64], start=False, stop=True),
                             reads=["ARK", "vtok"], writes=[pyk], inc=(h == 7))
                    pm, pmk = bank()
                    for h in range(8):
                        h2 = h // 2
                        oc = pm[:, h * 64:(h + 1) * 64]
                        P.op("pe", lambda e, oc=oc, h=h, h2=h2: e.matmul(oc, lhsT=btok[:, h2 * 128:(h2 + 1) * 128], rhs=SAsb[:, h * 64:(h + 1) * 64], start=True, stop=False),
                             reads=["btok", "Bb"], writes=[pmk], inc=False)
                        P.op("pe", lambda e, oc=oc, h=h, h2=h2: e.matmul(oc, lhsT=ktok[:, h2 * 128:(h2 + 1) * 128], rhs=vtok[:, h * 64:(h + 1) * 64], start=False, stop=True),
                             reads=["ktok", "vtok"], writes=[pmk], inc=(h == 7))
                    for hl in range(2):
                        pr = slice(hl * 64, hl * 64 + 64)
                        src = pm[pr, :].rearrange("p (a l v) -> p a l v", a=4, l=2)[:, :, hl, :]
                        P.op("dve", lambda e, pr=pr, src=src: e.tensor_tensor(out=Mt[pr, :, :], in0=src, in1=Mf[pr, :, :], op=ALU.add), reads=[pmk, "Mf"], writes=["Mt"])
                        P.op("dve", lambda e, pr=pr: e.tensor_tensor(out=Mf[pr, :, :], in0=Mt[pr, :, :], in1=WC[pr, :, ck:ck + 1].to_broadcast([64, 4, 64]), op=ALU.mult),
                             reads=["Mt", "WC"], writes=["Mf"])
                        P.op("act", lambda e, pr=pr: e.activation(out=Mb[pr, :, hl, :], in_=Mf[pr, :, :], func=AF.Copy), reads=["Mf"], writes=["Mb"])
                    pe_warm(10)
                    gn_stage(py[:, :], pyk, 128, 8, 5)
                    P.op("dve", lambda e: e.tensor_tensor(out=Yw[:, :], in0=Yw[:, :], in1=LNG[:, :], op=ALU.mult), reads=["Yw", "LNG"], writes=["Yw"])
                    P.op("dve", lambda e: e.tensor_tensor(out=Yw[:, :], in0=Yw[:, :], in1=LNB[:, :], op=ALU.add), reads=["Yw", "LNB"], writes=["Yw"])
                    pbn, pbnk = bank()
                    for hp in range(4):
                        P.op("pe", lambda e, hp=hp, pbn=pbn: e.matmul(pbn[:, 0:8], lhsT=RKB[:, hp, cs], rhs=RKI[:, hp, :], start=(hp == 0), stop=(hp == 3)),
                             reads=["RKB", "RKI"], writes=[pbnk], inc=(hp == 3))
                    P.op("act", lambda e, pbn=pbn: e.activation(out=gst[:, 0:8], in_=pbn[:, 0:8], func=AF.Copy), reads=[pbnk], writes=["gst"])
                    P.op("dve", lambda e: e.tensor_tensor(out=Yq[:, :].rearrange("p (h v) -> p h v", v=64), in0=vtok[:, :].rearrange("p (h v) -> p h v", v=64),
                                                           in1=gst[:, 0:8].unsqueeze(2).to_broadcast([128, 8, 64]), op=ALU.mult), reads=["vtok", "gst"], writes=["Yq"])
                    P.op("dve", lambda e: e.tensor_tensor(out=Yw[:, :], in0=Yw[:, :], in1=Yq[:, :], op=ALU.add), reads=["Yw", "Yq"], writes=["Yw"])
                    pg, pgk = bank()
                    P.op("pe", lambda e, pg=pg: e.matmul(pg[:, :], lhsT=CX["sgzb"][:, cs], rhs=GLO[:], start=True, stop=True), reads=[CX["ksg"], "GLO"], writes=[pgk])
                    P.op("dve", lambda e, pg=pg: e.tensor_tensor(out=Yo[:, :], in0=pg[:, :], in1=Yw[:, :], op=ALU.mult), reads=[pgk, "Yw"], writes=["Yo"])
                    pb, pk = bank()
                    pbb = pb[:].bitcast(BF16)
                    for hp in range(4):
                        P.op("pe", lambda e, hp=hp, pbb=pbb: e.transpose(out=pbb[:, hp * 128:(hp + 1) * 128], in_=Yo[:, hp * 128:(hp + 1) * 128], identity=identb),
                             reads=["Yo", "CB"], writes=[pk], inc=(hp == 3))
                    P.op("act", lambda e, pbb=pbb: e.activation(out=mixT[:, 4:8, cs], in_=pbb[:, 0:512].rearrange("p (c t) -> p c t", c=4), func=AF.Copy), reads=[pk], writes=["mixT"])
                    yield

            def OUT(st):
                ntl = ST // 128
                for tl in range(ntl):
                    ti = st * ntl + tl
                    out_proj(128, tl * 128, X1[:, ti, :], ("x1", ti), ti)
                if DEBUG_TAPS and st == 0:
                    tap("x1_0", X1[:, 0, :], [128, D], [("x1", 0)])

            def ctx(st, pn, banks):
                i = 0
                CX.update(hT=hTs[i], khT=("hT", i), sgzb=sgzbs[i], ksg=("sgzb", i), pn=pn, banks=banks,
                          mixdst=lambda g: (mixPs[i][:, g, 0:ST], ("mixP", i, g)),
                          mixsrc=lambda c, c0, npart: (mixPs[i][:, c, c0:c0 + npart], ("mixP", i, c)))

            ALLB = list(range(8))
            ctx(0, "all", ALLB)
            for _ in P1gen(0):
                pass
            for st in range(T // ST):
                ctx(st, "all", ALLB)
                P2(st)
                for _ in CHgen(st):
                    pass
                g_p1 = P1gen(st + 1) if st + 1 < T // ST else None
                if g_p1 is not None:
                    for _ in range(3):
                        next(g_p1)
                OUT(st)
                if g_p1 is not None:
                    for _ in g_p1:
                        pass
            if KSTOP <= 2.6:
                raise _Stop()
            for hp in range(4):
                pb, pk = bank()
                P.op("pe", lambda e, hp=hp, pb=pb: e.transpose(out=pb[0:64, 0:128], in_=Mf[:, hp, :], identity=identf[:]), reads=["Mf", "identf"], writes=[pk])
                P.op("dve", lambda e, pb=pb: e.tensor_copy(out=Yw[0:64, 0:128], in_=pb[0:64, 0:128]), reads=[pk], writes=["Yw"])
                P.dma(wkvp_d[2 * hp:2 * hp + 2].rearrange("l v k -> v l k"), Yw[0:64, 0:128].rearrange("p (l k) -> p l k", l=2), reads=["Yw"], sem="outp")

            if KSTOP <= 3:
                raise _Stop()
            P.barrier()
            M.release(mA)
            FB = 1024
            GF = M.alloc("GF", [128, D], F32)
            P.dma(GF[:], rows_d[2, :].partition_broadcast(128), writes=["GF"], sem="par")
            WUP = [M.alloc(f"WUP{i}", [128, 8, FB], BF16) for i in range(2)]
            WDN = [M.alloc(f"WDN{i}", [128, 8, D], BF16) for i in range(2)]
            stg = [M.alloc(f"stgm{i}", [128, FB], F32) for i in range(2)]
            xb16 = M.alloc("xb16", [128, D], BF16)
            H2T = M.alloc("H2T", [128, 8, T], BF16)
            H2Ts = M.alloc("H2Ts", [128, 8, NS], BF16)
            rl = M.alloc("rl", [128, 512], BF16)
            actT = M.alloc("actT", [128, 8, 512], BF16)
            statB = M.alloc("statB", [128, 8], F32)
            P.op("pool", lambda e: e.memset(statB[:, 7:8], RMS_EPS), writes=[("statB", 7)])
            nfb = DFF // FB
            sn = [0]

            def load_block(fb):
                sl = fb % 2
                for c in range(8):
                    s = sn[0] % 2
                    sn[0] += 1
                    P.dma(stg[s][:, :], w_up_d[c * 128:(c + 1) * 128, fb * FB:(fb + 1) * FB], writes=[("stgm", s)], sem=f"wm{s}")
                    P.op("dve", lambda e, c=c, s=s, sl=sl: e.tensor_scalar(out=WUP[sl][:, c, :], in0=stg[s][:, :], scalar1=G2(c), scalar2=None, op0=ALU.mult),
                         reads=[("stgm", s), "pcol"], writes=[("WUP", sl)])
                for fc in range(8):
                    s = sn[0] % 2
                    sn[0] += 1
                    r0 = fb * FB + fc * 128
                    P.dma(stg[s][:, :], w_dn_d[r0:r0 + 128, :], writes=[("stgm", s)], sem=f"wm{s}")
                    P.op("act", lambda e, fc=fc, s=s, sl=sl: e.activation(out=WDN[sl][:, fc, :], in_=stg[s][:, :], func=AF.Copy),
                         reads=[("stgm", s)], writes=[("WDN", sl)])

            def block_chunks(fb):
                sl = fb % 2
                out = []
                for k in range(16):
                    s_ = k % 2
                    if k < 8:
                        c = k
                        d = lambda s_=s_, c=c: P.dma(stg[s_][:, :], w_up_d[c * 128:(c + 1) * 128, fb * FB:(fb + 1) * FB], writes=[("stgm", s_)], sem=f"wm{s_}")
                        f = lambda s_=s_, c=c: P.op("dve", lambda e: e.tensor_scalar(out=WUP[sl][:, c, :], in0=stg[s_][:, :], scalar1=G2(c), scalar2=None, op0=ALU.mult),
                                                   reads=[("stgm", s_), "pcol"], writes=[("WUP", sl)])
                    else:
                        fc = k - 8
                        r0 = fb * FB + fc * 128
                        d = lambda s_=s_, r0=r0: P.dma(stg[s_][:, :], w_dn_d[r0:r0 + 128, :], writes=[("stgm", s_)], sem=f"wm{s_}")
                        f = lambda s_=s_, fc=fc: P.op("act", lambda e: e.activation(out=WDN[sl][:, fc, :], in_=stg[s_][:, :], func=AF.Copy),
                                                     reads=[("stgm", s_)], writes=[("WDN", sl)])
                    out.append((d, f))
                return out

            groups = [(g * 4, 4, 128) for g in range(NTILE // 4)] + [(NTILE, 1, NS)]
            xap = lambda tcol: (X1[:, tcol, :], ("x1", tcol)) if tcol < NTILE else (X1s[0:NS, :], "x1s")
            blk0 = block_chunks(0)
            blk0[0][0]()
            blk0[1][0]()
            for tcol in range(NTILE + 1):
                npart = 128 if tcol < NTILE else NS
                xa, xk = xap(tcol)
                P.op("act", lambda e: e.activation(out=xb16[0:npart, :], in_=xa, func=AF.Copy), reads=[xk], writes=["xb16"])
                pb, pk = bank()
                pbb = pb[:].bitcast(BF16)
                for c in range(8):
                    P.op("pe", lambda e: e.transpose(out=pbb[:, c * 128:c * 128 + npart], in_=xb16[0:npart, c * 128:(c + 1) * 128], identity=identb[0:npart, 0:npart]),
                         reads=["xb16", "CB"], writes=[pk], inc=(c == 7))
                dst = H2T[:, :, tcol * 128:(tcol + 1) * 128] if tcol < NTILE else H2Ts[:, :, 0:NS]
                P.op("dve", lambda e: e.tensor_copy(out=dst, in_=pbb.rearrange("p (c t) -> p c t", c=8)[:, :, 0:npart]),
                     reads=[pk], writes=[("H2T", tcol)])
                if tcol < 16:
                    blk0[tcol][1]()
                    if tcol + 2 < 16:
                        blk0[tcol + 2][0]()
            for fb in range(nfb):
                sl = fb % 2
                nxt = block_chunks(fb + 1) if fb + 1 < nfb else None
                if nxt is not None:
                    nxt[0][0]()
                    nxt[1][0]()
                gi = 0
                for (t0, nt, npart) in groups:
                    ncols = nt * npart
                    hsrc = (lambda c: H2T[:, c, t0 * 128:t0 * 128 + ncols]) if npart == 128 else (lambda c: H2Ts[:, c, 0:NS])
                    hkeys = [("H2T", t0 + k) for k in range(nt)]
                    if nxt is not None and gi < 4:
                        for kk_ in range(4 * gi, 4 * gi + 4):
                            nxt[kk_][1]()
                            if kk_ + 2 < 16:
                                nxt[kk_ + 2][0]()
                    gi += 1
                    for fc in range(8):
                        pb, pk = bank()
                        for c in range(8):
                            P.op("pe", lambda e: e.matmul(pb[:, 0:ncols], lhsT=WUP[sl][:, c, fc * 128:(fc + 1) * 128], rhs=hsrc(c), start=(c == 0), stop=(c == 7)),
                                 reads=[("WUP", sl)] + hkeys, writes=[pk], inc=(c == 7))
                        P.op("act", lambda e: e.activation(out=rl[:, 0:ncols], in_=pb[:, 0:ncols], func=AF.Relu), reads=[pk], writes=["rl"])
                        P.op("dve", lambda e: e.tensor_tensor(out=actT[:, fc, 0:ncols], in0=rl[:, 0:ncols], in1=rl[:, 0:ncols], op=ALU.mult), reads=["rl"], writes=[("actT", fc)])
                    for k in range(nt):
                        tcol = t0 + k
                        xa, xk = xap(tcol)
                        for dh in range(2):
                            pb, pk = bank()
                            for fc in range(8):
                                P.op("pe", lambda e: e.matmul(pb[0:npart, :], lhsT=actT[:, fc, k * npart:(k + 1) * npart], rhs=WDN[sl][:, fc, dh * 512:(dh + 1) * 512], start=(fc == 0), stop=(fc == 7)),
                                     reads=[("actT", fc), ("WDN", sl)], writes=[pk], inc=(fc == 7))
                            P.op("dve", lambda e: e.scalar_tensor_tensor(out=xa[:, dh * 512:(dh + 1) * 512], in0=pb[0:npart, :], scalar=rs2[0:npart, tcol:tcol + 1],
                                                                         in1=xa[:, dh * 512:(dh + 1) * 512], op0=ALU.mult, op1=ALU.add),
                                 reads=[pk, xk, "rs2"], writes=[xk])
                        if fb == nfb - 1:
                            P.op("act", lambda e: e.activation(out=xb16[0:npart, :], in_=xa, func=AF.Square, accum_out=statB[0:npart, 0:1]), reads=[xk], writes=["xb16", ("statB", 0)])
                            P.op("act", lambda e: e.activation(out=statB[0:npart, 1:2], in_=statB[0:npart, 0:1], func=AF.Ln, scale=1.0 / D, bias=statB[0:npart, 7:8]),
                                 reads=[("statB", 0), ("statB", 7)], writes=[("statB", 1)])
                            P.op("act", lambda e: e.activation(out=statB[0:npart, 2:3], in_=statB[0:npart, 1:2], func=AF.Exp, scale=-0.5), reads=[("statB", 1)], writes=[("statB", 2)])
                            P.op("dve", lambda e: e.scalar_tensor_tensor(out=xa, in0=xa, scalar=statB[0:npart, 2:3], in1=GF[0:npart, :], op0=ALU.mult, op1=ALU.mult),
                                 reads=[xk, ("statB", 2), "GF"], writes=[xk])
                            if npart == 128:
                                P.dma(y_d[tcol * 128:(tcol + 1) * 128, :], xa, reads=[xk], sem=f"yo{tcol % 4}", q="pool")
                            else:
                                P.dma(ys_d[:, :], xa, reads=[xk], sem=f"yo{tcol % 4}", q="pool")
        except _Stop:
            pass
        P.finish()
        block = es.enter_context(nc.Block())
        P.emit(block)
    return nc, cst_np, taps


_CACHE = {}


def kernel(x_prompt, x_sample, state_wkv, state_shift, state_pool, norm1_g, w_in, shift_mu,
           pool_w, pool_scale, w0, w_lora_up, a0, a_lora_up, g_lora_up, k_k, k_a, r_k,
           ln_x_g, ln_x_b, w_out, norm2_g, w_up, w_down, norm_f_g):
    f = lambda a: np.ascontiguousarray(np.asarray(a, dtype=np.float32))
    if "nc" not in _CACHE:
        _CACHE["nc"] = build_program()
    nc, cst_np, taps = _CACHE["nc"]
    col = lambda v, n: f(v).reshape(n, 128).T
    pcol = np.zeros((128, 64), np.float32)
    pcol[:, 0:14] = col(shift_mu[0], 14)
    pcol[:, 14:18] = col(w0[0], 4)
    pcol[:, 18:22] = col(a0[0], 4)
    pcol[:, 22:26] = col(k_k[0], 4)
    pcol[:, 26:30] = col(k_a[0], 4)
    pcol[:, 30:34] = col(f(r_k[0]).reshape(512), 4)
    pcol[:, 34:38] = col(pool_scale[0], 4)
    pcol[:, 38:46] = col(norm1_g[0], 8)
    pcol[:, 46:54] = col(norm2_g[0], 8)
    rows = np.zeros((3, 1024), np.float32)
    rows[0, 0:512] = f(ln_x_g[0]); rows[1, 0:512] = f(ln_x_b[0]); rows[2, :] = f(norm_f_g)
    bh = np.zeros((128, 192), np.float32)
    bh[:, 0:64] = np.tile(f(ln_x_g[0]).reshape(8, 64), (NS, 1))
    bh[:, 64:128] = np.tile(f(ln_x_b[0]).reshape(8, 64), (NS, 1))
    bh[:, 128:192] = np.tile(f(r_k[0]).reshape(8, 64), (NS, 1))
    lora12 = np.concatenate([f(w_lora_up[0]), f(a_lora_up[0])], axis=0)
    shared = {"w_in": f(w_in[0]), "w_out": f(w_out[0]), "w_up": f(w_up[0]), "w_down": f(w_down[0]),
              "pool_w": f(pool_w[0]), "lora12": lora12, "g_lora_up": f(g_lora_up[0]),
              "pcol": pcol, "rows": rows, "bhrows": bh, "cst": cst_np}
    xp, xs = f(x_prompt), f(x_sample)
    swkv, ssh, spl = f(state_wkv[0]), f(state_shift[0]), f(state_pool[0])
    in_maps = []
    for i in range(NCORES):
        b = slice(i * NS, (i + 1) * NS)
        m = dict(shared)
        m["x"] = xp[i]
        m["xs"] = xs[b, 0, :]
        m["swkv"] = swkv[b].reshape(NS * 8, 4096)
        m["sshift"] = ssh[b, 0, :]
        m["spool"] = spl[b].reshape(NS * 15, 512)
        in_maps.append(m)
    res = run_bass_kernel_spmd(nc, in_maps, core_ids=list(range(NCORES)))
    R = res.results
    _CACHE["last"] = R
    y_prompt = np.stack([R[i]["y"] for i in range(NCORES)], axis=0)
    y_sample = np.concatenate([R[i]["ys"] for i in range(NCORES)], axis=0)[:, None, :]
    wkv_p = np.stack([R[i]["wkv_p"] for i in range(NCORES)], axis=0)[None]
    sh_p = np.stack([R[i]["shift_p"].reshape(1, SHIFT_W) for i in range(NCORES)], axis=0)[None]
    pl_p = np.stack([R[i]["pool_p"][1:16] for i in range(NCORES)], axis=0)[None]
    wkv_s = np.concatenate([R[i]["wkv_s"].reshape(NS, 8, 64, 64) for i in range(NCORES)], axis=0)[None]
    sh_s = np.concatenate([R[i]["shift_s"] for i in range(NCORES)], axis=0)[:, None, :][None]
    pl_s = np.concatenate([R[i]["pool_s"] for i in range(NCORES)], axis=0)[None]
    out = (y_prompt, y_sample, wkv_p, sh_p, pl_p, wkv_s, sh_s, pl_s)
    return tuple(np.ascontiguousarray(o.astype(np.float32)) for o in out)
```

```python
import numpy as np
from contextlib import ExitStack
import concourse.bass as bass
import concourse.mybir as mybir
from concourse.bass_utils import run_bass_kernel_spmd

F32, BF16 = mybir.dt.float32, mybir.dt.bfloat16
AF = mybir.ActivationFunctionType
ALU = mybir.AluOpType
AX = mybir.AxisListType

NCORES = 8
D = 1024
T = 2048
NTILE = T // 128
NS = 16
IN_W = 2304
SHIFT_W = 1792
DFF = 4096
CW = -float(np.exp(-0.5))
RMS_EPS = 1e-6
GN_EPS = 64e-5
ST = 256
DEBUG_TAPS = False
KSTOP = 99.0
KVAR = 0


class _Stop(Exception):
    pass


class _Rec:
    def __getattr__(self, name):
        return lambda *a, **k: (name, a, k)


_REC = _Rec()


class Prog:
    def __init__(self, nc, es):
        self.nc, self.es = nc, es
        self.streams = {k: [] for k in ("pe", "dve", "act", "pool", "sp")}
        self.csem = {k: es.enter_context(nc.semaphore("c_" + k)) for k in ("pe", "dve", "act", "pool")}
        self.cnt = {k: 0 for k in self.csem}
        self.dsem, self.dcnt = {}, {}
        self.seen = {k: {} for k in self.streams}
        self.reg = {}
        self.pend = {k: [] for k in self.streams}
        self.alias = {}
        self.vc = {k: {} for k in self.streams}
        self.hist = {}

    def _exp(self, keys):
        out = []
        for k in keys:
            out.extend(self.alias.get(k, [k]))
        return out

    def _need(self, eng, key, val):
        if key in self.dcnt:
            val = self.dcnt[key]
        else:
            if eng == "pe" and key == "pe":
                return
            assert val <= self.cnt[key], (eng, key, val, self.cnt[key])
        vc = self.vc[eng]
        if vc.get(key, 0) >= val:
            return
        for k2, v2 in self.hist.get((key, val), {key: val}).items():
            if vc.get(k2, 0) < v2:
                vc[k2] = v2
        sem = self.csem[key] if key in self.csem else self.dsem[key]
        self.pend[eng].append((sem, val))

    def _flush(self, eng, keep_last):
        p = self.pend[eng]
        last = p.pop() if (keep_last and p) else None
        for sem, val in p:
            self.streams[eng].append(lambda e, sem=sem, val=val: e.wait_ge(sem, val))
        self.pend[eng] = []
        return last

    def _deps(self, eng, reads, writes):
        for r in reads:
            st = self.reg.get(r)
            if st and st["w"]:
                self._need(eng, *st["w"])
        for w in writes:
            st = self.reg.get(w)
            if st:
                if st["w"]:
                    self._need(eng, *st["w"])
                for k, v in st["r"].items():
                    self._need(eng, k, v)

    def _mark(self, ev, reads, writes):
        for r in reads:
            st = self.reg.setdefault(r, {"w": None, "r": {}})
            st["r"][ev[0]] = max(st["r"].get(ev[0], 0), ev[1])
        for w in writes:
            self.reg[w] = {"w": ev, "r": {}}

    def op(self, eng, fn, reads=(), writes=(), inc=True):
        name, a, k = fn(_REC)
        reads, writes = self._exp(reads), self._exp(writes)
        self._deps(eng, reads, writes)
        w = self._flush(eng, True)

        def emit(e, name=name, a=a, k=k, w=w, sem=(self.csem[eng] if inc else None)):
            ins = getattr(e, name)(*a, **k)
            if w is not None:
                ins = ins._wait_ge(w[0], w[1])
            if sem is not None:
                ins.then_inc(sem, 1)
        if inc:
            self.cnt[eng] += 1
            ev = (eng, self.cnt[eng])
            snap = dict(self.vc[eng])
            snap[eng] = self.cnt[eng]
            self.hist[ev] = snap
            self.vc[eng][eng] = self.cnt[eng] if eng == "pe" else self.vc[eng].get(eng, 0)
        else:
            ev = (eng, self.cnt[eng] + 1)
        self.streams[eng].append(emit)
        self._mark(ev, reads, writes)

    def dma(self, out, in_, reads=(), writes=(), sem="d0", q="sp", chain=False, **kw):
        if sem not in self.dsem:
            self.dsem[sem] = self.es.enter_context(self.nc.semaphore("d_" + sem))
            self.dcnt[sem] = 0
        if not chain and self.dcnt[sem] > 0:
            self._need(q, sem, self.dcnt[sem])
        reads, writes = self._exp(reads), self._exp(writes)
        self._deps(q, reads, writes)
        w = self._flush(q, True)
        self.dcnt[sem] += 16
        ev = (sem, self.dcnt[sem])
        snap = dict(self.vc[q])
        snap[sem] = self.dcnt[sem]
        self.hist[ev] = snap
        s = self.dsem[sem]

        def emit(e, out=out, in_=in_, s=s, kw=kw, w=w):
            ins = e.dma_start(out=out, in_=in_, **kw)
            if w is not None:
                ins = ins._wait_ge(w[0], w[1])
            ins.then_inc(s, 16)
        self.streams[q].append(emit)
        self._mark(ev, reads, writes)

    def barrier(self):
        for e in self.streams:
            for k in self.csem:
                if self.cnt[k] > 0:
                    self._need(e, k, self.cnt[k])
            for k in self.dcnt:
                if self.dcnt[k] > 0:
                    self._need(e, k, self.dcnt[k])
            self._flush(e, False)

    def finish(self):
        for k in self.csem:
            if self.cnt[k] > 0:
                self._need("sp", k, self.cnt[k])
        for k in self.dcnt:
            self._need("sp", k, self.dcnt[k])
        for e in self.streams:
            self._flush(e, False)

    def emit(self, block):
        S = self.streams

        @block.sync
        def _(e):
            for f in S["sp"]:
                f(e)

        @block.tensor
        def _(e):
            for f in S["pe"]:
                f(e)

        @block.vector
        def _(e):
            for f in S["dve"]:
                f(e)

        @block.scalar
        def _(e):
            for f in S["act"]:
                f(e)

        @block.gpsimd
        def _(e):
            for f in S["pool"]:
                f(e)


class Mem:
    BASE, LIMIT = 16512, 229344

    def __init__(self, nc):
        self.nc, self.off, self.n = nc, self.BASE, 0

    def alloc(self, name, shape, dtype):
        nb = 2 if dtype == BF16 else 4
        size = int(np.prod(shape[1:])) * nb
        size = (size + 63) // 64 * 64
        assert self.off + size <= self.LIMIT, ("SBUF overflow", name, self.off, size)
        self.n += 1
        t = self.nc.alloc_sbuf_tensor_at(f"{name}_{self.n}", list(shape), dtype, offset=self.off)
        self.off += size
        return t

    def mark(self):
        return self.off

    def release(self, m):
        self.off = m


def _make_consts():
    c = {}
    i = np.arange(128)
    c["ident"] = np.eye(128, dtype=np.float32)
    c["m_su"] = (i[:, None] < i[None, :]).astype(np.float32)
    c["m_iu"] = (i[:, None] <= i[None, :]).astype(np.float32)
    c["m_sl"] = (i[:, None] > i[None, :]).astype(np.float32)
    c["hones"] = ((i[:, None] // 64) == (i[None, :] // 64)).astype(np.float32)
    rst = np.ones((128, ST), np.float32)
    rst[:, ::128] = 0.0
    c["restart"] = rst
    wins = (2, 4, 8, 16)
    band = np.zeros((4, 128, 128), np.float32)
    bprev = np.zeros((4, 128, 128), np.float32)
    bfirst = np.zeros((4, 128, 128), np.float64)
    for g, w in enumerate(wins):
        for t in range(128):
            for s in range(t - w + 1, t + 1):
                if s >= 0:
                    band[g, s, t] += 1.0 / w
                    bfirst[g, s, t] += 1.0 / min(w, t + 1)
                else:
                    bprev[g, 128 + s, t] += 1.0 / w
            band[g, t, t] -= 1.0
            bfirst[g, t, t] -= 1.0
    c["band"] = np.concatenate(list(band), axis=1)
    c["bprev"] = np.concatenate(list(bprev), axis=1)
    import ml_dtypes
    hi = bfirst.astype(np.float32).astype(ml_dtypes.bfloat16).astype(np.float32)
    lo = (bfirst - hi).astype(np.float32)
    c["bfhi"] = np.concatenate(list(hi), axis=1)
    c["bflo"] = np.concatenate(list(lo), axis=1)
    sel = np.zeros((128, 2, 4, 16), np.float32)
    for tl in range(2):
        for bl in range(8):
            for j in range(15):
                for g, w in enumerate(wins):
                    if j >= 16 - w:
                        sel[bl * 15 + j, tl, g, tl * 8 + bl] = 1.0 / w
    c["sel"] = sel.reshape(128, 128)
    ind = np.zeros((128, 4, 8), np.float32)
    for p in range(128):
        for hp in range(4):
            ind[p, hp, 2 * hp + p // 64] = 1.0
    c["ind"] = ind.reshape(128, 32)
    return c


_CONST_ORDER = ["ident", "m_su", "m_iu", "m_sl", "hones", "restart", "band", "bprev", "bfhi", "bflo", "sel", "ind"]


def _pack_consts():
    c = _make_consts()
    offs, cols, o = {}, [], 0
    for k in _CONST_ORDER:
        offs[k] = (o, c[k].shape[1])
        o += c[k].shape[1]
        cols.append(c[k])
    return np.concatenate(cols, axis=1).astype(np.float32), offs


def build_program():
    cst_np, coff = _pack_consts()
    NCST = cst_np.shape[1]
    nc = bass.Bass("TRN2", target_bir_lowering=False)
    dram = lambda n, s, k="ExternalInput": nc.dram_tensor(n, list(s), F32, kind=k).ap()
    x_d = dram("x", [T, D]); xs_d = dram("xs", [NS, D])
    swkv_d = dram("swkv", [128, 4096]); sshift_d = dram("sshift", [NS, SHIFT_W]); spool_d = dram("spool", [NS * 15, 512])
    w_in_d = dram("w_in", [D, IN_W]); w_out_d = dram("w_out", [D, D]); w_up_d = dram("w_up", [D, DFF]); w_dn_d = dram("w_down", [DFF, D])
    poolw_d = dram("pool_w", [4, 128, 128]); lora12_d = dram("lora12", [128, 512]); glora_d = dram("g_lora_up", [128, 512])
    pcol_d = dram("pcol", [128, 64]); rows_d = dram("rows", [3, 1024]); bh_d = dram("bhrows", [128, 192])
    cst_d = dram("cst", [128, NCST])
    y_d = dram("y", [T, D], "ExternalOutput"); ys_d = dram("ys", [NS, D], "ExternalOutput")
    wkvp_d = dram("wkv_p", [8, 64, 64], "ExternalOutput"); shp_d = dram("shift_p", [14, 128], "ExternalOutput")
    plp_d = dram("pool_p", [16, 512], "ExternalOutput")
    wkvs_d = dram("wkv_s", [128, 4096], "ExternalOutput"); shs_d = dram("shift_s", [NS, SHIFT_W], "ExternalOutput")
    pls_d = dram("pool_s", [NS, 15, 512], "ExternalOutput")
    scr_d = dram("scr", [8, NS, 512], "Internal")
    taps = {}

    es = ExitStack()
    with es:
        P = Prog(nc, es)
        M = Mem(nc)
        PS = [es.enter_context(nc.psum_tensor(f"ps{i}", [128, 512], F32)) for i in range(8)]
        psn = [0]

        CX = {"banks": list(range(8)), "pn": "all"}
        pcount = {}

        def bank():
            lst = CX["banks"]
            n = pcount.get(CX["pn"], 0)
            pcount[CX["pn"]] = n + 1
            i = lst[n % len(lst)]
            return PS[i], ("ps", i)

        def pe_warm(nmm):
            pw, pwk = bank()
            for _ in range(nmm):
                P.op("pe", lambda e: e.matmul(pw[:, :], lhsT=WARM[0], rhs=WARM[1], start=True, stop=True), reads=["CB"], writes=[pwk], inc=False)

        WARM = [None, None]

        def pe_warm_small(nmm, pw, pwk):
            for _ in range(nmm):
                P.op("pe", lambda e: e.matmul(pw[:, 0:128], lhsT=WARM[0], rhs=WARM[1][:, 0:128], start=True, stop=True), reads=["CB"], writes=[pwk], inc=False)

        def tap(name, ap, shape, reads):
            if not DEBUG_TAPS:
                return
            t = nc.dram_tensor("tap_" + name, list(shape), ap.dtype, kind="ExternalOutput").ap()
            taps[name] = t
            P.dma(t, ap, reads=reads, sem="tap")

        X1 = M.alloc("X1", [128, NTILE, D], F32)
        X1s = M.alloc("X1s", [128, D], F32)
        CB = M.alloc("CB", [128, NCST], BF16)
        identf = M.alloc("identf", [128, 128], F32)
        honesf = M.alloc("honesf", [128, 128], F32)
        restart = M.alloc("restart", [128, ST], F32)
        pcol = M.alloc("pcol", [128, 64], F32)
        pder = M.alloc("pder", [128, 32], F32)
        rs2 = M.alloc("rs2", [128, NTILE + 1], F32)
        cb = lambda k: CB[:, coff[k][0]:coff[k][0] + coff[k][1]]
        identb = cb("ident")
        WARM[0], WARM[1] = identb, CB[:, 0:512]
        MU = lambda j: pcol[:, j:j + 1]
        OMMU = lambda j: pder[:, j:j + 1]
        W0 = lambda hp: pcol[:, 14 + hp:15 + hp]
        A0 = lambda hp: pcol[:, 18 + hp:19 + hp]
        KKc = lambda hp: pcol[:, 22 + hp:23 + hp]
        KAc = lambda hp: pcol[:, 26 + hp:27 + hp]
        OMKA = lambda hp: pder[:, 14 + hp:15 + hp]
        RKc = lambda hp: pcol[:, 30 + hp:31 + hp]
        PSC = lambda g: pcol[:, 34 + g:35 + g]
        G1 = lambda c: pcol[:, 38 + c:39 + c]
        G2 = lambda c: pcol[:, 46 + c:47 + c]

        try:
            P.dma(pcol[:], pcol_d[:, :], writes=["pcol"], sem="par")
            P.dma(X1s[0:NS, :], xs_d[:, :], writes=["x1s"], sem="xs")

            m0 = M.mark()
            stg = [M.alloc(f"stg{i}", [128, IN_W], F32) for i in range(2)]
            half = NCST // 2 + 1
            for i, (a, b) in enumerate([(0, min(IN_W, NCST)), (min(IN_W, NCST), NCST)]):
                if b <= a:
                    continue
                P.dma(stg[i][:, 0:b - a], cst_d[:, a:b], writes=[("stg", i)], sem="cst")
                P.op("dve", lambda e, i=i, a=a, b=b: e.tensor_copy(out=CB[:, a:b], in_=stg[i][:, 0:b - a]),
                     reads=[("stg", i)], writes=["CB"])
            assert NCST <= 2 * IN_W
            o, n = coff["ident"]
            P.dma(identf[:], cst_d[:, o:o + n], writes=["identf"], sem="cst")
            o, n = coff["hones"]
            P.dma(honesf[:], cst_d[:, o:o + n], writes=["honesf"], sem="cst")
            o, n = coff["restart"]
            P.dma(restart[:], cst_d[:, o:o + n], writes=["restart"], sem="cst")
            P.op("dve", lambda e: e.tensor_scalar(out=pder[:, 0:14], in0=pcol[:, 0:14], scalar1=-1.0, scalar2=1.0, op0=ALU.mult, op1=ALU.add),
                 reads=["pcol"], writes=["pder"])
            P.op("dve", lambda e: e.tensor_scalar(out=pder[:, 14:18], in0=pcol[:, 26:30], scalar1=-1.0, scalar2=1.0, op0=ALU.mult, op1=ALU.add),
                 reads=["pcol"], writes=["pder"])
            M.release(m0)

            mA = M.mark()
            WIN = M.alloc("WIN", [128, 8, IN_W], BF16)
            WOUT = M.alloc("WOUT", [128, 8, D], BF16)
            POOLW = M.alloc("POOLW", [128, 4, 128], BF16)
            L12 = M.alloc("L12", [128, 512], BF16)
            GLO = M.alloc("GLO", [128, 512], BF16)
            RKI = M.alloc("RKI", [128, 4, 8], BF16)
            LNG = M.alloc("LNG", [128, 512], BF16); LNB = M.alloc("LNB", [128, 512], BF16)
            m1 = M.mark()
            stg = [M.alloc(f"stgw{i}", [128, IN_W], F32) for i in range(2)]
            for c in range(8):
                s = c % 2
                P.dma(stg[s][:, :], w_in_d[c * 128:(c + 1) * 128, :], writes=[("stg", s)], sem=f"wst{s}")
                P.op("dve",
                     lambda e, c=c, s=s: e.tensor_scalar(out=WIN[:, c, :], in0=stg[s][:, :], scalar1=G1(c), scalar2=None, op0=ALU.mult),
                     reads=[("stg", s), "pcol"], writes=["WIN"])
            for c in range(8):
                s = c % 2
                P.dma(stg[s][:, 0:D], w_out_d[c * 128:(c + 1) * 128, :], writes=[("stg", s)], sem=f"wst{s}")
                P.op("dve", lambda e, c=c, s=s: e.tensor_copy(out=WOUT[:, c, :], in_=stg[s][:, 0:D]),
                     reads=[("stg", s)], writes=["WOUT"])
            small = [(POOLW[:].rearrange("p g d -> p (g d)"), None, "POOLW"), (L12[:], lora12_d[:, :], "L12"), (GLO[:], glora_d[:, :], "GLO")]
            for i, (dst, src, key) in enumerate(small):
                s = i % 2
                if key == "POOLW":
                    P.dma(stg[s][:, 0:512].rearrange("p (g d) -> p g d", g=4), poolw_d.rearrange("g c d -> c g d"), writes=[("stg", s)], sem=f"wst{s}")
                else:
                    P.dma(stg[s][:, 0:512], src, writes=[("stg", s)], sem=f"wst{s}")
                P.op("dve", lambda e, dst=dst, s=s: e.tensor_copy(out=dst, in_=stg[s][:, 0:512]), reads=[("stg", s)], writes=[key])
            for i, (dst, key) in enumerate([(LNG, "LNG"), (LNB, "LNB")]):
                P.dma(stg[i][:, 0:512], rows_d[i, 0:512].partition_broadcast(128), writes=[("stg", i)], sem=f"wst{i}")
                P.op("dve", lambda e: e.tensor_copy(out=dst[:], in_=stg[i][:, 0:512]), reads=[("stg", i)], writes=[key])
            for hp in range(4):
                o, n = coff["ind"]
                P.op("dve", lambda e, hp=hp, o=o: e.tensor_scalar(out=RKI[:, hp, :], in0=CB[:, o + hp * 8:o + hp * 8 + 8], scalar1=RKc(hp), scalar2=None, op0=ALU.mult),
                     reads=["CB", "pcol"], writes=["RKI"])
            P.barrier()
            M.release(m1)
            for ti in range(NTILE):
                P.dma(X1[:, ti, :], x_d[ti * 128:(ti + 1) * 128, :], writes=[("x1", ti)], sem=f"x{ti // 4}", chain=True)
            if KSTOP <= 1:
                raise _Stop()

            hTs = [M.alloc("hT0", [128, 8, ST + 1], BF16)] * 2
            sgzbs = [M.alloc("sgzb0", [128, ST], BF16)] * 2
            mixPs = [M.alloc("mixP0", [128, 4, ST], BF16)] * 2
            CX.update(hT=hTs[0], khT=("hT", 0), sgzb=sgzbs[0], ksg=("sgzb", 0), mixdst=None, mixsrc=None)
            hnb = M.alloc("hnb", [128, D], BF16)
            junk = hnb
            stat = M.alloc("stat", [128, 8], F32)
            dTb4 = M.alloc("dTb", [128, 4, ST], BF16)
            mixT = M.alloc("mixT", [128, 8, ST], BF16)
            plast = M.alloc("plast", [128, 16], F32)
            E1 = M.alloc("E1", [128, ST], F32); E3 = M.alloc("E3", [128, ST], F32); Z12 = E1; Z13 = E3
            z12b = M.alloc("z12b", [128, ST], BF16)
            RKV = [[M.alloc(f"{nm}{i}", [128, ST], F32) for nm in ("Rb", "Kb", "Vb")] for i in range(1)] * 2
            SG4 = M.alloc("SG4", [128, 4, ST], F32); AS4 = M.alloc("AS4", [128, 4, ST], F32)
            KK2 = M.alloc("KK2", [128, ST], F32); RN = M.alloc("RN", [128, ST], F32); tmpS = RN
            KKN = M.alloc("KKN", [128, ST], F32); Bb = M.alloc("Bb", [128, ST], F32)
            KF = M.alloc("KF", [128, ST], F32)
            Mf = M.alloc("Mf", [128, 4, 64], F32); Mb = M.alloc("Mb", [128, 4, 2, 64], BF16); Mt = M.alloc("Mt", [128, 4, 64], F32)
            Yw = M.alloc("Yw", [128, 512], F32); Yq = M.alloc("Yq", [128, 512], F32); Yo = M.alloc("Yo", [128, 512], BF16)
            gst = M.alloc("gst", [128, 40], F32)
            P.alias["mixT"] = [("mixT", c) for c in range(8)]
            mS = M.mark()
            SMP = M.alloc("SMP", [128, 6, 4, NS], F32)
            SQ = 8
            S0qs = [M.alloc(f"S0q{i}", [128, SQ, 64], F32) for i in range(2)]; S1qs = [M.alloc("S1q0", [128, SQ, 64], F32)] * 2
            Stq = M.alloc("Stq", [128, SQ, 64], F32)
            BH = M.alloc("BH", [128, 6, 64], F32)
            BHR = M.alloc("BHR", [128, 192], F32)
            sa_s = M.alloc("sa_s", [128, 64], F32); y_s = M.alloc("y_s", [128, 64], F32); y_s2 = M.alloc("y_s2", [128, 64], F32)
            spl = M.alloc("spl", [128, 2, 512], BF16)
            shT = M.alloc("shT", [128, 14, NS], F32); shtok = M.alloc("shtok", [128, SHIFT_W], F32)
            ptok = M.alloc("ptok", [128, SHIFT_W], F32)
            ytok = M.alloc("ytok", [128, 512], F32)
            Stq2 = ytok[:, :].rearrange("p (v k) -> p v k", k=64)
            splf = ytok
            stgp = shtok[:, 0:1024].rearrange("p (a c) -> p a c", a=2)

            o_su, _ = coff["m_su"]; o_iu, _ = coff["m_iu"]; o_sl, _ = coff["m_sl"]
            mask_ai = CB[:, o_su:o_su + 256].rearrange("p (a t) -> p a t", a=2)
            mask_sl = CB[:, o_sl:o_sl + 128]

            def rms_stats(eng_in, key, npart, col):
                P.op("act", lambda e: e.activation(out=junk[0:npart, :], in_=eng_in, func=AF.Square, accum_out=stat[0:npart, col:col + 1]),
                     reads=[key], writes=["hnb", ("stat", col)])

            def norm_to_hT(xin, key, npart, c0, ncols_total):
                if npart == 128:
                    pe_warm(6)
                rms_stats(xin, key, npart, 0)
                P.op("act", lambda e: e.activation(out=stat[0:npart, 1:2], in_=stat[0:npart, 0:1], func=AF.Ln, scale=1.0 / D, bias=stat[0:npart, 7:8]),
                     reads=[("stat", 0), ("stat", 7)], writes=[("stat", 1)])
                P.op("act", lambda e: e.activation(out=stat[0:npart, 2:3], in_=stat[0:npart, 1:2], func=AF.Exp, scale=-0.5), reads=[("stat", 1)], writes=[("stat", 2)])
                P.op("act", lambda e: e.activation(out=hnb[0:npart, :], in_=xin, func=AF.Copy, scale=stat[0:npart, 2:3]),
                     reads=[key, ("stat", 2)], writes=["hnb"])
                pb, pk = bank()
                pbb = pb[:].bitcast(BF16)
                for c in range(8):
                    P.op("pe", lambda e, c=c: e.transpose(out=pbb[:, c * 128:c * 128 + npart], in_=hnb[0:npart, c * 128:(c + 1) * 128], identity=identb[0:npart, 0:npart]),
                         reads=["hnb", "CB"], writes=[pk], inc=(c == 7))
                P.op("dve", lambda e: e.tensor_copy(out=CX["hT"][:, :, c0:c0 + npart], in_=pbb.rearrange("p (c t) -> p c t", c=8)[:, :, 0:npart]),
                     reads=[pk], writes=[CX["khT"]])

            def proj_fm(j, ncols):
                pb, pk = bank()
                for c in range(8):
                    P.op("pe", lambda e, c=c: e.matmul(pb[:, 0:ncols], lhsT=WIN[:, c, 512 + j * 128:512 + (j + 1) * 128], rhs=CX["hT"][:, c, 0:ncols], start=(c == 0), stop=(c == 7)),
                         reads=["WIN", CX["khT"]], writes=[pk], inc=(c == 7))
                return pb, pk

            def shift_evac(pb, pk, j, out, okey, ncols, prevT=None):
                P.op("act", lambda e: e.activation(out=tmpS[:, 0:ncols], in_=pb[:, 0:ncols], func=AF.Copy, scale=OMMU(j)),
                     reads=[pk, "pder"], writes=["RN"])
                if prevT is None:
                    raise AssertionError("prompt path uses shift_evac_p")
                else:
                    P.op("dve", lambda e: e.scalar_tensor_tensor(out=out[:, 0:ncols], in0=prevT, scalar=MU(j), in1=tmpS[:, 0:ncols], op0=ALU.mult, op1=ALU.add),
                         reads=["shT", "RN", "pcol"], writes=[okey])

            sh_ctr = [0]

            def shift_evac_p(pb, pk, j, out, okey, last):
                tb, tkey = ((RN, "RN"), (KKN, "KKN"))[sh_ctr[0] % 2]
                sh_ctr[0] += 1
                P.op("act", lambda e: e.activation(out=tb[:, 0:ST], in_=pb[:, 1:ST + 1], func=AF.Copy, scale=OMMU(j)),
                     reads=[pk, "pder"], writes=[tkey])
                P.op("dve", lambda e: e.scalar_tensor_tensor(out=out[:, 0:ST], in0=pb[:, 0:ST], scalar=MU(j), in1=tb[:, 0:ST], op0=ALU.mult, op1=ALU.add),
                     reads=[pk, tkey, "pcol"], writes=[okey])
                if last:
                    P.op("dve", lambda e: e.tensor_copy(out=plast[:, j:j + 1], in_=pb[:, ST:ST + 1]), reads=[pk], writes=[("plast", j)])

            def lora_stage(ncols):
                P.op("act", lambda e: e.activation(out=z12b[0:64, 0:ncols], in_=Z12[0:64, 0:ncols], func=AF.Tanh), reads=["E1"], writes=["z12b"])
                P.op("dve", lambda e: e.tensor_copy(out=z12b[64:128, 0:ncols], in_=Z12[64:128, 0:ncols]), reads=["E1"], writes=["z12b"])
                P.op("act", lambda e: e.activation(out=CX["sgzb"][:, 0:ncols], in_=Z13[:, 0:ncols], func=AF.Sigmoid), reads=["E3"], writes=[CX["ksg"]])

            def sigm_stage(ncols):
                n = ncols
                for hp in range(4):
                    pb, pk = bank()
                    P.op("pe", lambda e: e.matmul(pb[:, 0:n], lhsT=L12[0:64, hp * 128:(hp + 1) * 128], rhs=z12b[0:64, 0:n], start=True, stop=True),
                         reads=["L12", "z12b"], writes=[pk])
                    pb2, pk2 = bank()
                    P.op("pe", lambda e: e.matmul(pb2[:, 0:n], lhsT=L12[64:128, hp * 128:(hp + 1) * 128], rhs=z12b[64:128, 0:n], start=True, stop=True),
                         reads=["L12", "z12b"], writes=[pk2])
                    P.op("act", lambda e: e.activation(out=SG4[:, hp, 0:n], in_=pb[:, 0:n], func=AF.Sigmoid, bias=W0(hp)), reads=[pk, "pcol"], writes=[("SG", hp)])
                    P.op("act", lambda e: e.activation(out=AS4[:, hp, 0:n], in_=pb2[:, 0:n], func=AF.Sigmoid, bias=A0(hp)), reads=[pk2, "pcol"], writes=[("AS", hp)])

            def preprocess(hp, ncols, sample):
                n = ncols
                Rb, Kb, Vb = RKV[hp % 2]
                kR, kK, kV = ("Rb", 0), ("Kb", 0), ("Vb", 0)
                SG, AS = SG4[:, hp, :], AS4[:, hp, :]
                P.op("act", lambda e: e.activation(out=KK2[:, 0:n], in_=Kb[:, 0:n], func=AF.Square, scale=KKc(hp)), reads=[kK, "pcol"], writes=["KK2"])
                pb3, pk3 = bank()
                P.op("pe", lambda e: e.matmul(pb3[:, 0:n], lhsT=honesf[:], rhs=KK2[:, 0:n], start=True, stop=True), reads=["honesf", "KK2"], writes=[pk3])
                if not sample:
                    pw, pwk = bank()
                    for _ in range(14):
                        P.op("pe", lambda e: e.matmul(pw[:, :], lhsT=identb, rhs=CB[:, 0:512], start=True, stop=True), reads=["CB"], writes=[pwk], inc=False)
                P.op("act", lambda e: e.activation(out=RN[:, 0:n], in_=pb3[:, 0:n], func=AF.Ln, bias=stat[:, 6:7]), reads=[pk3, ("stat", 6)], writes=["RN"])
                P.op("act", lambda e: e.activation(out=RN[:, 0:n], in_=RN[:, 0:n], func=AF.Exp, scale=-0.5), reads=["RN"], writes=["RN"])
                P.op("dve", lambda e: e.scalar_tensor_tensor(out=KKN[:, 0:n], in0=Kb[:, 0:n], scalar=KKc(hp), in1=RN[:, 0:n], op0=ALU.mult, op1=ALU.mult), reads=[kK, "RN", "pcol"], writes=["KKN"])
                P.op("dve", lambda e: e.tensor_tensor(out=Bb[:, 0:n], in0=KKN[:, 0:n], in1=AS[:, 0:n], op=ALU.mult), reads=["KKN", ("AS", hp)], writes=["Bb"])
                P.op("dve", lambda e: e.tensor_scalar(out=KK2[:, 0:n], in0=AS[:, 0:n], scalar1=KAc(hp), scalar2=OMKA(hp), op0=ALU.mult, op1=ALU.add),
                     reads=[("AS", hp), "pcol", "pder"], writes=["KK2"])
                P.op("dve", lambda e: e.tensor_tensor(out=KF[:, 0:n], in0=Kb[:, 0:n], in1=KK2[:, 0:n], op=ALU.mult), reads=[kK, "KK2"], writes=["KF"])
                if sample:
                    P.op("act", lambda e: e.activation(out=SMP[:, 0, hp, :], in_=Rb[:, 0:n], func=AF.Copy), reads=[kR], writes=["SMP"])
                    P.op("act", lambda e: e.activation(out=SMP[:, 1, hp, :], in_=SG[:, 0:n], func=AF.Exp, scale=CW), reads=[("SG", hp)], writes=["SMP"])
                    P.op("dve", lambda e: e.tensor_copy(out=SMP[:, 2, hp, :], in_=KF[:, 0:n]), reads=["KF"], writes=["SMP"])
                    P.op("act", lambda e: e.activation(out=SMP[:, 3, hp, :], in_=Vb[:, 0:n], func=AF.Copy), reads=[kV], writes=["SMP"])
                    P.op("dve", lambda e: e.tensor_scalar(out=SMP[:, 4, hp, :], in0=KKN[:, 0:n], scalar1=-1.0, scalar2=None, op0=ALU.mult), reads=["KKN"], writes=["SMP"])
                    P.op("dve", lambda e: e.tensor_copy(out=SMP[:, 5, hp, :], in_=Bb[:, 0:n]), reads=["Bb"], writes=["SMP"])
                    return
                P.op("pool", lambda e: e.tensor_tensor(out=RKB[:, hp, 0:n], in0=Rb[:, 0:n], in1=KF[:, 0:n], op=ALU.mult), reads=[kR, "KF"], writes=["RKB"])
                P.op("pool", lambda e: e.tensor_copy(out=VBF[:, hp, 0:n], in_=Vb[:, 0:n]), reads=[kV], writes=["VBF"])
                P.op("dve", lambda e: e.tensor_tensor_scan(out=CUM[:, 0:n], data0=restart[:, 0:n], data1=SG[:, 0:n], initial=0.0, op0=ALU.mult, op1=ALU.add),
                     reads=["restart", ("SG", hp)], writes=["CUM"])
                P.op("act", lambda e: e.activation(out=E1[:, 0:n], in_=CUM[:, 0:n], func=AF.Exp, scale=CW), reads=["CUM"], writes=["E1"])
                P.op("dve", lambda e: e.tensor_tensor(out=ART[:, hp, 1, 0:n], in0=Rb[:, 0:n], in1=E1[:, 0:n], op=ALU.mult), reads=[kR, "E1"], writes=["ART"])
                P.op("dve", lambda e: e.tensor_tensor(out=CM[:, 0:n], in0=CUM[:, 0:n], in1=SG[:, 0:n], op=ALU.subtract), reads=["CUM", ("SG", hp)], writes=["E2"])
                P.op("act", lambda e: e.activation(out=E2[:, 0:n], in_=CM[:, 0:n], func=AF.Exp, scale=CW), reads=["E2"], writes=["E2"])
                P.op("dve", lambda e: e.scalar_tensor_tensor(out=ART[:, hp, 0, 0:n], in0=KKN[:, 0:n], scalar=-1.0, in1=E2[:, 0:n], op0=ALU.mult, op1=ALU.mult),
                     reads=["KKN", "E2"], writes=["ART"])
                P.op("act", lambda e: e.activation(out=E3[:, 0:n], in_=CUM[:, 0:n], func=AF.Exp, scale=-CW), reads=["CUM"], writes=["E3"])
                P.op("dve", lambda e: e.tensor_tensor(out=BT[:, hp, 0:n], in0=Bb[:, 0:n], in1=E3[:, 0:n], op=ALU.mult), reads=["Bb", "E3"], writes=["BT"])
                P.op("pool", lambda e: e.tensor_tensor(out=KT[:, hp, 0:n], in0=KF[:, 0:n], in1=E3[:, 0:n], op=ALU.mult), reads=["KF", "E3"], writes=["KT"])
                P.op("act", lambda e: e.activation(out=WC[:, hp, :], in_=E1[:, 0:n].rearrange("p (c t) -> p c t", t=128)[:, :, 127], func=AF.Copy),
                     reads=["E1"], writes=["WC"])

            def out_proj(npart, c0, x1ap, x1key, ti_stat):
                for dh in range(2):
                    pb, pk = bank()
                    for c in range(8):
                        msrc, mkey = (mixT[:, c, c0:c0 + npart], ("mixT", c)) if (c >= 4 or CX["mixsrc"] is None) else CX["mixsrc"](c, c0, npart)
                        P.op("pe", lambda e, c=c, dh=dh, pb=pb: e.matmul(pb[0:npart, :], lhsT=msrc, rhs=WOUT[:, c, dh * 512:(dh + 1) * 512], start=(c == 0), stop=(c == 7)),
                             reads=[mkey, "WOUT"], writes=[pk], inc=(c == 7))
                    P.op("dve", lambda e, dh=dh, pb=pb: e.tensor_tensor(out=x1ap[:, dh * 512:(dh + 1) * 512], in0=pb[0:npart, :], in1=x1ap[:, dh * 512:(dh + 1) * 512], op=ALU.add),
                         reads=[pk, x1key], writes=[x1key])
                rms_stats(x1ap, x1key, npart, 3)
                P.op("dve", lambda e: e.tensor_scalar(out=stat[0:npart, 4:5], in0=stat[0:npart, 3:4], scalar1=1.0 / D, scalar2=RMS_EPS, op0=ALU.mult, op1=ALU.add),
                     reads=[("stat", 3)], writes=[("stat", 4)])
                P.op("dve", lambda e: e.reciprocal(out=rs2[0:npart, ti_stat:ti_stat + 1], in_=stat[0:npart, 4:5]), reads=[("stat", 4)], writes=["rs2"])

            def pool_mix_a(pd, pdk, g, ncols):
                P.op("act" if g % 2 else "dve", (lambda e: e.activation(out=dTb4[:, g, 0:ncols], in_=pd[:, 0:ncols], func=AF.Copy)) if g % 2 else (lambda e: e.tensor_copy(out=dTb4[:, g, 0:ncols], in_=pd[:, 0:ncols])),
                     reads=[pdk], writes=[("dTb", g)])

            def pool_mix_b(g, ncols):
                pb, pk = bank()
                P.op("pe", lambda e: e.matmul(pb[:, 0:ncols], lhsT=POOLW[:, g, :], rhs=dTb4[:, g, 0:ncols], start=True, stop=True), reads=["POOLW", ("dTb", g)], writes=[pk])
                mdst, mkey = (mixT[:, g, 0:ncols], ("mixT", g)) if CX["mixdst"] is None else CX["mixdst"](g)
                P.op("act", lambda e: e.activation(out=mdst, in_=pb[:, 0:ncols], func=AF.Copy, scale=PSC(g)), reads=[pk, "pcol"], writes=[mkey])

            def gn_stage(ysrc, ykey, npart, nh, eps_col):
                y3 = lambda ap: ap.rearrange("p (h v) -> p h v", v=64)
                P.op("dve", lambda e: e.tensor_reduce(out=gst[0:npart, 0:nh], in_=y3(ysrc), axis=AX.X, op=ALU.add), reads=[ykey], writes=["gst"])
                P.op("dve", lambda e: e.tensor_scalar(out=gst[0:npart, 8:8 + nh], in0=gst[0:npart, 0:nh], scalar1=1.0 / 64, scalar2=None, op0=ALU.mult), reads=["gst"], writes=["gst"])
                P.op("dve", lambda e: e.tensor_tensor(out=y3(Yw[0:npart, 0:nh * 64]), in0=y3(ysrc), in1=gst[0:npart, 8:8 + nh].unsqueeze(2).to_broadcast([npart, nh, 64]), op=ALU.subtract),
                     reads=[ykey, "gst"], writes=["Yw"])
                P.op("act", lambda e: e.activation(out=Yq[0:npart, 0:nh * 64], in_=Yw[0:npart, 0:nh * 64], func=AF.Square), reads=["Yw"], writes=["Yq"])
                P.op("dve", lambda e: e.tensor_reduce(out=gst[0:npart, 16:16 + nh], in_=y3(Yq[0:npart, 0:nh * 64]), axis=AX.X, op=ALU.add), reads=["Yq"], writes=["gst"])
                P.op("act", lambda e: e.activation(out=gst[0:npart, 24:24 + nh], in_=gst[0:npart, 16:16 + nh], func=AF.Ln, scale=1.0 / 64, bias=stat[0:npart, eps_col:eps_col + 1]),
                     reads=["gst", ("stat", eps_col)], writes=["gst"])
                P.op("act", lambda e: e.activation(out=gst[0:npart, 32:32 + nh], in_=gst[0:npart, 24:24 + nh], func=AF.Exp, scale=-0.5), reads=["gst"], writes=["gst"])
                P.op("dve", lambda e: e.tensor_tensor(out=y3(Yw[0:npart, 0:nh * 64]), in0=y3(Yw[0:npart, 0:nh * 64]), in1=gst[0:npart, 32:32 + nh].unsqueeze(2).to_broadcast([npart, nh, 64]), op=ALU.mult),
                     reads=["Yw", "gst"], writes=["Yw"])

            P.op("pool", lambda e: e.memset(stat[:, 7:8], RMS_EPS), writes=[("stat", 7)])
            P.op("pool", lambda e: e.memset(stat[:, 6:7], 1e-12), writes=[("stat", 6)])
            P.op("pool", lambda e: e.memset(stat[:, 5:6], GN_EPS), writes=[("stat", 5)])
            P.op("pool", lambda e: e.memset(plast[:], 0.0), writes=[("plast", j) for j in range(14)])
            P.op("pool", lambda e: e.memset(Mf[:], 0.0), writes=["Mf"])
            P.op("pool", lambda e: e.memset(Mb[:], 0.0), writes=["Mb"])

            n = NS
            norm_to_hT(X1s[0:n, :], "x1s", n, 0, n)
            for (c0, c1, dst) in [(0, 512, None), (512, 1024, 0), (1024, 1536, 512), (1536, 2048, 1024), (2048, 2304, 1536)]:
                pb, pk = bank()
                w = c1 - c0
                for c in range(8):
                    P.op("pe", lambda e, c=c, pb=pb, c0=c0, c1=c1, w=w: e.matmul(pb[0:n, 0:w], lhsT=CX["hT"][:, c, 0:n], rhs=WIN[:, c, c0:c1], start=(c == 0), stop=(c == 7)),
                         reads=["WIN", CX["khT"]], writes=[pk], inc=(c == 7))
                if dst is None:
                    P.op("act", lambda e, pb=pb: e.activation(out=splf[0:n, :], in_=pb[0:n, :], func=AF.Copy), reads=[pk], writes=["ytok"])
                else:
                    P.op("act", lambda e, pb=pb, dst=dst, w=w: e.activation(out=ptok[0:n, dst:dst + w], in_=pb[0:n, 0:w], func=AF.Copy), reads=[pk], writes=["ptok"])
            P.dma(shs_d[:, :], ptok[0:n, :], reads=["ptok"], sem="outs")
            P.dma(pls_d[:, 14, :], splf[0:n, :], reads=["ytok"], sem="outs")
            P.dma(pls_d[:, 0:14, :], spool_d.rearrange("(b j) c -> b j c", j=15)[:, 1:15, :], sem="outs")
            P.dma(shtok[0:n, :], sshift_d[:, :], writes=["shtok"], sem="sin")
            pbs = []
            for j in range(14):
                if j % 8 == 0:
                    pb, pk = bank()
                    pbs.append((pb, pk))
                P.op("pe", lambda e, j=j, pb=pb: e.transpose(out=pb[:, (j % 8) * n:(j % 8 + 1) * n], in_=shtok[0:n, j * 128:(j + 1) * 128], identity=identf[0:n, 0:n]),
                     reads=["shtok", "identf"], writes=[pk], inc=(j % 8 == 7 or j == 13))
            P.op("dve", lambda e: e.tensor_copy(out=shT[:, 0:8, :], in_=pbs[0][0][:, 0:8 * n].rearrange("p (j b) -> p j b", b=n)), reads=[pbs[0][1]], writes=["shT"])
            P.op("dve", lambda e: e.tensor_copy(out=shT[:, 8:14, :], in_=pbs[1][0][:, 0:6 * n].rearrange("p (j b) -> p j b", b=n)), reads=[pbs[1][1]], writes=["shT"])
            for tl in range(2):
                P.dma(stgp[0:120, tl, :], spool_d[tl * 120:(tl + 1) * 120, :], writes=["shtok"], sem="sin")
            P.op("dve", lambda e: e.tensor_copy(out=spl[0:120, :, :], in_=stgp[0:120, :, :]), reads=["shtok"], writes=["spl"])
            o_sel, _ = coff["sel"]
            wins = (2, 4, 8, 16)
            for g in range(4):
                pu, puk = bank()
                for c in range(8):
                    P.op("pe", lambda e, c=c, pu=pu, g=g: e.matmul(pu[:, 0:n], lhsT=WIN[:, c, g * 128:(g + 1) * 128], rhs=CX["hT"][:, c, 0:n], start=(c == 0), stop=(c == 7)),
                         reads=["WIN", CX["khT"]], writes=[puk], inc=(c == 7))
                P.op("act", lambda e, pu=pu: e.activation(out=tmpS[:, 0:n], in_=pu[:, 0:n], func=AF.Copy, scale=1.0 / wins[g] - 1.0), reads=[puk], writes=["RN"])
                pd, pdk = bank()
                for tl in range(2):
                    P.op("pe", lambda e, tl=tl, g=g, pd=pd: e.matmul(pd[:, 0:n], lhsT=spl[0:120, tl, g * 128:(g + 1) * 128], rhs=CB[0:120, o_sel + tl * 64 + g * 16:o_sel + tl * 64 + g * 16 + 16], start=(tl == 0), stop=(tl == 1)),
                         reads=["spl", "CB"], writes=[pdk], inc=(tl == 1))
                P.op("dve", lambda e, pd=pd: e.tensor_tensor(out=dTb4[:, 0, 0:n], in0=pd[:, 0:n], in1=tmpS[:, 0:n], op=ALU.add), reads=[pdk, "RN"], writes=[("dTb", 0)])
                pb, pk = bank()
                P.op("pe", lambda e, pb=pb, g=g: e.matmul(pb[:, 0:n], lhsT=POOLW[:, g, :], rhs=dTb4[:, 0, 0:n], start=True, stop=True), reads=["POOLW", ("dTb", 0)], writes=[pk])
                P.op("act", lambda e, pb=pb, g=g: e.activation(out=mixT[:, g, 0:n], in_=pb[:, 0:n], func=AF.Copy, scale=PSC(g)), reads=[pk, "pcol"], writes=["mixT"])
            for j, (dst, key) in [(12, (Z12, "E1")), (13, (Z13, "E3"))]:
                pb, pk = proj_fm(j, n)
                shift_evac(pb, pk, j, dst, key, n, prevT=shT[:, j, :])
            lora_stage(n)
            sigm_stage(n)
            def proj_pair_s(hp):
                for q, nm in enumerate(("Rb", "Kb", "Vb")):
                    j = 4 * q + hp
                    pb, pk = proj_fm(j, n)
                    shift_evac(pb, pk, j, RKV[hp % 2][q], (nm, 0), n, prevT=shT[:, j, :])
            for hp in range(4):
                proj_pair_s(hp)
                preprocess(hp, n, True)
            for q in range(6):
                pb, pk = bank()
                for hp in range(4):
                    P.op("pe", lambda e, q=q, hp=hp, pb=pb: e.transpose(out=pb[0:n, hp * 128:(hp + 1) * 128], in_=SMP[:, q, hp, :], identity=identf[:]),
                         reads=["SMP", "identf"], writes=[pk], inc=(hp == 3))
                P.op("act" if q % 2 else "dve", (lambda e, pb=pb: e.activation(out=ytok[0:n, :], in_=pb[0:n, :], func=AF.Copy)) if q % 2 else (lambda e, pb=pb: e.tensor_copy(out=ytok[0:n, :], in_=pb[0:n, :])),
                     reads=[pk], writes=["ytok"])
                P.dma(scr_d[q], ytok[0:n, :], reads=["ytok"], writes=[("scr", q)], sem="scr")
                P.dma(BH[:, q, :], scr_d[q].rearrange("b (h k) -> (b h) k", k=64), reads=[("scr", q)], writes=["BH"], sem="scr")
            P.dma(BHR[:], bh_d[:, :], writes=["BHR"], sem="sin")
            bq = lambda q: BH[:, q, :]
            bc_v = lambda ap: ap.unsqueeze(1).to_broadcast([128, SQ, 64])
            for sq in range(64 // SQ):
                vs = slice(sq * SQ, (sq + 1) * SQ)
                bc_k = lambda ap: ap[:, vs].unsqueeze(2).to_broadcast([128, SQ, 64])
                bi = sq % 2
                S0q, S1q = S0qs[bi], S1qs[bi]
                k0, k1 = ("S0q", bi), ("S1q", 0)
                if sq == 0:
                    P.dma(S0q[:].rearrange("p v k -> p (v k)"), swkv_d[:, 0:SQ * 64], writes=[k0], sem="sin0")
                if sq + 1 < 64 // SQ:
                    P.dma(S0qs[1 - bi][:].rearrange("p v k -> p (v k)"), swkv_d[:, (sq + 1) * SQ * 64:(sq + 2) * SQ * 64], writes=[("S0q", 1 - bi)], sem=f"sin{1 - bi}")
                P.op("pool", lambda e, bc_k=bc_k: e.tensor_tensor(out=Stq2[:], in0=bc_v(bq(2)), in1=bc_k(bq(3)), op=ALU.mult), reads=["BH"], writes=["ytok"])
                P.op("dve", lambda e: e.tensor_tensor(out=Stq[:], in0=S0q[:], in1=bc_v(bq(4)), op=ALU.mult), reads=[k0, "BH"], writes=["Stq"])
                P.op("dve", lambda e, vs=vs: e.tensor_reduce(out=sa_s[:, vs], in_=Stq[:], axis=AX.X, op=ALU.add), reads=["Stq"], writes=["sa_s"])
                P.op("dve", lambda e: e.tensor_tensor(out=S1q[:], in0=S0q[:], in1=bc_v(bq(1)), op=ALU.mult), reads=[k0, "BH"], writes=[k1])
                P.op("dve", lambda e, bc_k=bc_k: e.tensor_tensor(out=Stq[:], in0=bc_v(bq(5)), in1=bc_k(sa_s), op=ALU.mult), reads=["sa_s", "BH"], writes=["Stq"])
                P.op("dve", lambda e: e.tensor_tensor(out=S1q[:], in0=S1q[:], in1=Stq[:], op=ALU.add), reads=[k1, "Stq"], writes=[k1])
                P.op("dve", lambda e: e.tensor_tensor(out=S1q[:], in0=S1q[:], in1=Stq2[:], op=ALU.add), reads=[k1, "ytok"], writes=[k1])
                P.dma(wkvs_d[:, sq * SQ * 64:(sq + 1) * SQ * 64], S1q[:].rearrange("p v k -> p (v k)"), reads=[k1], sem=f"outw{bi}")
                P.op("dve", lambda e: e.tensor_tensor(out=Stq[:], in0=S1q[:], in1=bc_v(bq(0)), op=ALU.mult), reads=[k1, "BH"], writes=["Stq"])
                P.op("dve", lambda e, vs=vs: e.tensor_reduce(out=y_s[:, vs], in_=Stq[:], axis=AX.X, op=ALU.add), reads=["Stq"], writes=["y_s"])
            gn_stage(y_s[:, :], "y_s", 128, 1, 5)
            P.op("dve", lambda e: e.tensor_tensor(out=y_s2[:], in0=Yw[:, 0:64], in1=BHR[:, 0:64], op=ALU.mult), reads=["Yw", "BHR"], writes=["y_s2"])
            P.op("dve", lambda e: e.tensor_tensor(out=y_s2[:], in0=y_s2[:], in1=BHR[:, 64:128], op=ALU.add), reads=["y_s2", "BHR"], writes=["y_s2"])
            P.op("dve", lambda e: e.tensor_tensor(out=y_s[:], in0=bq(0), in1=bq(2), op=ALU.mult), reads=["BH", "y_s"], writes=["y_s"])
            P.op("dve", lambda e: e.tensor_tensor(out=y_s[:], in0=y_s[:], in1=BHR[:, 128:192], op=ALU.mult), reads=["y_s", "BHR"], writes=["y_s"])
            P.op("dve", lambda e: e.tensor_reduce(out=gst[:, 39:40], in_=y_s[:], axis=AX.X, op=ALU.add), reads=["y_s"], writes=["gst"])
            P.op("dve", lambda e: e.scalar_tensor_tensor(out=y_s2[:], in0=bq(3), scalar=gst[:, 39:40], in1=y_s2[:], op0=ALU.mult, op1=ALU.add), reads=["BH", "gst", "y_s2"], writes=["y_s2"])
            P.dma(scr_d[6].rearrange("b (h k) -> (b h) k", k=64), y_s2[:], reads=["y_s2"], writes=[("scr", 6)], sem="scr")
            P.dma(ytok[0:n, :], scr_d[6], reads=[("scr", 6)], writes=["ytok"], sem="scr")
            pg, pgk = bank()
            P.op("pe", lambda e: e.matmul(pg[0:n, :], lhsT=CX["sgzb"][:, 0:n], rhs=GLO[:], start=True, stop=True), reads=[CX["ksg"], "GLO"], writes=[pgk])
            P.op("dve", lambda e: e.tensor_tensor(out=Yo[0:n, :], in0=pg[0:n, :], in1=ytok[0:n, :], op=ALU.mult), reads=[pgk, "ytok"], writes=["Yo"])
            pb, pk = bank()
            pbb = pb[:].bitcast(BF16)
            for hp in range(4):
                P.op("pe", lambda e, hp=hp: e.transpose(out=pbb[:, hp * n:(hp + 1) * n], in_=Yo[0:n, hp * 128:(hp + 1) * 128], identity=identb[0:n, 0:n]),
                     reads=["Yo", "CB"], writes=[pk], inc=(hp == 3))
            P.op("dve", lambda e: e.tensor_copy(out=mixT[:, 4:8, 0:n], in_=pbb[:, 0:4 * n].rearrange("p (c t) -> p c t", t=n)), reads=[pk], writes=["mixT"])
            out_proj(n, 0, X1s[0:n, :], "x1s", NTILE)

            if KSTOP <= 2:
                raise _Stop()
            P.barrier()
            M.release(mS)
            ubf = M.alloc("ubf", [128, 3, 512], BF16)
            uf32 = Yq
            CUM = M.alloc("CUM", [128, ST], F32)
            E2 = M.alloc("E2", [128, ST], F32)
            CM = E2
            ART = M.alloc("ART", [128, 4, 2, ST], BF16)
            BT = M.alloc("BT", [128, 4, ST], BF16); KT = M.alloc("KT", [128, 4, ST], BF16)
            VBF = M.alloc("VBF", [128, 4, ST], BF16); RKB = M.alloc("RKB", [128, 4, ST], BF16)
            WC = M.alloc("WC", [128, 4, ST // 128], F32)
            btok = M.alloc("btok", [128, 512], BF16); ktok = M.alloc("ktok", [128, 512], BF16); vtok = M.alloc("vtok", [128, 512], BF16)
            L0 = M.alloc("L0", [128, 8, 128], BF16); Q0 = M.alloc("Q0", [128, 8, 128], BF16)
            UA = M.alloc("UA", [128, 8, 2, 128], BF16)
            KA = M.alloc("KA", [128, 8, 2, 128], BF16)
            Qb = [Q0, Q0]
            Xsb = KKN[:].bitcast(BF16); SAsb = Bb[:].bitcast(BF16)
            def P1gen(st):
                ntl = ST // 128
                for tl in range(ntl):
                    ti = st * ntl + tl
                    if tl == 0:
                        if st == 0:
                            P.op("pool", lambda e: e.memset(CX["hT"][:, :, 0:1], 0.0), writes=[CX["khT"]])
                        else:
                            P.op("dve", lambda e: e.tensor_copy(out=CX["hT"][:, :, 0:1], in_=hTs[0][:, :, ST:ST + 1]), reads=[("hT", 0)], writes=[CX["khT"]])
                    norm_to_hT(X1[:, ti, :], ("x1", ti), 128, 1 + tl * 128, ST)
                    yield
                for tl in range(ntl):
                    ti = st * ntl + tl
                    pb, pk = bank()
                    for c in range(8):
                        P.op("pe", lambda e, c=c, pb=pb, tl=tl: e.matmul(pb[:, :], lhsT=CX["hT"][:, c, 1 + tl * 128:1 + (tl + 1) * 128], rhs=WIN[:, c, 0:512], start=(c == 0), stop=(c == 7)),
                             reads=["WIN", CX["khT"]], writes=[pk], inc=(c == 7))
                    P.op("act", lambda e, pb=pb, ti=ti: e.activation(out=ubf[:, ti % 3, :], in_=pb[:, :], func=AF.Copy), reads=[pk], writes=[("ubf", ti % 3)])
                    if ti == NTILE - 1 and KVAR != 2 and KVAR != 3:
                        P.op("act", lambda e, pb=pb: e.activation(out=uf32[:, :], in_=pb[:, :], func=AF.Copy), reads=[pk], writes=["Yq"])
                        if KVAR == 6:
                            continue
                        pb2, pk2 = bank()
                        P.op("pe", lambda e: e.matmul(pb2[0:16, :], lhsT=identf[:, 112:128], rhs=uf32[:, :], start=True, stop=True),
                             reads=["Yq", "identf"], writes=[pk2])
                        P.op("dve", lambda e: e.tensor_copy(out=Yw[0:16, :], in_=pb2[0:16, :]), reads=[pk2], writes=["Yw"])
                        if KVAR != 5:
                            P.dma(plp_d[:, :], Yw[0:16, :], reads=["Yw"], sem="outp")
                yield
                ob, _ = coff["band"]; obp, _ = coff["bprev"]; obh, _ = coff["bfhi"]; obl, _ = coff["bflo"]
                for g in range(4):
                    pd, pdk = bank()
                    for tl in range(ntl):
                        ti = st * ntl + tl
                        lhs = ubf[:, ti % 3, g * 128:(g + 1) * 128]
                        ocol = pd[:, tl * 128:(tl + 1) * 128]
                        if ti == 0:
                            P.op("pe", lambda e, lhs=lhs, ocol=ocol, g=g: e.matmul(ocol, lhsT=lhs, rhs=CB[:, obh + g * 128:obh + (g + 1) * 128], start=True, stop=False),
                                 reads=[("ubf", 0), "CB"], writes=[pdk], inc=False)
                            P.op("pe", lambda e, lhs=lhs, ocol=ocol, g=g: e.matmul(ocol, lhsT=lhs, rhs=CB[:, obl + g * 128:obl + (g + 1) * 128], start=False, stop=True),
                                 reads=[("ubf", 0), "CB"], writes=[pdk], inc=True)
                        else:
                            lhsp = ubf[:, (ti - 1) % 3, g * 128:(g + 1) * 128]
                            P.op("pe", lambda e, lhs=lhs, ocol=ocol, g=g: e.matmul(ocol, lhsT=lhs, rhs=CB[:, ob + g * 128:ob + (g + 1) * 128], start=True, stop=False),
                                 reads=[("ubf", ti % 3), "CB"], writes=[pdk], inc=False)
                            P.op("pe", lambda e, lhsp=lhsp, ocol=ocol, g=g: e.matmul(ocol, lhsT=lhsp, rhs=CB[:, obp + g * 128:obp + (g + 1) * 128], start=False, stop=True),
                                 reads=[("ubf", (ti - 1) % 3), "CB"], writes=[pdk], inc=True)
                    pool_mix_a(pd, pdk, g, ST)
                    yield
                for g in range(4):
                    pool_mix_b(g, ST)
                    yield
                for j, (dst, key) in [(12, (Z12, "E1")), (13, (Z13, "E3"))]:
                    pb, pk = proj_fm(j, ST + 1)
                    shift_evac_p(pb, pk, j, dst, key, st == T // ST - 1)
                    yield
                lora_stage(ST)
                sigm_stage(ST)
                yield

            def P2(st):
                def proj_pair(hp):
                    for q, nm in enumerate(("Rb", "Kb", "Vb")):
                        j = 4 * q + hp
                        pb, pk = proj_fm(j, ST + 1)
                        shift_evac_p(pb, pk, j, RKV[hp % 2][q], (nm, 0), st == T // ST - 1)
                for hp in range(4):
                    proj_pair(hp)
                    preprocess(hp, ST, False)
                if st == T // ST - 1 and KVAR != 2 and KVAR != 4:
                    pb, pk = bank()
                    P.op("pe", lambda e: e.transpose(out=pb[0:14, 0:128], in_=plast[:, 0:14], identity=identf[:]),
                         reads=[("plast", j) for j in range(14)] + ["identf"], writes=[pk])
                    P.op("dve", lambda e: e.tensor_copy(out=tmpS[0:14, 0:128], in_=pb[0:14, 0:128]), reads=[pk], writes=["RN"])
                    P.dma(shp_d[:, :], tmpS[0:14, 0:128], reads=["RN"], sem="outp")
                if DEBUG_TAPS and st == 0:
                    tap("art", ART[:].rearrange("p a b t -> p (a b t)"), [128, 8 * ST], ["ART"])
                    tap("bt", BT[:].rearrange("p a t -> p (a t)"), [128, 4 * ST], ["BT"])
                    tap("kt", KT[:].rearrange("p a t -> p (a t)"), [128, 4 * ST], ["KT"])
                    tap("mixT", mixT[:].rearrange("p a t -> p (a t)"), [128, 8 * ST], ["mixT"])

            def CHgen(st):
                ntl = ST // 128
                for ck in range(ntl):
                    ti = st * ntl + ck
                    cs = slice(ck * 128, (ck + 1) * 128)
                    for src, skey, dst, dkey in [(BT, "BT", btok, "btok"), (KT, "KT", ktok, "ktok"), (VBF, "VBF", vtok, "vtok")]:
                        pb, pk = bank()
                        pbb = pb[:].bitcast(BF16)
                        for hp in range(4):
                            P.op("pe", lambda e, hp=hp, src=src, pbb=pbb: e.transpose(out=pbb[:, hp * 128:(hp + 1) * 128], in_=src[:, hp, cs], identity=identb),
                                 reads=[skey, "CB"], writes=[pk], inc=(hp == 3))
                        P.op("act", lambda e, dst=dst, pbb=pbb: e.activation(out=dst[:, :], in_=pbb[:, 0:512], func=AF.Copy), reads=[pk], writes=[dkey])
                    yield
                    for hl in range(2):
                        pr = slice(hl * 64, hl * 64 + 64)
                        pl, plk = bank()
                        for hb in range(2):
                            pa, pak = bank()
                            pk2b, pk2k = bank()
                            for j in range(2):
                                h2 = 2 * hb + j
                                h = 2 * h2 + hl
                                rhs_ar = ART[pr, h2, :, cs]
                                P.op("pe", lambda e: e.matmul(pa[:, j * 256:(j + 1) * 256].rearrange("p (a t) -> p a t", a=2), lhsT=BT[pr, h2, cs], rhs=rhs_ar, start=True, stop=True),
                                     reads=["BT", "ART"], writes=[pak], inc=(j == 1))
                                P.op("pe", lambda e: e.matmul(pk2b[:, j * 256:(j + 1) * 256].rearrange("p (a t) -> p a t", a=2), lhsT=KT[pr, h2, cs], rhs=rhs_ar, start=True, stop=True),
                                     reads=["KT", "ART"], writes=[pk2k], inc=(j == 1))
                                P.op("pe", lambda e: e.matmul(pl[:, h2 * 128:(h2 + 1) * 128], lhsT=ART[pr, h2, 0, cs], rhs=BT[pr, h2, cs], start=True, stop=True),
                                     reads=["BT", "ART"], writes=[plk], inc=(j == 1))
                            if KVAR != 1:
                                mb4 = mask_ai.unsqueeze(1).to_broadcast([128, 2, 2, 128])
                                uav = UA[:].rearrange("p (b j l) a t -> p b j l a t", b=2, j=2, l=2)[:, hb, :, hl, :, :]
                                kav = KA[:].rearrange("p (b j l) a t -> p b j l a t", b=2, j=2, l=2)[:, hb, :, hl, :, :]
                                P.op("dve", lambda e: e.tensor_tensor(out=uav, in0=pa[:, :].rearrange("p (j a t) -> p j a t", j=2, a=2), in1=mb4, op=ALU.mult),
                                     reads=[pak, "CB"], writes=[("U", hb), "ARB"])
                                P.op("dve", lambda e: e.tensor_tensor(out=kav, in0=pk2b[:, :].rearrange("p (j a t) -> p j a t", j=2, a=2), in1=mb4, op=ALU.mult),
                                     reads=[pk2k, "CB"], writes=["AAK", "ARK"])
                        if KVAR != 1:
                            lav = L0[:].rearrange("p (a l) t -> p a l t", l=2)[:, :, hl, :]
                            P.op("dve", lambda e: e.tensor_tensor(out=lav, in0=pl[:, :].rearrange("p (a t) -> p a t", a=4), in1=mask_sl.unsqueeze(1).to_broadcast([128, 4, 128]), op=ALU.mult),
                                 reads=[plk, "CB"], writes=[("L", 0), ("L", 1)])
                    yield
                    P.op("pool", lambda e: e.tensor_tensor(out=Q0[:], in0=UA[:, :, 0, :], in1=identb.unsqueeze(1).to_broadcast([128, 8, 128]), op=ALU.add),
                         reads=[("U", 0), ("U", 1), "CB"], writes=[("Q", 0), ("Q", 1)])
                    for lv in range(1, 7):
                        need_u = lv < 6
                        bk = {}
                        for hg in range(2):
                            pu, puk = bank() if need_u else (None, None)
                            pl, plk = bank()
                            bk[hg] = (pu, puk, pl, plk)
                            for hh in range(4):
                                h = hg * 4 + hh
                                if need_u:
                                    P.op("pe", lambda e: e.matmul(pu[:, hh * 128:(hh + 1) * 128], lhsT=L0[:, h, :], rhs=UA[:, h, 0, :], start=True, stop=True),
                                         reads=[("L", hg), ("U", hg)], writes=[puk], inc=(hh == 3))
                                P.op("pe", lambda e: e.matmul(pl[:, hh * 128:(hh + 1) * 128], lhsT=UA[:, h, 0, :], rhs=L0[:, h, :], start=True, stop=True),
                                     reads=[("L", hg), ("U", hg)], writes=[plk], inc=(hh == 3))
                        for hg in range(2):
                            hs = slice(hg * 4, hg * 4 + 4)
                            pu, puk, pl, plk = bk[hg]
                            P.op("act", lambda e: e.activation(out=L0[:, hs, :], in_=pl[:, :].rearrange("p (h t) -> p h t", h=4), func=AF.Copy),
                                 reads=[plk], writes=[("L", hg)])
                            if need_u:
                                if hg == 0:
                                    P.op("act", lambda e: e.activation(out=UA[:, hs, 0, :], in_=pu[:, :].rearrange("p (h t) -> p h t", h=4), func=AF.Copy),
                                         reads=[puk], writes=[("U", hg)])
                                else:
                                    P.op("dve", lambda e: e.tensor_copy(out=UA[:, hs, 0, :], in_=pu[:, :].rearrange("p (h t) -> p h t", h=4)),
                                         reads=[puk], writes=[("U", hg)])
                        bq_ = {}
                        for hg in range(2):
                            pq, pqk = bank()
                            bq_[hg] = (pq, pqk)
                            pe_warm_small(2, pq, pqk)
                            for hh in range(4):
                                h = hg * 4 + hh
                                P.op("pe", lambda e: e.matmul(pq[:, hh * 128:(hh + 1) * 128], lhsT=L0[:, h, :], rhs=Q0[:, h, :], start=True, stop=True),
                                     reads=[("L", hg), ("Q", hg)], writes=[pqk], inc=(hh == 3))
                        for hg in range(2):
                            hs = slice(hg * 4, hg * 4 + 4)
                            pq, pqk = bq_[hg]
                            P.op("dve", lambda e: e.tensor_tensor(out=Q0[:, hs, :], in0=pq[:, :].rearrange("p (h t) -> p h t", h=4), in1=Q0[:, hs, :], op=ALU.add),
                                 reads=[pqk, ("Q", hg)], writes=[("Q", hg)])
                        yield
                    Qf = Qb[0]
                    px, pxk = bank()
                    for h in range(8):
                        h2, hl = h // 2, h % 2
                        pr = slice(hl * 64, hl * 64 + 64)
                        oc = px[:, h * 64:(h + 1) * 64]
                        P.op("pe", lambda e, oc=oc, pr=pr, h2=h2: e.matmul(oc, lhsT=ART[:, h2, 0, cs], rhs=Mb[:, h2, hl, :], start=True, stop=False),
                             reads=["ART", "Mb"], writes=[pxk], inc=False)
                        P.op("pe", lambda e, oc=oc, h=h: e.matmul(oc, lhsT=KA[:, h, 0, :], rhs=vtok[:, h * 64:(h + 1) * 64], start=False, stop=True),
                             reads=["AAK", "vtok"], writes=[pxk], inc=(h == 7))
                    P.op("act", lambda e, px=px: e.activation(out=Xsb[:, :], in_=px[:, :], func=AF.Copy), reads=[pxk], writes=["KKN"])
                    yield
                    psa, psak = bank()
                    for h in range(8):
                        P.op("pe", lambda e, h=h, psa=psa: e.matmul(psa[:, h * 64:(h + 1) * 64], lhsT=Qf[:, h, :], rhs=Xsb[:, h * 64:(h + 1) * 64], start=True, stop=True),
                             reads=[("Q", h // 4), "KKN"], writes=[psak], inc=(h == 7))
                    P.op("act", lambda e, psa=psa: e.activation(out=SAsb[:, :], in_=psa[:, :], func=AF.Copy), reads=[psak], writes=["Bb"])
                    yield
                    py, pyk = bank()
                    for h in range(8):
                        h2, hl = h // 2, h % 2
                        pr = slice(hl * 64, hl * 64 + 64)
                        oc = py[:, h * 64:(h + 1) * 64]
                        P.op("pe", lambda e, oc=oc, pr=pr, h2=h2: e.matmul(oc, lhsT=ART[:, h2, 1, cs], rhs=Mb[:, h2, hl, :], start=True, stop=False),
                             reads=["ART", "Mb"], writes=[pyk], inc=False)
                        P.op("pe", lambda e, oc=oc, h=h: e.matmul(oc, lhsT=UA[:, h, 1, :], rhs=SAsb[:, h * 64:(h + 1) * 64], start=False, stop=False),
                             reads=["ARB", "Bb"], writes=[pyk], inc=False)
                        P.op("pe", lambda e, oc=oc, h=h: e.matmul(oc, lhsT=KA[:, h, 1, :], rhs=vtok[:, h * 64:(h + 1) * 64], start=False, stop=True),
                             reads=["ARK", "vtok"], writes=[pyk], inc=(h == 7))
                    pm, pmk = bank()
                    for h in range(8):
                        h2 = h // 2
                        oc = pm[:, h * 64:(h + 1) * 64]
                        P.op("pe", lambda e, oc=oc, h=h, h2=h2: e.matmul(oc, lhsT=btok[:, h2 * 128:(h2 + 1) * 128], rhs=SAsb[:, h * 64:(h + 1) * 64], start=True, stop=False),
                             reads=["btok", "Bb"], writes=[pmk], inc=False)
                        P.op("pe", lambda e, oc=oc, h=h, h2=h2: e.matmul(oc, lhsT=ktok[:, h2 * 128:(h2 + 1) * 128], rhs=vtok[:, h * 64:(h + 1) * 64], start=False, stop=True),
                             reads=["ktok", "vtok"], writes=[pmk], inc=(h == 7))
                    for hl in range(2):
                        pr = slice(hl * 64, hl * 64 + 64)
                        src = pm[pr, :].rearrange("p (a l v) -> p a l v", a=4, l=2)[:, :, hl, :]
                        P.op("dve", lambda e, pr=pr, src=src: e.tensor_tensor(out=Mt[pr, :, :], in0=src, in1=Mf[pr, :, :], op=ALU.add), reads=[pmk, "Mf"], writes=["Mt"])
                        P.op("dve", lambda e, pr=pr: e.tensor_tensor(out=Mf[pr, :, :], in0=Mt[pr, :, :], in1=WC[pr, :, ck:ck + 1].to_broadcast([64, 4, 64]), op=ALU.mult),
                             reads=["Mt", "WC"], writes=["Mf"])
                        P.op("act", lambda e, pr=pr: e.activation(out=Mb[pr, :, hl, :], in_=Mf[pr, :, :], func=AF.Copy), reads=["Mf"], writes=["Mb"])
                    pe_warm(10)
                    gn_stage(py[:, :], pyk, 128, 8, 5)
                    P.op("dve", lambda e: e.tensor_tensor(out=Yw[:, :], in0=Yw[:, :], in1=LNG[:, :], op=ALU.mult), reads=["Yw", "LNG"], writes=["Yw"])
                    P.op("dve", lambda e: e.tensor_tensor(out=Yw[:, :], in0=Yw[:, :], in1=LNB[:, :], op=ALU.add), reads=["Yw", "LNB"], writes=["Yw"])
                    pbn, pbnk = bank()
                    for hp in range(4):
                        P.op("pe", lambda e, hp=hp, pbn=pbn: e.matmul(pbn[:, 0:8], lhsT=RKB[:, hp, cs], rhs=RKI[:, hp, :], start=(hp == 0), stop=(hp == 3)),
                             reads=["RKB", "RKI"], writes=[pbnk], inc=(hp == 3))
                    P.op("act", lambda e, pbn=pbn: e.activation(out=gst[:, 0:8], in_=pbn[:, 0:8], func=AF.Copy), reads=[pbnk], writes=["gst"])
                    P.op("dve", lambda e: e.tensor_tensor(out=Yq[:, :].rearrange("p (h v) -> p h v", v=64), in0=vtok[:, :].rearrange("p (h v) -> p h v", v=64),
                                                           in1=gst[:, 0:8].unsqueeze(2).to_broadcast([128, 8, 64]), op=ALU.mult), reads=["vtok", "gst"], writes=["Yq"])
                    P.op("dve", lambda e: e.tensor_tensor(out=Yw[:, :], in0=Yw[:, :], in1=Yq[:, :], op=ALU.add), reads=["Yw", "Yq"], writes=["Yw"])
                    pg, pgk = bank()
                    P.op("pe", lambda e, pg=pg: e.matmul(pg[:, :], lhsT=CX["sgzb"][:, cs], rhs=GLO[:], start=True, stop=True), reads=[CX["ksg"], "GLO"], writes=[pgk])
                    P.op("dve", lambda e, pg=pg: e.tensor_tensor(out=Yo[:, :], in0=pg[:, :], in1=Yw[:, :], op=ALU.mult), reads=[pgk, "Yw"], writes=["Yo"])
                    pb, pk = bank()
                    pbb = pb[:].bitcast(BF16)
                    for hp in range(4):
                        P.op("pe", lambda e, hp=hp, pbb=pbb: e.transpose(out=pbb[:, hp * 128:(hp + 1) * 128], in_=Yo[:, hp * 128:(hp + 1) * 128], identity=identb),
                             reads=["Yo", "CB"], writes=[pk], inc=(hp == 3))
                    P.op("act", lambda e, pbb=pbb: e.activation(out=mixT[:, 4:8, cs], in_=pbb[:, 0:512].rearrange("p (c t) -> p c t", c=4), func=AF.Copy), reads=[pk], writes=["mixT"])
                    yield

            def OUT(st):
                ntl = ST // 128
                for tl in range(ntl):
                    ti = st * ntl + tl
                    out_proj(128, tl * 128, X1[:, ti, :], ("x1", ti), ti)
                if DEBUG_TAPS and st == 0:
                    tap("x1_0", X1[:, 0, :], [128, D], [("x1", 0)])

            def ctx(st, pn, banks):
                i = 0
                CX.update(hT=hTs[i], khT=("hT", i), sgzb=sgzbs[i], ksg=("sgzb", i), pn=pn, banks=banks,
                          mixdst=lambda g: (mixPs[i][:, g, 0:ST], ("mixP", i, g)),
                          mixsrc=lambda c, c0, npart: (mixPs[i][:, c, c0:c0 + npart], ("mixP", i, c)))

            ALLB = list(range(8))
            ctx(0, "all", ALLB)
            for _ in P1gen(0):
                pass
            for st in range(T // ST):
                ctx(st, "all", ALLB)
                P2(st)
                for _ in CHgen(st):
                    pass
                g_p1 = P1gen(st + 1) if st + 1 < T // ST else None
                if g_p1 is not None:
                    for _ in range(3):
                        next(g_p1)
                OUT(st)
                if g_p1 is not None:
                    for _ in g_p1:
                        pass
            if KSTOP <= 2.6:
                raise _Stop()
            for hp in range(4):
                pb, pk = bank()
                P.op("pe", lambda e, hp=hp, pb=pb: e.transpose(out=pb[0:64, 0:128], in_=Mf[:, hp, :], identity=identf[:]), reads=["Mf", "identf"], writes=[pk])
                P.op("dve", lambda e, pb=pb: e.tensor_copy(out=Yw[0:64, 0:128], in_=pb[0:64, 0:128]), reads=[pk], writes=["Yw"])
                P.dma(wkvp_d[2 * hp:2 * hp + 2].rearrange("l v k -> v l k"), Yw[0:64, 0:128].rearrange("p (l k) -> p l k", l=2), reads=["Yw"], sem="outp")

            if KSTOP <= 3:
                raise _Stop()
            P.barrier()
            M.release(mA)
            FB = 1024
            GF = M.alloc("GF", [128, D], F32)
            P.dma(GF[:], rows_d[2, :].partition_broadcast(128), writes=["GF"], sem="par")
            WUP = [M.alloc(f"WUP{i}", [128, 8, FB], BF16) for i in range(2)]
            WDN = [M.alloc(f"WDN{i}", [128, 8, D], BF16) for i in range(2)]
            stg = [M.alloc(f"stgm{i}", [128, FB], F32) for i in range(2)]
            xb16 = M.alloc("xb16", [128, D], BF16)
            H2T = M.alloc("H2T", [128, 8, T], BF16)
            H2Ts = M.alloc("H2Ts", [128, 8, NS], BF16)
            rl = M.alloc("rl", [128, 512], BF16)
            actT = M.alloc("actT", [128, 8, 512], BF16)
            statB = M.alloc("statB", [128, 8], F32)
            P.op("pool", lambda e: e.memset(statB[:, 7:8], RMS_EPS), writes=[("statB", 7)])
            nfb = DFF // FB
            sn = [0]

            def load_block(fb):
                sl = fb % 2
                for c in range(8):
                    s = sn[0] % 2
                    sn[0] += 1
                    P.dma(stg[s][:, :], w_up_d[c * 128:(c + 1) * 128, fb * FB:(fb + 1) * FB], writes=[("stgm", s)], sem=f"wm{s}")
                    P.op("dve", lambda e, c=c, s=s, sl=sl: e.tensor_scalar(out=WUP[sl][:, c, :], in0=stg[s][:, :], scalar1=G2(c), scalar2=None, op0=ALU.mult),
                         reads=[("stgm", s), "pcol"], writes=[("WUP", sl)])
                for fc in range(8):
                    s = sn[0] % 2
                    sn[0] += 1
                    r0 = fb * FB + fc * 128
                    P.dma(stg[s][:, :], w_dn_d[r0:r0 + 128, :], writes=[("stgm", s)], sem=f"wm{s}")
                    P.op("act", lambda e, fc=fc, s=s, sl=sl: e.activation(out=WDN[sl][:, fc, :], in_=stg[s][:, :], func=AF.Copy),
                         reads=[("stgm", s)], writes=[("WDN", sl)])

            def block_chunks(fb):
                sl = fb % 2
                out = []
                for k in range(16):
                    s_ = k % 2
                    if k < 8:
                        c = k
                        d = lambda s_=s_, c=c: P.dma(stg[s_][:, :], w_up_d[c * 128:(c + 1) * 128, fb * FB:(fb + 1) * FB], writes=[("stgm", s_)], sem=f"wm{s_}")
                        f = lambda s_=s_, c=c: P.op("dve", lambda e: e.tensor_scalar(out=WUP[sl][:, c, :], in0=stg[s_][:, :], scalar1=G2(c), scalar2=None, op0=ALU.mult),
                                                   reads=[("stgm", s_), "pcol"], writes=[("WUP", sl)])
                    else:
                        fc = k - 8
                        r0 = fb * FB + fc * 128
                        d = lambda s_=s_, r0=r0: P.dma(stg[s_][:, :], w_dn_d[r0:r0 + 128, :], writes=[("stgm", s_)], sem=f"wm{s_}")
                        f = lambda s_=s_, fc=fc: P.op("act", lambda e: e.activation(out=WDN[sl][:, fc, :], in_=stg[s_][:, :], func=AF.Copy),
                                                     reads=[("stgm", s_)], writes=[("WDN", sl)])
                    out.append((d, f))
                return out

            groups = [(g * 4, 4, 128) for g in range(NTILE // 4)] + [(NTILE, 1, NS)]
            xap = lambda tcol: (X1[:, tcol, :], ("x1", tcol)) if tcol < NTILE else (X1s[0:NS, :], "x1s")
            blk0 = block_chunks(0)
            blk0[0][0]()
            blk0[1][0]()
            for tcol in range(NTILE + 1):
                npart = 128 if tcol < NTILE else NS
                xa, xk = xap(tcol)
                P.op("act", lambda e: e.activation(out=xb16[0:npart, :], in_=xa, func=AF.Copy), reads=[xk], writes=["xb16"])
                pb, pk = bank()
                pbb = pb[:].bitcast(BF16)
                for c in range(8):
                    P.op("pe", lambda e: e.transpose(out=pbb[:, c * 128:c * 128 + npart], in_=xb16[0:npart, c * 128:(c + 1) * 128], identity=identb[0:npart, 0:npart]),
                         reads=["xb16", "CB"], writes=[pk], inc=(c == 7))
                dst = H2T[:, :, tcol * 128:(tcol + 1) * 128] if tcol < NTILE else H2Ts[:, :, 0:NS]
                P.op("dve", lambda e: e.tensor_copy(out=dst, in_=pbb.rearrange("p (c t) -> p c t", c=8)[:, :, 0:npart]),
                     reads=[pk], writes=[("H2T", tcol)])
                if tcol < 16:
                    blk0[tcol][1]()
                    if tcol + 2 < 16:
                        blk0[tcol + 2][0]()
            for fb in range(nfb):
                sl = fb % 2
                nxt = block_chunks(fb + 1) if fb + 1 < nfb else None
                if nxt is not None:
                    nxt[0][0]()
                    nxt[1][0]()
                gi = 0
                for (t0, nt, npart) in groups:
                    ncols = nt * npart
                    hsrc = (lambda c: H2T[:, c, t0 * 128:t0 * 128 + ncols]) if npart == 128 else (lambda c: H2Ts[:, c, 0:NS])
                    hkeys = [("H2T", t0 + k) for k in range(nt)]
                    if nxt is not None and gi < 4:
                        for kk_ in range(4 * gi, 4 * gi + 4):
                            nxt[kk_][1]()
                            if kk_ + 2 < 16:
                                nxt[kk_ + 2][0]()
                    gi += 1
                    for fc in range(8):
                        pb, pk = bank()
                        for c in range(8):
                            P.op("pe", lambda e: e.matmul(pb[:, 0:ncols], lhsT=WUP[sl][:, c, fc * 128:(fc + 1) * 128], rhs=hsrc(c), start=(c == 0), stop=(c == 7)),
                                 reads=[("WUP", sl)] + hkeys, writes=[pk], inc=(c == 7))
                        P.op("act", lambda e: e.activation(out=rl[:, 0:ncols], in_=pb[:, 0:ncols], func=AF.Relu), reads=[pk], writes=["rl"])
                        P.op("dve", lambda e: e.tensor_tensor(out=actT[:, fc, 0:ncols], in0=rl[:, 0:ncols], in1=rl[:, 0:ncols], op=ALU.mult), reads=["rl"], writes=[("actT", fc)])
                    for k in range(nt):
                        tcol = t0 + k
                        xa, xk = xap(tcol)
                        for dh in range(2):
                            pb, pk = bank()
                            for fc in range(8):
                                P.op("pe", lambda e: e.matmul(pb[0:npart, :], lhsT=actT[:, fc, k * npart:(k + 1) * npart], rhs=WDN[sl][:, fc, dh * 512:(dh + 1) * 512], start=(fc == 0), stop=(fc == 7)),
                                     reads=[("actT", fc), ("WDN", sl)], writes=[pk], inc=(fc == 7))
                            P.op("dve", lambda e: e.scalar_tensor_tensor(out=xa[:, dh * 512:(dh + 1) * 512], in0=pb[0:npart, :], scalar=rs2[0:npart, tcol:tcol + 1],
                                                                         in1=xa[:, dh * 512:(dh + 1) * 512], op0=ALU.mult, op1=ALU.add),
                                 reads=[pk, xk, "rs2"], writes=[xk])
                        if fb == nfb - 1:
                            P.op("act", lambda e: e.activation(out=xb16[0:npart, :], in_=xa, func=AF.Square, accum_out=statB[0:npart, 0:1]), reads=[xk], writes=["xb16", ("statB", 0)])
                            P.op("act", lambda e: e.activation(out=statB[0:npart, 1:2], in_=statB[0:npart, 0:1], func=AF.Ln, scale=1.0 / D, bias=statB[0:npart, 7:8]),
                                 reads=[("statB", 0), ("statB", 7)], writes=[("statB", 1)])
                            P.op("act", lambda e: e.activation(out=statB[0:npart, 2:3], in_=statB[0:npart, 1:2], func=AF.Exp, scale=-0.5), reads=[("statB", 1)], writes=[("statB", 2)])
                            P.op("dve", lambda e: e.scalar_tensor_tensor(out=xa, in0=xa, scalar=statB[0:npart, 2:3], in1=GF[0:npart, :], op0=ALU.mult, op1=ALU.mult),
                                 reads=[xk, ("statB", 2), "GF"], writes=[xk])
                            if npart == 128:
                                P.dma(y_d[tcol * 128:(tcol + 1) * 128, :], xa, reads=[xk], sem=f"yo{tcol % 4}")
                            else:
                                P.dma(ys_d[:, :], xa, reads=[xk], sem=f"yo{tcol % 4}")
        except _Stop:
            pass
        P.finish()
        block = es.enter_context(nc.Block())
        P.emit(block)
    return nc, cst_np, taps


_CACHE = {}


def kernel(x_prompt, x_sample, state_wkv, state_shift, state_pool, norm1_g, w_in, shift_mu,
           pool_w, pool_scale, w0, w_lora_up, a0, a_lora_up, g_lora_up, k_k, k_a, r_k,
           ln_x_g, ln_x_b, w_out, norm2_g, w_up, w_down, norm_f_g):
    f = lambda a: np.ascontiguousarray(np.asarray(a, dtype=np.float32))
    if "nc" not in _CACHE:
        _CACHE["nc"] = build_program()
    nc, cst_np, taps = _CACHE["nc"]
    col = lambda v, n: f(v).reshape(n, 128).T
    pcol = np.zeros((128, 64), np.float32)
    pcol[:, 0:14] = col(shift_mu[0], 14)
    pcol[:, 14:18] = col(w0[0], 4)
    pcol[:, 18:22] = col(a0[0], 4)
    pcol[:, 22:26] = col(k_k[0], 4)
    pcol[:, 26:30] = col(k_a[0], 4)
    pcol[:, 30:34] = col(f(r_k[0]).reshape(512), 4)
    pcol[:, 34:38] = col(pool_scale[0], 4)
    pcol[:, 38:46] = col(norm1_g[0], 8)
    pcol[:, 46:54] = col(norm2_g[0], 8)
    rows = np.zeros((3, 1024), np.float32)
    rows[0, 0:512] = f(ln_x_g[0]); rows[1, 0:512] = f(ln_x_b[0]); rows[2, :] = f(norm_f_g)
    bh = np.zeros((128, 192), np.float32)
    bh[:, 0:64] = np.tile(f(ln_x_g[0]).reshape(8, 64), (NS, 1))
    bh[:, 64:128] = np.tile(f(ln_x_b[0]).reshape(8, 64), (NS, 1))
    bh[:, 128:192] = np.tile(f(r_k[0]).reshape(8, 64), (NS, 1))
    lora12 = np.concatenate([f(w_lora_up[0]), f(a_lora_up[0])], axis=0)
    shared = {"w_in": f(w_in[0]), "w_out": f(w_out[0]), "w_up": f(w_up[0]), "w_down": f(w_down[0]),
              "pool_w": f(pool_w[0]), "lora12": lora12, "g_lora_up": f(g_lora_up[0]),
              "pcol": pcol, "rows": rows, "bhrows": bh, "cst": cst_np}
    xp, xs = f(x_prompt), f(x_sample)
    swkv, ssh, spl = f(state_wkv[0]), f(state_shift[0]), f(state_pool[0])
    in_maps = []
    for i in range(NCORES):
        b = slice(i * NS, (i + 1) * NS)
        m = dict(shared)
        m["x"] = xp[i]
        m["xs"] = xs[b, 0, :]
        m["swkv"] = swkv[b].reshape(NS * 8, 4096)
        m["sshift"] = ssh[b, 0, :]
        m["spool"] = spl[b].reshape(NS * 15, 512)
        in_maps.append(m)
    res = run_bass_kernel_spmd(nc, in_maps, core_ids=list(range(NCORES)))
    R = res.results
    _CACHE["last"] = R
    y_prompt = np.stack([R[i]["y"] for i in range(NCORES)], axis=0)
    y_sample = np.concatenate([R[i]["ys"] for i in range(NCORES)], axis=0)[:, None, :]
    wkv_p = np.stack([R[i]["wkv_p"] for i in range(NCORES)], axis=0)[None]
    sh_p = np.stack([R[i]["shift_p"].reshape(1, SHIFT_W) for i in range(NCORES)], axis=0)[None]
    pl_p = np.stack([R[i]["pool_p"][1:16] for i in range(NCORES)], axis=0)[None]
    wkv_s = np.concatenate([R[i]["wkv_s"].reshape(NS, 8, 64, 64) for i in range(NCORES)], axis=0)[None]
    sh_s = np.concatenate([R[i]["shift_s"] for i in range(NCORES)], axis=0)[:, None, :][None]
    pl_s = np.concatenate([R[i]["pool_s"] for i in range(NCORES)], axis=0)[None]
    out = (y_prompt, y_sample, wkv_p, sh_p, pl_p, wkv_s, sh_s, pl_s)
    return tuple(np.ascontiguousarray(o.astype(np.float32)) for o in out)
```
